# Optimizing a Trainium2 kernel written in Bass

```python
import jax, jax.numpy as jnp
from jax import lax
import numpy as np

D_MODEL = 1024
BATCH = 2
SEQ = 8192
DEPTH = 1
DEC_BATCH = 32
DEC_SEQ = 1
PAST_LEN = 16384
PAGE_SIZE = 128

N_HEADS = 8
HEAD_DIM = 64
ATT_WIDTH = N_HEADS * HEAD_DIM
POOL_WINDOWS = (2, 4, 8, 16)
N_POOL_GROUPS = len(POOL_WINDOWS)
POOL_WIDTH = D_MODEL // 2
POOL_GROUP = POOL_WIDTH // N_POOL_GROUPS
POOL_STATE = max(POOL_WINDOWS) - 1
MOBA_BLOCK = 256
MOBA_TOPK = 3
Q_CHUNK = 64
ROPE_THETA = 10000.0
PLE_DIM = 256
LN_EPS = 1e-5
DEEPNORM_ALPHA = (2 * DEPTH) ** 0.25
DEEPNORM_BETA = (8 * DEPTH) ** -0.25
IN_SPLITS = (ATT_WIDTH, ATT_WIDTH, ATT_WIDTH, ATT_WIDTH, POOL_WIDTH, POOL_WIDTH, D_MODEL, D_MODEL)
IN_WIDTH = sum(IN_SPLITS)

kernel_name = 'hybrid_pool_moba_decode_step'


def _layer_norm(x, g, b):
    xf = x.astype(jnp.float32)
    mu = jnp.mean(xf, axis=-1, keepdims=True)
    var = jnp.mean(jnp.square(xf - mu), axis=-1, keepdims=True)
    return ((xf - mu) * lax.rsqrt(var + LN_EPS) * g + b).astype(x.dtype)


def _rope(x, pos):
    half = HEAD_DIM // 2
    inv = ROPE_THETA ** (-jnp.arange(half, dtype=jnp.float32) / half)
    ang = pos.astype(jnp.float32)[:, None] * inv[None, :]
    cos = jnp.cos(ang)[None, :, None, :]
    sin = jnp.sin(ang)[None, :, None, :]
    xf = x.astype(jnp.float32)
    x1, x2 = xf[..., :half], xf[..., half:]
    return jnp.concatenate([x1 * cos - x2 * sin, x2 * cos + x1 * sin], axis=-1).astype(x.dtype)


def _mixer_inputs(x, w_in, pos):
    b, n, _ = x.shape
    offs = np.cumsum(IN_SPLITS)[:-1].tolist()
    q, k, v, zb, u, za, ga, gb = jnp.split(jnp.einsum('bnd,de->bne', x, w_in), offs, axis=-1)
    q = _rope(q.reshape(b, n, N_HEADS, HEAD_DIM), pos)
    k = _rope(k.reshape(b, n, N_HEADS, HEAD_DIM), pos)
    v = v.reshape(b, n, N_HEADS, HEAD_DIM)
    return q, k, v, zb, u, za, ga, gb


def _pool_mix(u_hist, u, pos0, w_mix, scale):
    b, n, c = u.shape
    ext = jnp.concatenate([u_hist, u], axis=1).astype(jnp.float32)
    csum = jnp.cumsum(jnp.concatenate([jnp.zeros((b, 1, c), jnp.float32), ext], axis=1), axis=1)
    end = csum[:, POOL_STATE + 1:]
    cur = ext[:, POOL_STATE:]
    pos = pos0 + jnp.arange(n)
    outs = []
    for g, w in enumerate(POOL_WINDOWS):
        sl = slice(g * POOL_GROUP, (g + 1) * POOL_GROUP)
        start = csum[:, POOL_STATE + 1 - w: POOL_STATE + 1 - w + n, sl]
        cnt = jnp.minimum(pos + 1, w).astype(jnp.float32)[None, :, None]
        diff = (end[..., sl] - start) / cnt - cur[..., sl]
        outs.append(jnp.einsum('bnc,ce->bne', diff, w_mix[g].astype(jnp.float32)))
    y = jnp.concatenate(outs, axis=-1) * scale.astype(jnp.float32)
    return y.astype(u.dtype)


def _moba_select(q, kmean, qpos):
    nb = kmean.shape[2]
    ksel = min(MOBA_TOPK, nb)
    s = jnp.einsum('bhqd,bhjd->bhqj', q.astype(jnp.float32), kmean.astype(jnp.float32))
    qblk = qpos // MOBA_BLOCK
    past = jnp.arange(nb)[None, :] < qblk[:, None]
    s = jnp.where(past[None, None], s, -jnp.inf)
    _, top = lax.top_k(s, ksel)
    own = jnp.broadcast_to(qblk[None, None, :, None], top.shape[:3] + (1,))
    blocks = jnp.concatenate([top, own], axis=-1).astype(jnp.int32)
    valid_sel = jnp.arange(ksel)[None, :] < qblk[:, None]
    valid = jnp.concatenate([valid_sel, jnp.ones((qpos.shape[0], 1), bool)], axis=-1)
    return blocks, valid


def _moba_attend(q, kg, vg, blocks, valid, qpos):
    key_pos = blocks[..., None] * MOBA_BLOCK + jnp.arange(MOBA_BLOCK)
    mask = valid[None, None, :, :, None] & (key_pos <= qpos[None, None, :, None, None])
    logits = jnp.einsum('bhqd,bhqjkd->bhqjk', q.astype(jnp.float32), kg.astype(jnp.float32)) * (HEAD_DIM ** -0.5)
    logits = jnp.where(mask, logits, -jnp.inf)
    b, h, n, j, kk = logits.shape
    probs = jax.nn.softmax(logits.reshape(b, h, n, j * kk), axis=-1).reshape(b, h, n, j, kk)
    out = jnp.einsum('bhqjk,bhqjkd->bhqd', probs, vg.astype(jnp.float32))
    return out.astype(q.dtype)


def _moba_prompt(q, k, v):
    b, s = q.shape[:2]
    nb = -(-s // MOBA_BLOCK)
    pad = ((0, 0), (0, nb * MOBA_BLOCK - s), (0, 0), (0, 0))
    kb = jnp.pad(k, pad).reshape(b, nb, MOBA_BLOCK, N_HEADS, HEAD_DIM).transpose(0, 3, 1, 2, 4)
    vb = jnp.pad(v, pad).reshape(b, nb, MOBA_BLOCK, N_HEADS, HEAD_DIM).transpose(0, 3, 1, 2, 4)
    kmean = jnp.sum(kb.astype(jnp.float32), axis=3) / MOBA_BLOCK
    qh = q.transpose(0, 2, 1, 3)
    bidx = jnp.arange(b)[:, None, None, None]
    hidx = jnp.arange(N_HEADS)[None, :, None, None]

    def chunk(c):
        qpos = c * Q_CHUNK + jnp.arange(Q_CHUNK)
        qc = lax.dynamic_slice_in_dim(qh, c * Q_CHUNK, Q_CHUNK, axis=2)
        blocks, valid = _moba_select(qc, kmean, qpos)
        kg = kb[bidx, hidx, blocks]
        vg = vb[bidx, hidx, blocks]
        return _moba_attend(qc, kg, vg, blocks, valid, qpos)

    out = lax.map(chunk, jnp.arange(s // Q_CHUNK))
    return out.transpose(1, 0, 3, 2, 4).reshape(b, s, ATT_WIDTH)


def _moba_sample(q, k, v, cache_k, cache_v, page_table):
    db, n = q.shape[:2]
    n_pages = page_table.shape[1]
    ppb = MOBA_BLOCK // PAGE_SIZE
    nnp = -(-n // PAGE_SIZE)
    pad = ((0, 0), (0, nnp * PAGE_SIZE - n), (0, 0), (0, 0))
    k_new = jnp.pad(k, pad).reshape(db, nnp, PAGE_SIZE, N_HEADS, HEAD_DIM)
    v_new = jnp.pad(v, pad).reshape(db, nnp, PAGE_SIZE, N_HEADS, HEAD_DIM)
    ps_cache = jnp.sum(cache_k[page_table].astype(jnp.float32), axis=2)
    ps_new = jnp.sum(k_new.astype(jnp.float32), axis=2)
    ps = jnp.concatenate([ps_cache, ps_new], axis=1)
    n_lp = n_pages + nnp
    nb = -(-n_lp // ppb)
    ps = jnp.pad(ps, ((0, 0), (0, nb * ppb - n_lp), (0, 0), (0, 0)))
    kmean = ps.reshape(db, nb, ppb, N_HEADS, HEAD_DIM).sum(2).transpose(0, 2, 1, 3) / MOBA_BLOCK
    qpos = PAST_LEN + jnp.arange(n)
    qh = q.transpose(0, 2, 1, 3)
    blocks, valid = _moba_select(qh, kmean, qpos)
    lp = blocks[..., None] * ppb + jnp.arange(ppb)
    in_cache = lp < n_pages
    bidx = jnp.arange(db)[:, None, None, None, None]
    hidx = jnp.arange(N_HEADS)[None, :, None, None, None]
    phys = page_table[bidx, jnp.clip(lp, 0, n_pages - 1)]
    newp = jnp.clip(lp - n_pages, 0, nnp - 1)
    rows = jnp.arange(PAGE_SIZE)

    def gather(pool, new_rows):
        from_cache = pool[phys[..., None], rows, hidx[..., None]]
        from_new = new_rows[bidx[..., None], newp[..., None], rows, hidx[..., None]]
        sel = jnp.where(in_cache[..., None, None], from_cache, from_new)
        return sel.reshape(db, N_HEADS, n, blocks.shape[-1], MOBA_BLOCK, HEAD_DIM)

    out = _moba_attend(qh, gather(cache_k, k_new), gather(cache_v, v_new), blocks, valid, qpos)
    return out.transpose(0, 2, 1, 3).reshape(db, n, ATT_WIDTH)


def _layer_out(x, att, zb, pool, za, ga, gb, p, w_pool_out, w_att_out, w_o, ln_g, ln_b, w_ple, w_ple_gate):
    y_pool = jnp.einsum('bnc,cd->bnd', pool * jax.nn.silu(za), w_pool_out)
    y_att = jnp.einsum('bnc,cd->bnd', att * jax.nn.silu(zb), w_att_out)
    merged = jax.nn.sigmoid(ga) * y_pool + jax.nn.sigmoid(gb) * y_att
    h = _layer_norm(DEEPNORM_ALPHA * x + jnp.einsum('bnc,cd->bnd', merged, w_o), ln_g, ln_b)
    gate = jax.nn.sigmoid(jnp.einsum('bnd,de->bne', h, w_ple_gate))
    return h + gate * jnp.einsum('bnp,pd->bnd', p, w_ple)


def setup_inputs(seed: int = 0) -> dict:
    key = jax.random.key(seed)
    ks = jax.random.split(key, 20)
    n_pages = PAST_LEN // PAGE_SIZE
    n_used = DEC_BATCH * n_pages
    n_phys = n_used + n_used // 4
    page_table = jax.random.permutation(ks[0], n_phys)[:n_used].reshape(DEC_BATCH, n_pages).astype(jnp.int32)
    f32 = jnp.float32
    col_scale = np.ones((IN_WIDTH,), np.float32)
    v_off = 2 * ATT_WIDTH
    u_off = 4 * ATT_WIDTH
    col_scale[v_off:v_off + ATT_WIDTH] = DEEPNORM_BETA
    col_scale[u_off:u_off + POOL_WIDTH] = DEEPNORM_BETA
    w_in = jax.random.normal(ks[1], (DEPTH, D_MODEL, IN_WIDTH), f32) * (D_MODEL ** -0.5) * jnp.asarray(col_scale)
    return {
        'x_prompt': jax.random.normal(ks[2], (BATCH, SEQ, D_MODEL), f32),
        'x_sample': jax.random.normal(ks[3], (DEC_BATCH, DEC_SEQ, D_MODEL), f32),
        'cache_k': jax.random.normal(ks[4], (DEPTH, n_phys, PAGE_SIZE, N_HEADS, HEAD_DIM), f32),
        'cache_v': jax.random.normal(ks[5], (DEPTH, n_phys, PAGE_SIZE, N_HEADS, HEAD_DIM), f32) * DEEPNORM_BETA,
        'state_pool': jax.random.normal(ks[6], (DEPTH, DEC_BATCH, POOL_STATE, POOL_WIDTH), f32) * DEEPNORM_BETA,
        'page_table': page_table,
        'p_prompt': jax.random.normal(ks[7], (DEPTH, BATCH, SEQ, PLE_DIM), f32),
        'p_sample': jax.random.normal(ks[8], (DEPTH, DEC_BATCH, DEC_SEQ, PLE_DIM), f32),
        'w_in': w_in,
        'w_pool_mix': jax.random.normal(ks[9], (DEPTH, N_POOL_GROUPS, POOL_GROUP, POOL_GROUP), f32) * (POOL_GROUP ** -0.5),
        'pool_scale': 1.0 + 0.1 * jax.random.normal(ks[10], (DEPTH, POOL_WIDTH), f32),
        'w_pool_out': jax.random.normal(ks[11], (DEPTH, POOL_WIDTH, D_MODEL), f32) * (POOL_WIDTH ** -0.5) * DEEPNORM_BETA,
        'w_att_out': jax.random.normal(ks[12], (DEPTH, ATT_WIDTH, D_MODEL), f32) * (ATT_WIDTH ** -0.5) * DEEPNORM_BETA,
        'w_o': jax.random.normal(ks[13], (DEPTH, D_MODEL, D_MODEL), f32) * (D_MODEL ** -0.5) * DEEPNORM_BETA,
        'ln_g': 1.0 + 0.1 * jax.random.normal(ks[14], (DEPTH, D_MODEL), f32),
        'ln_b': 0.1 * jax.random.normal(ks[15], (DEPTH, D_MODEL), f32),
        'w_ple': jax.random.normal(ks[16], (DEPTH, PLE_DIM, D_MODEL), f32) * (PLE_DIM ** -0.5),
        'w_ple_gate': jax.random.normal(ks[17], (DEPTH, D_MODEL, D_MODEL), f32) * (D_MODEL ** -0.5),
    }


def reference(x_prompt, x_sample, cache_k, cache_v, state_pool, page_table, p_prompt, p_sample,
              w_in, w_pool_mix, pool_scale, w_pool_out, w_att_out, w_o, ln_g, ln_b, w_ple, w_ple_gate):
    xp, xs = x_prompt, x_sample
    b, s, _ = xp.shape
    db, n, _ = xs.shape
    pos_p = jnp.arange(s)
    pos_s = PAST_LEN + jnp.arange(n)
    kp_l, vp_l, pp_l, ks_l, vs_l, psm_l = [], [], [], [], [], []
    for i in range(DEPTH):
        q, k, v, zb, u, za, ga, gb = _mixer_inputs(xp, w_in[i], pos_p)
        att = _moba_prompt(q, k, v)
        hist = jnp.zeros((b, POOL_STATE, POOL_WIDTH), u.dtype)
        pool = _pool_mix(hist, u, 0, w_pool_mix[i], pool_scale[i])
        kp_l.append(k)
        vp_l.append(v)
        pp_l.append(jnp.concatenate([hist, u], axis=1)[:, -POOL_STATE:])
        xp = _layer_out(xp, att, zb, pool, za, ga, gb, p_prompt[i], w_pool_out[i], w_att_out[i], w_o[i],
                        ln_g[i], ln_b[i], w_ple[i], w_ple_gate[i])
        q, k, v, zb, u, za, ga, gb = _mixer_inputs(xs, w_in[i], pos_s)
        att = _moba_sample(q, k, v, cache_k[i], cache_v[i], page_table)
        hist = state_pool[i].astype(u.dtype)
        pool = _pool_mix(hist, u, PAST_LEN, w_pool_mix[i], pool_scale[i])
        ks_l.append(k)
        vs_l.append(v)
        psm_l.append(jnp.concatenate([hist, u], axis=1)[:, -POOL_STATE:])
        xs = _layer_out(xs, att, zb, pool, za, ga, gb, p_sample[i], w_pool_out[i], w_att_out[i], w_o[i],
                        ln_g[i], ln_b[i], w_ple[i], w_ple_gate[i])
    k_prompt = jnp.stack(kp_l)
    v_prompt = jnp.stack(vp_l)
    pool_prompt = jnp.stack(pp_l)
    k_sample = jnp.stack(ks_l)
    v_sample = jnp.stack(vs_l)
    pool_sample = jnp.stack(psm_l)
    return (xp, xs, k_prompt, v_prompt, pool_prompt, k_sample, v_sample, pool_sample)
```

```python
import contextlib
import numpy as np
import concourse.bass as bass
import concourse.mybir as mybir
from concourse.bass_utils import run_bass_kernel_spmd

F32 = mybir.dt.float32
BF16 = mybir.dt.bfloat16
I32 = mybir.dt.int32
U32 = mybir.dt.uint32
AF = mybir.ActivationFunctionType
ALU = mybir.AluOpType
AX = mybir.AxisListType

NCORES = 8
D = 1024
SEQ = 8192
H = 8
HD = 64
AW = 512
PLE = 256
PAST = 16384
NPAGE = 128
NPHYS = 5120
DEV_SKIP_DECODE = False
DEBUG = False
_last = None
SBK = 512
KMAX = [3, 7, 11, 15]
BIG = 30000.0
ALPHA = 2.0 ** 0.25
EPS = 1e-5
NSEQ = 4
QTOT = 2048 + 128


def own_sbs(r):
    return [r, 7 - r, 8 + r, 15 - r]


class Op:
    __slots__ = ("eng", "fn", "dma", "deps", "needed", "event")

    def __init__(self, eng, fn, dma):
        self.eng = eng
        self.fn = fn
        self.dma = dma
        self.deps = set()
        self.needed = False
        self.event = None


class Sched:
    ENG = ("pe", "act", "dve", "pool", "sp")

    def __init__(self, nc, es):
        self.nc = nc
        self.es = es
        self.h = {"pe": nc.tensor, "act": nc.scalar, "dve": nc.vector, "pool": nc.gpsimd, "sp": nc.sync}
        self.ops = {e: [] for e in self.ENG}
        self.res = {}
        self.bar = set()
        self.dma_since = []

    def add(self, eng, fn, reads=(), writes=(), dma=None):
        op = Op(eng, fn, dma)
        deps = set(self.bar)
        for r in reads:
            st = self.res.get(r)
            if st is not None and st[0] is not None:
                deps.add(st[0])
        for w in writes:
            st = self.res.get(w)
            if st is not None:
                if st[0] is not None:
                    deps.add(st[0])
                deps.update(st[1])
        if eng == "pe" and dma is None:
            deps = {d for d in deps if not (d.eng == "pe" and d.dma is None)}
        op.deps = deps
        for d in deps:
            d.needed = True
        for r in reads:
            self.res.setdefault(r, [None, []])[1].append(op)
        for w in writes:
            self.res[w] = [op, []]
        self.ops[eng].append(op)
        if dma is not None:
            self.dma_since.append(op)
        return op

    def barrier(self):
        b = set()
        for e in self.ENG:
            for op in reversed(self.ops[e]):
                if op.dma is None:
                    b.add(op)
                    break
        b.update(self.dma_since)
        self.dma_since = []
        for d in b:
            d.needed = True
        self.bar = b

    def emit(self, block):
        nc = self.nc
        esem = {e: self.es.enter_context(nc.semaphore("sem_" + e)) for e in self.ENG}
        dsem = {}
        dcnt = {}
        for e in self.ENG:
            cnt = 0
            for op in self.ops[e]:
                if op.dma is not None:
                    if op.dma not in dsem:
                        dsem[op.dma] = self.es.enter_context(nc.semaphore("dq_%d" % len(dsem)))
                        dcnt[op.dma] = 0
                    dcnt[op.dma] += 16
                    op.event = (dsem[op.dma], dcnt[op.dma], 16)
                elif op.needed:
                    cnt += 1
                    op.event = (esem[e], cnt, 1)

        def run(e, handle):
            waited = {}
            for op in self.ops[e]:
                need = {}
                for d in op.deps:
                    sem, val, _ = d.event
                    k = id(sem)
                    if k not in need or need[k][1] < val:
                        need[k] = (sem, val)
                for k, (sem, val) in need.items():
                    if waited.get(k, 0) < val:
                        handle.wait_ge(sem, val)
                        waited[k] = val
                ins = op.fn(handle)
                if op.event is not None:
                    ins.then_inc(op.event[0], op.event[2])

        @block.sync
        def _(eng):
            run("sp", eng)

        @block.gpsimd
        def _(eng):
            run("pool", eng)

        @block.tensor
        def _(eng):
            run("pe", eng)

        @block.vector
        def _(eng):
            run("dve", eng)

        @block.scalar
        def _(eng):
            run("act", eng)


def build_nc():
    nc = bass.Bass("TRN2", target_bir_lowering=False)

    def din(name, shape, dt=F32):
        return nc.dram_tensor(name, list(shape), dt, kind="ExternalInput").ap()

    def dout(name, shape, dt=F32):
        return nc.dram_tensor(name, list(shape), dt, kind="ExternalOutput").ap()

    xall = din("xall", [SEQ, D])
    xown = din("xown", [2048, D])
    xhalo = din("xhalo", [16, 16, D])
    pown = din("pown", [2048, PLE])
    xs = din("xs", [128, D])
    pss = din("pss", [128, PLE])
    cs_all = din("cs_all", [SEQ, 64])
    cs_own = din("cs_own", [2048, 64])
    cs_s = din("cs_s", [128, 64])
    pastneg = din("pastneg", [2048, 32])
    notown = din("notown", [2048, 32])
    ownA = din("ownA", [2048, 32])
    isAB = din("isAB", [2048, 2])
    oh_all = din("oh_all", [16, 32, SBK])
    oh_own = din("oh_own", [32, SBK])
    tri = din("tri", [128, 4 * SBK])
    corr = din("corr", [16, 4, 16])
    ident_d = din("ident", [128, 128])
    w_in = din("w_in", [D, 5120])
    w_mix = din("w_mix", [4, 128, 128])
    pscale = din("pscale", [512])
    w_po = din("w_po", [512, D])
    w_ao = din("w_ao", [512, D])
    w_o = din("w_o", [D, D])
    ln_g = din("ln_g", [D])
    ln_b = din("ln_b", [D])
    w_ple = din("w_ple", [PLE, D])
    w_pg = din("w_pg", [D, D])
    cache_k = din("cache_k", [NPHYS * 128 * 8, 64])
    cache_v = din("cache_v", [NPHYS * 128 * 8, 64])
    ptab = din("ptab", [NSEQ, NPAGE], I32)
    spool = din("spool", [NSEQ, 15, 512])
    pairm = din("pairm", [128, 64])
    iota128 = din("iota128", [8, 128])
    hsel = din("hsel", [8, 48])
    pofs = din("pofs", [128, 48])

    y_own = dout("y_own", [2048, D])
    k_own = dout("k_own", [2048, AW])
    v_own = dout("v_own", [2048, AW])
    poolp = dout("poolp", [15, 512])
    y_s = dout("y_s", [128, D])
    k_s = dout("k_s", [128, AW])
    v_s = dout("v_s", [128, AW])
    pool_s = dout("pool_s", [NSEQ, 15, 512])

    if DEBUG:
        dbg_att = dout("dbg_att", [128, 17, AW])
        dbg_mg = dout("dbg_mg", [5, 128, 8, SBK])
        dbg_qt = dout("dbg_qt", [96, H, QTOT])
    KT_d = nc.dram_tensor("KT_d", [H, 96, 20 * SBK], BF16, kind="Internal").ap()
    V_d = nc.dram_tensor("V_d", [H, 80, 128, 65], BF16, kind="Internal").ap()
    mg_d = nc.dram_tensor("mg_d", [5, 128, 8, SBK], BF16, kind="Internal").ap()

    with contextlib.ExitStack() as es:
        S = Sched(nc, es)
        uid = [0]

        def sb(es_, shape, dt=F32, name=None):
            uid[0] += 1
            return es_.enter_context(nc.sbuf_tensor("%s_%d" % (name or "t", uid[0]), list(shape), dt))

        PX = es.enter_context(nc.psum_tensor("PX", [128, 1024], F32))
        PB = [es.enter_context(nc.psum_tensor("PB%d" % i, [128, 512], F32)) for i in range(6)]

        ident = sb(es, [128, 128], F32, "ident")
        identb = sb(es, [128, 128], BF16, "identb")
        att = sb(es, [128, 17, AW], BF16, "att")
        kmT = sb(es, [64, H, 32], F32, "kmT")
        kmTb = sb(es, [64, H, 32], BF16, "kmTb")
        kmx = sb(es, [128, H], F32, "kmx")
        kmxb = sb(es, [128, H], F32, "kmxb")
        trib = sb(es, [128, 4, SBK], BF16, "trib")
        ones_f = sb(es, [128, 128], F32, "ones_f")
        neghalf = sb(es, [128, 1], F32, "neghalf")
        qs_f = sb(es, [128, AW], F32, "qs_f")
        ks_f = sb(es, [128, AW], F32, "ks_f")
        vs_f = sb(es, [128, AW], F32, "vs_f")
        esQ = contextlib.ExitStack()
        QT = sb(esQ, [96, H, QTOT], BF16, "QT")

        S.add("sp", lambda e: e.dma_start(out=ident[:], in_=ident_d[:, :]), writes=["ident"], dma="c0")
        S.add("pool", lambda e: e.dma_start(out=identb[:], in_=ident_d[:, :]), writes=["identb"], dma="c1")
        S.add("pool", lambda e: e.dma_start(out=trib[:].rearrange("p a b -> p (a b)"), in_=tri[:, :]), writes=["trib"], dma="c2")
        S.add("dve", lambda e: e.memset(kmx[:], 0.0), writes=["kmx"])
        S.add("dve", lambda e: e.memset(ones_f[:], 1.0), writes=["ones_f"])
        S.add("dve", lambda e: e.memset(neghalf[:], -0.5), writes=["neghalf"])

        esA = contextlib.ExitStack()
        wq = sb(esA, [128, 8, 512], BF16, "wq")
        wk = sb(esA, [128, 8, 512], BF16, "wk")
        wv = sb(esA, [128, 8, 512], BF16, "wv")
        for nm, wt, c0 in (("wq", wq, 0), ("wk", wk, 512), ("wv", wv, 1024)):
            S.add("pool", lambda e, wt=wt, c0=c0: e.dma_start(
                out=wt[:], in_=w_in[:, c0:c0 + 512].rearrange("(c p) n -> p c n", p=128)),
                writes=[nm], dma="w" + nm)
        xt = [sb(esA, [128, D], F32, "xt") for _ in range(2)]
        cst = [sb(esA, [128, 64], F32, "cst") for _ in range(2)]
        xT = [sb(esA, [128, 8, 128], BF16, "xT") for _ in range(2)]
        r1 = sb(esA, [128, 256], F32, "r1")
        r2 = sb(esA, [128, 256], F32, "r2")
        r3 = sb(esA, [128, 256], F32, "r3")
        r4 = sb(esA, [128, 256], F32, "r4")
        kr = [sb(esA, [128, AW], F32, "kr") for _ in range(2)]
        vf = [sb(esA, [128, AW], F32, "vf") for _ in range(2)]
        qr = sb(esA, [128, AW], F32, "qr")
        kb = [sb(esA, [128, AW], BF16, "kb") for _ in range(2)]
        qb = sb(esA, [128, AW], BF16, "qb")
        sq = sb(esA, [128, AW], F32, "sq")
        ksq = sb(esA, [128, H], F32, "ksq")
        qsq = sb(esA, [128, H], F32, "qsq")
        KTs = [sb(esA, [96, H, SBK], BF16, "KTs") for _ in range(2)]
        Vs = [sb(esA, [128, 4, H, 65], BF16, "Vs") for _ in range(2)]
        for j in range(2):
            S.add("pool", lambda e, j=j: e.memset(Vs[j][:], 1.0), writes=["Vs%d" % j])
        sel_c = sb(esA, [128, 3, 32], F32, "sel_c")
        isab = sb(esA, [128, 2], F32, "isab")
        sc = sb(esA, [128, H, 32], F32, "sc")
        sel = sb(esA, [128, H, 32], F32, "sel")
        tmp3 = sb(esA, [128, H, 32], F32, "tmp3")
        mx8 = sb(esA, [128, H, 8], F32, "mx8")
        selA = sb(esA, [128, H], F32, "selA")
        tq = sb(esA, [128, H], F32, "tq")
        mbb = sb(esA, [128, H, 32], BF16, "mbb")

        PSQ, PST, PSS, PSM = ("PB4", "PB5", "PB4", "PB5")
        pq, pt_, ps_, pm_ = PB[4], PB[5], PB[4], PB[5]

        def rope(src_ps, src_key, cs_t, cs_key, dst, dst_key):
            s3 = src_ps[:].rearrange("p (h d) -> p h d", h=H)
            d3 = dst[:].rearrange("p (h d) -> p h d", h=H)
            cosb = cs_t[:, None, 0:32].to_broadcast([128, H, 32])
            sinb = cs_t[:, None, 32:64].to_broadcast([128, H, 32])
            a3 = r1[:].rearrange("p (h d) -> p h d", h=H)
            b3 = r2[:].rearrange("p (h d) -> p h d", h=H)
            c3 = r3[:].rearrange("p (h d) -> p h d", h=H)
            e3 = r4[:].rearrange("p (h d) -> p h d", h=H)
            S.add("dve", lambda e: e.tensor_tensor(out=a3, in0=s3[:, :, 0:32], in1=cosb, op=ALU.mult),
                  reads=[src_key, cs_key], writes=["r1"])
            S.add("dve", lambda e: e.tensor_tensor(out=b3, in0=s3[:, :, 32:64], in1=sinb, op=ALU.mult),
                  reads=[src_key, cs_key], writes=["r2"])
            S.add("dve", lambda e: e.tensor_tensor(out=c3, in0=s3[:, :, 32:64], in1=cosb, op=ALU.mult),
                  reads=[src_key, cs_key], writes=["r3"])
            S.add("dve", lambda e: e.tensor_tensor(out=e3, in0=s3[:, :, 0:32], in1=sinb, op=ALU.mult),
                  reads=[src_key, cs_key], writes=["r4"])
            S.add("pool", lambda e: e.tensor_tensor(out=d3[:, :, 0:32], in0=a3, in1=b3, op=ALU.subtract),
                  reads=["r1", "r2"], writes=[dst_key + "a"])
            S.add("pool", lambda e: e.tensor_tensor(out=d3[:, :, 32:64], in0=c3, in1=e3, op=ALU.add),
                  reads=["r3", "r4"], writes=[dst_key + "b"])

        gt = [0]
        late2 = [None]

        def load_tile(xsrc, cssrc):
            g = gt[0] % 2
            S.add("sp", lambda e: e.dma_start(out=xt[g][:], in_=xsrc), writes=["xt%d" % g], dma="xt%d" % g)
            S.add("sp", lambda e: e.dma_start(out=cst[g][:], in_=cssrc), writes=["cst%d" % g], dma="cst%d" % g)

        def proc_tile(t, kts_j, mode, qcol=None, orow=None, okv=None, after3=None):
            g = gt[0] % 2
            gt[0] += 1
            xk, ck, xTk = "xt%d" % g, "cst%d" % g, "xT%d" % g
            pk, pv = PB[g], PB[2 + g]
            PSK, PSV = "PB%d" % g, "PB%d" % (2 + g)

            def tr(e):
                for c in range(8):
                    i = e.transpose(PX[:, c * 128:(c + 1) * 128], xt[g][:, c * 128:(c + 1) * 128], ident[:])
                return i
            S.add("pe", tr, reads=[xk, "ident"], writes=["PX"])
            S.add("act", lambda e: e.activation(out=xT[g][:].rearrange("p c t -> p (c t)"), in_=PX[:], func=AF.Copy),
                  reads=["PX"], writes=[xTk])

            def proj(pbank, wt):
                def f(e):
                    for c in range(8):
                        i = e.matmul(pbank[:], xT[g][:, c, :], wt[:, c, :], start=(c == 0), stop=(c == 7))
                    return i
                return f
            S.add("pe", proj(pk, wk), reads=[xTk, "wk"], writes=[PSK])
            S.add("pe", proj(pv, wv), reads=[xTk, "wv"], writes=[PSV])
            if mode != "gen":
                S.add("pe", proj(pq, wq), reads=[xTk, "wq"], writes=[PSQ])
            kg = g
            krk = "kr%d" % kg

            def s2():
                rope(pk, PSK, cst[g], ck, kr[kg], krk)
                if mode != "smp":
                    vsl = Vs[kts_j][:, t, :, 0:64]
                    S.add("act", lambda e: e.activation(out=vsl, in_=pv[:].rearrange("p (h d) -> p h d", h=H), func=AF.Copy),
                          reads=[PSV], writes=["Vs%d" % kts_j])
                if mode != "gen":
                    S.add("act", lambda e: e.activation(out=vf[kg][:], in_=pv[:], func=AF.Copy),
                          reads=[PSV], writes=["vf%d" % kg])
                    S.add("sp", lambda e: e.dma_start(out=okv[0], in_=kr[kg][:]), reads=[krk + "a", krk + "b"], writes=["OUTk%d" % kg], dma="ok%d" % kg)
                    S.add("sp", lambda e: e.dma_start(out=okv[1], in_=vf[kg][:]), reads=["vf%d" % kg], writes=["OUTv%d" % kg], dma="ov%d" % kg)
                if mode == "smp":
                    return
                S.add("act", lambda e: e.activation(out=kb[kg][:], in_=kr[kg][:], func=AF.Copy), reads=[krk + "a", krk + "b"], writes=["kb%d" % kg])
                if late2[0] is not None:
                    late2[0]()
                    late2[0] = None
                S.add("pool", lambda e: e.tensor_tensor(out=sq[:], in0=kr[kg][:], in1=kr[kg][:], op=ALU.mult),
                      reads=[krk + "a", krk + "b"], writes=["sq"])

                def late():
                    S.add("dve", lambda e: e.tensor_reduce(out=ksq[:], in_=sq[:].rearrange("p (h d) -> p h d", h=H), axis=AX.X, op=ALU.add),
                          reads=["sq"], writes=["ksq"])
                    S.add("dve", lambda e: e.tensor_tensor(out=kmx[:], in0=kmx[:], in1=ksq[:], op=ALU.max),
                          reads=["ksq", "kmx"], writes=["kmx"])
                if mode == "gen":
                    late2[0] = late
                else:
                    late()

            def s3():
                if mode == "smp":
                    return
                ptb = pt_[:].bitcast(BF16)

                def trk(e):
                    for h in range(H):
                        i = e.transpose(ptb[0:64, h * 128:(h + 1) * 128], kb[kg][:, h * 64:(h + 1) * 64], identb[:])
                    return i
                S.add("pe", trk, reads=["kb%d" % kg, "identb"], writes=[PST])
                S.add("act", lambda e: e.activation(out=KTs[kts_j][0:64, :, t * 128:(t + 1) * 128],
                                                    in_=ptb[0:64, 0:1024].rearrange("p (h t) -> p h t", h=H), func=AF.Copy),
                      reads=[PST], writes=["KTs%d" % kts_j])
                if after3 is not None:
                    after3()
            if mode == "gen":
                return s2, s3
            s2()
            s3()
            return kg

        qcnt = [0]
        qsq2 = [qsq, sb(esA, [128, H], F32, "qsqb")]

        def q_tile(g_prev, qcol, selrow):
            qi = qcnt[0] % 2
            qcnt[0] += 1
            qsq = qsq2[qi]
            QSQ = "qsq%d" % qi
            rope(pq, PSQ, cst[g_prev], "cst%d" % g_prev, qr, "qr")
            S.add("pool", lambda e: e.tensor_copy(out=qb[:], in_=qr[:]), reads=["qra", "qrb"], writes=["qb"])
            S.add("pool", lambda e: e.tensor_tensor(out=sq[:], in0=qr[:], in1=qr[:], op=ALU.mult),
                  reads=["qra", "qrb"], writes=["sq"])
            S.add("dve", lambda e: e.tensor_reduce(out=qsq[:], in_=sq[:].rearrange("p (h d) -> p h d", h=H), axis=AX.X, op=ALU.add),
                  reads=["sq"], writes=[QSQ])
            ptb = pt_[:].bitcast(BF16)

            def trq(e):
                for h in range(H):
                    i = e.transpose(ptb[0:64, h * 128:(h + 1) * 128], qb[:, h * 64:(h + 1) * 64], identb[:])
                return i
            S.add("pe", trq, reads=["qb", "identb"], writes=[PST])
            S.add("act", lambda e: e.activation(out=QT[0:64, :, qcol:qcol + 128],
                                                in_=ptb[0:64, 0:1024].rearrange("p (h t) -> p h t", h=H), func=AF.Copy),
                  reads=[PST], writes=["QT"])
            if selrow is None:
                return None

            def selpart():
                S.add("sp", lambda e: e.dma_start(out=sel_c[:, 0, :], in_=pastneg[selrow:selrow + 128, :]), writes=["selc0"], dma="selc0")
                S.add("sp", lambda e: e.dma_start(out=sel_c[:, 1, :], in_=notown[selrow:selrow + 128, :]), writes=["selc1"], dma="selc1")
                S.add("sp", lambda e: e.dma_start(out=sel_c[:, 2, :], in_=ownA[selrow:selrow + 128, :]), writes=["selc2"], dma="selc2")
                S.add("sp", lambda e: e.dma_start(out=isab[:], in_=isAB[selrow:selrow + 128, :]), writes=["isab"], dma="isab")

                def scm(e):
                    for h in range(H):
                        i = e.matmul(ps_[:, h * 32:(h + 1) * 32], QT[0:64, h, qcol:qcol + 128], kmTb[:, h, :], start=True, stop=True)
                    return i
                S.add("pe", scm, reads=["QT", "kmTb"], writes=[PSS])
                S.add("dve", lambda e: e.tensor_tensor(out=sc[:], in0=ps_[:, 0:256].rearrange("p (h j) -> p h j", h=H),
                                                       in1=sel_c[:, 0:1, :].to_broadcast([128, H, 32]), op=ALU.add),
                      reads=[PSS, "selc0"], writes=["sc"])

                def mx(e):
                    for h in range(H):
                        i = e.max(out=mx8[:, h, :], in_=sc[:, h, :])
                    return i
                S.add("dve", mx, reads=["sc"], writes=["mx8"])
                S.add("dve", lambda e: e.tensor_tensor(out=sel[:], in0=sc[:], in1=mx8[:, :, 2:3].to_broadcast([128, H, 32]), op=ALU.is_ge),
                      reads=["sc", "mx8"], writes=["sel"])
                S.add("dve", lambda e: e.tensor_scalar(out=tmp3[:], in0=sc[:], scalar1=-1e29, scalar2=None, op0=ALU.is_gt),
                      reads=["sc"], writes=["tmp3"])
                S.add("dve", lambda e: e.tensor_tensor(out=sel[:], in0=sel[:], in1=tmp3[:], op=ALU.mult),
                      reads=["sel", "tmp3"], writes=["sel"])
                S.add("dve", lambda e: e.tensor_tensor(out=tmp3[:], in0=sel[:], in1=sel_c[:, 2:3, :].to_broadcast([128, H, 32]), op=ALU.mult),
                      reads=["sel", "selc2"], writes=["tmp3"])
                S.add("dve", lambda e: e.tensor_reduce(out=selA[:], in_=tmp3[:], axis=AX.X, op=ALU.add),
                      reads=["tmp3"], writes=["selA"])
                S.add("dve", lambda e: e.tensor_tensor(out=sel[:], in0=sel[:], in1=sel_c[:, 1:2, :].to_broadcast([128, H, 32]), op=ALU.mult),
                      reads=["sel", "selc1"], writes=["sel"])
                S.add("dve", lambda e: e.tensor_scalar(out=sel[:, :, 30], in0=selA[:], scalar1=isab[:, 0:1], scalar2=None, op0=ALU.add),
                      reads=["selA", "isab", "sel"], writes=["sel"])
                S.add("dve", lambda e: e.tensor_copy(out=sel[:, :, 31], in_=isab[:, 1:2].to_broadcast([128, H])),
                      reads=["isab", "sel"], writes=["sel"])
                S.add("dve", lambda e: e.tensor_tensor(out=tq[:], in0=qsq[:], in1=kmxb[:], op=ALU.add),
                      reads=[QSQ, "kmxb"], writes=["tq"])
                S.add("dve", lambda e: e.tensor_scalar(out=tq[:], in0=tq[:], scalar1=-0.5, scalar2=BIG, op0=ALU.mult, op1=ALU.add),
                      reads=["tq"], writes=["tq"])
                S.add("dve", lambda e: e.tensor_tensor(out=tmp3[:], in0=sel[:], in1=tq[:, :, None].to_broadcast([128, H, 32]), op=ALU.mult),
                      reads=["sel", "tq"], writes=["tmp3"])
                S.add("dve", lambda e: e.tensor_scalar(out=mbb[:], in0=tmp3[:], scalar1=-BIG, scalar2=None, op0=ALU.add),
                      reads=["tmp3"], writes=["mbb"])

                def trm(e):
                    for h in range(H):
                        i = e.transpose(ptb[64:96, h * 128:(h + 1) * 128], mbb[:, h, :], identb[:])
                    return i
                S.add("pe", trm, reads=["mbb", "identb"], writes=[PST])
                S.add("act", lambda e: e.activation(out=QT[64:96, :, qcol:qcol + 128],
                                                    in_=ptb[64:96, 0:1024].rearrange("p (h t) -> p h t", h=H), func=AF.Copy),
                      reads=[PST], writes=["QT"])

            return selpart

        def finish_block(kts_j, slot, oh_src, gen_s):
            S.add("pool", lambda e: e.dma_start(out=KTs[kts_j][64:96, :, :], in_=oh_src[:, None, :].to_broadcast([32, H, SBK])),
                  writes=["KTs%d" % kts_j], dma="oh%d" % kts_j)
            if gen_s is not None:
                S.add("dve", lambda e: e.tensor_reduce(out=kmT[:, :, 2 * gen_s:2 * gen_s + 2],
                                                       in_=KTs[kts_j][0:64, :, :].rearrange("p h (b k) -> p h b k", b=2),
                                                       axis=AX.X, op=ALU.add),
                      reads=["KTs%d" % kts_j], writes=["kmT"])
            S.add("sp", lambda e: e.dma_start(out=KT_d[:, :, slot * SBK:(slot + 1) * SBK].rearrange("h r k -> r h k"), in_=KTs[kts_j][:]),
                  reads=["KTs%d" % kts_j], writes=["KTd%d" % slot], dma="kst%d" % kts_j)
            for t4 in range(4):
                S.add("sp", lambda e, t4=t4: e.dma_start(out=V_d[:, slot * 4 + t4, :, :].rearrange("h p e -> p h e"), in_=Vs[kts_j][:, t4, :, :]),
                      reads=["Vs%d" % kts_j], writes=["Vd%d_%d" % (slot, t4)], dma="vst%d_%d" % (kts_j, t4))

        blk = 16
        tiles = [(s_, t_) for s_ in range(16) for t_ in range(4)]
        pend = []

        def ld(n_):
            s_, t_ = tiles[n_]
            r0_ = s_ * SBK + t_ * 128
            load_tile(xall[r0_:r0_ + 128, :], cs_all[r0_:r0_ + 128, :])
        ld(0)
        for n_, (s_, t_) in enumerate(tiles):
            j_ = s_ % 2
            fin = (lambda j_=j_, s_=s_: finish_block(j_, s_, oh_all[s_], s_)) if t_ == 3 else None
            st = proc_tile(t_, j_, "gen", after3=fin)
            if len(pend) >= 1:
                pend[-1][0]()
            if n_ + 1 < len(tiles):
                ld(n_ + 1)
            if len(pend) >= 2:
                pend[-2][1]()
            pend.append(st)
        pend[-1][0]()
        pend[-2][1]()
        pend[-1][1]()
        if late2[0] is not None:
            late2[0]()
            late2[0] = None

        pmx = pm_
        S.add("pe", lambda e: e.transpose(pmx[0:8, 0:128], kmx[:], ident[:]), reads=["kmx", "ident"], writes=[PSM])
        kmr = sb(esA, [8, 1], F32, "kmr")
        kmd = sb(esA, [8, 8], F32, "kmd")
        S.add("dve", lambda e: e.tensor_reduce(out=kmr[:], in_=pmx[0:8, 0:128], axis=AX.X, op=ALU.max), reads=[PSM], writes=["kmr"])
        S.add("dve", lambda e: e.tensor_scalar(out=kmd[:], in0=ident[0:8, 0:8], scalar1=kmr[:, 0:1], scalar2=None, op0=ALU.mult),
              reads=["kmr", "ident"], writes=["kmd"])
        S.add("pe", lambda e: e.matmul(pmx[:, 0:8], ones_f[0:8, :], kmd[:], start=True, stop=True), reads=["kmd", "ones_f"], writes=[PSM])
        S.add("dve", lambda e: e.tensor_copy(out=kmxb[:], in_=pmx[:, 0:8]), reads=[PSM], writes=["kmxb"])
        S.add("pool", lambda e: e.tensor_copy(out=kmTb[:], in_=kmT[:]), reads=["kmT"], writes=["kmTb"])

        pend_sel = [None]
        for i in range(4):
            j = blk % 2
            blk += 1
            for t in range(4):
                r0 = i * SBK + t * 128
                load_tile(xown[r0:r0 + 128, :], cs_own[r0:r0 + 128, :])
                g_prev = gt[0] % 2
                proc_tile(t, j, "own", okv=(k_own[r0:r0 + 128, :], v_own[r0:r0 + 128, :]))
                sp_ = q_tile(g_prev, r0, r0)
                if pend_sel[0] is not None:
                    pend_sel[0]()
                pend_sel[0] = sp_
            finish_block(j, 16 + i, oh_own, None)
        pend_sel[0]()
        load_tile(xs[:, :], cs_s[:, :])
        g_prev = gt[0] % 2
        kg_s = proc_tile(0, 0, "smp", okv=(k_s[:, :], v_s[:, :]))
        q_tile(g_prev, 2048, None)
        S.add("pool", lambda e: e.tensor_copy(out=qs_f[:], in_=qr[:]), reads=["qra", "qrb"], writes=["qs_f"])
        S.add("pool", lambda e: e.tensor_copy(out=ks_f[:], in_=kr[kg_s][:]), reads=["kr%da" % kg_s, "kr%db" % kg_s], writes=["ks_f"])
        S.add("pool", lambda e: e.tensor_copy(out=vs_f[:], in_=vf[kg_s][:]), reads=["vf%d" % kg_s], writes=["vs_f"])
        S.barrier()
        esA.close()

        esP = contextlib.ExitStack()
        ck16 = cache_k.rearrange("(n r) d -> n (r d)", r=64)
        ptT_i = sb(esP, [128, NSEQ], I32, "ptT_i")
        ptf = sb(esP, [128, NSEQ], F32, "ptf")
        idxc = sb(esP, [128, NSEQ, 16], I32, "idxc")
        pgs4 = sb(esP, [128, NSEQ, AW], F32, "pgs4")
        pg1 = [sb(esP, [128, AW], F32, "pg1") for _ in range(2)]
        chb = [sb(esP, [128, 4096], F32, "ch") for _ in range(2)]
        if not DEV_SKIP_DECODE:
            S.add("sp", lambda e: e.dma_start(out=ptT_i[:], in_=ptab.rearrange("s p -> p s"), allow_slow_non_contiguous=True), writes=["ptT_i"], dma="d_pt")
            S.add("dve", lambda e: e.tensor_copy(out=ptf[:], in_=ptT_i[:]), reads=["ptT_i"], writes=["ptf"])
            for c in range(16):
                S.add("dve", lambda e, c=c: e.tensor_scalar(out=idxc[:, :, c], in0=ptf[:], scalar1=16.0, scalar2=float(c), op0=ALU.mult, op1=ALU.add),
                      reads=["ptf"], writes=["idxc"])
            S.add("pool", lambda e: e.memset(pgs4[:], 0.0), writes=["pgs4"])

        def ps_dma(k):
            s_, c_ = k // 16, k % 16
            j = k % 2
            S.add("pool", lambda e: e.indirect_dma_start(
                out=chb[j][:], out_offset=None, in_=ck16[:, :], in_offset=bass.IndirectOffsetOnAxis(ap=idxc[:, s_, c_:c_ + 1], axis=0)),
                reads=["idxc"], writes=["ch%d" % j], dma="d_ch%d" % j)

        def ps_red(k):
            j = k % 2
            S.add("dve", lambda e: e.tensor_reduce(out=pg1[j][:], in_=chb[j][:].rearrange("p (r d) -> p d r", r=8), axis=AX.X, op=ALU.add),
                  reads=["ch%d" % j], writes=["pg1_%d" % j])

        def ps_add(k):
            s_ = k // 16
            j = k % 2
            S.add("pool", lambda e: e.tensor_tensor(out=pgs4[:, s_, :], in0=pgs4[:, s_, :], in1=pg1[j][:], op=ALU.add),
                  reads=["pgs4", "pg1_%d" % j], writes=["pgs4"])

        def ps_step(k):
            if DEV_SKIP_DECODE:
                return
            if 0 <= k < 64:
                ps_dma(k)
            if 0 <= k - 1 < 64:
                ps_red(k - 1)
            if 0 <= k - 2 < 64:
                ps_add(k - 2)

        esB = contextlib.ExitStack()
        NKB = 4
        Kc = [sb(esB, [96, SBK], BF16, "Kc") for _ in range(NKB)]
        Vc = [sb(esB, [128, 4, 65], BF16, "Vc") for _ in range(NKB)]
        Pt = [sb(esB, [128, SBK], BF16, "Pt") for _ in range(3)]
        rs = sb(esB, [128, 4], F32, "rs")
        sbanks = [(PB[0], "PB0"), (PB[1], "PB1"), (PB[2], "PB2")]
        accs = [(PX[:, 0:512], "PXa"), (PX[:, 512:1024], "PXb")]
        groups = []
        hi = 0
        for i in range(4):
            slots = list(range(KMAX[i])) + [16 + i]
            for h in range(H):
                for si, slot in enumerate(slots):
                    groups.append((i, h, si, slot, len(slots), hi))
                hi += 1

        def gload(gi_):
            i, h, si, slot, ns, hidx = groups[gi_]
            b = gi_ % NKB
            S.add("sp", lambda e: e.dma_start(out=Kc[b][:], in_=KT_d[h, :, slot * SBK:(slot + 1) * SBK]),
                  reads=["KTd%d" % slot], writes=["Kc%d" % b], dma="kc%d" % b)
            S.add("sp", lambda e: e.dma_start(out=Vc[b][:], in_=V_d[h, slot * 4:(slot + 1) * 4, :, :].rearrange("t p e -> p t e")),
                  reads=["Vd%d_%d" % (slot, t4) for t4 in range(4)], writes=["Vc%d" % b], dma="vc%d" % b)

        backs = []

        def front(gi_, kt, n_):
            i, h, si, slot, ns, hidx = groups[gi_]
            b = gi_ % NKB
            sbk, sbkk = sbanks[n_ % 3]
            p_i = n_ % 3
            acc, acck = accs[hidx % 2]
            acc3 = acc.rearrange("p (q e) -> p q e", q=4)
            S.add("pe", lambda e: e.matmul(sbk[:], Kc[b][:, kt * 128:(kt + 1) * 128], QT[:, h, i * SBK:(i + 1) * SBK], start=True, stop=True),
                  reads=["Kc%d" % b, "QT"], writes=[sbkk])
            S.add("act", lambda e: e.activation(out=Pt[p_i][:], in_=sbk[:], func=AF.Exp, scale=0.125),
                  reads=[sbkk], writes=["Pt%d" % p_i])
            if slot >= 16:
                S.add("pool", lambda e: e.tensor_tensor(out=Pt[p_i][:], in0=Pt[p_i][:], in1=trib[:, kt, :], op=ALU.mult),
                      reads=["Pt%d" % p_i, "trib"], writes=["Pt%d" % p_i])
            first = (si == 0 and kt == 0)
            last = (si == ns - 1 and kt == 3)

            def back():
                def pv_(e):
                    for qt in range(4):
                        i_ = e.matmul(acc3[:, qt, 0:65], Pt[p_i][:, qt * 128:(qt + 1) * 128], Vc[b][:, kt, :],
                                      start=(first and qt == 0), stop=last, skip_group_check=True)
                    return i_
                S.add("pe", pv_, reads=["Pt%d" % p_i, "Vc%d" % b], writes=[acck])
                if last:
                    S.add("dve", lambda e: e.reciprocal(out=rs[:], in_=acc3[:, :, 64]), reads=[acck], writes=["rs"])
                    for qt in range(4):
                        S.add("dve", lambda e, qt=qt: e.tensor_scalar(
                            out=att[:, i * 4 + qt, h * 64:(h + 1) * 64], in0=acc3[:, qt, 0:64], scalar1=rs[:, qt:qt + 1], scalar2=None, op0=ALU.mult),
                            reads=[acck, "rs"], writes=["att"])
            return back

        gload(0)
        gload(1)
        n_ = 0
        for gi_ in range(len(groups)):
            if gi_ + 2 < len(groups):
                gload(gi_ + 2)
            if gi_ % 4 == 0:
                ps_step(gi_ // 4)
            for kt in range(4):
                backs.append(front(gi_, kt, n_))
                if n_ >= 2:
                    backs[n_ - 2]()
                n_ += 1
        backs[n_ - 2]()
        backs[n_ - 1]()
        S.barrier()
        esB.close()

        if DEBUG:
            S.add("pool", lambda e: e.dma_start(out=dbg_att[:, :, :], in_=att[:]), reads=["att"], writes=["OUTdbga"], dma="dbga")
            for hh_ in range(H):
                S.add("pool", lambda e, hh_=hh_: e.dma_start(out=dbg_qt[:, hh_, :], in_=QT[:, hh_, :]), reads=["QT"], writes=["OUTdbgq%d" % hh_], dma="dbgq")
        esD = contextlib.ExitStack()
        if not DEV_SKIP_DECODE:
            decode_attention(nc, S, esD, sb, locals())
        else:
            S.add("pool", lambda e: e.memset(att[:, 16, :], 0.0), writes=["att"])
        S.barrier()
        esD.close()
        esP.close()
        esQ.close()

        phase_front(nc, S, sb, locals())
        S.barrier()
        phase_tail(nc, S, sb, locals())

        if DEBUG:
            for u_ in range(5):
                S.add("pool", lambda e, u_=u_: e.dma_start(out=dbg_mg[u_], in_=mg_d[u_]), reads=["mgd%d" % u_], writes=["OUTdbgm%d" % u_], dma="dbgm")
        outs = [k for k in S.res if k.startswith("OUT")]
        S.add("sp", lambda e: e.nop(), reads=outs)
        with nc.Block() as block:
            S.emit(block)
    return nc


def decode_attention(nc, S, es_, sb, L):
    cache_k, cache_v, ptab = L["cache_k"], L["cache_v"], L["ptab"]
    pairm, iota128, hsel, pofs = L["pairm"], L["iota128"], L["hsel"], L["pofs"]
    PX, PB, ident, ones_f, att = L["PX"], L["PB"], L["ident"], L["ones_f"], L["att"]
    qs_f, ks_f, vs_f = L["qs_f"], L["ks_f"], L["vs_f"]
    pgs4 = L["pgs4"]
    selS = sb(es_, [128, NSEQ, 128], F32, "selS")
    pair_sb = sb(es_, [128, 64], F32, "pair_sb")
    iota8 = sb(es_, [8, 128], F32, "iota8")
    hsel_sb = sb(es_, [8, 48], F32, "hsel_sb")
    pofs_sb = sb(es_, [128, 48], F32, "pofs_sb")
    ptr_i = sb(es_, [8, 128], I32, "ptr_i")
    ptr_f = sb(es_, [8, 128], F32, "ptr_f")
    qbc = sb(es_, [128, AW], F32, "qbc")
    vbc = sb(es_, [128, AW], F32, "vbc")
    tmpq = sb(es_, [128, AW], F32, "tmpq")
    spg = sb(es_, [128, H], F32, "spg")
    ssa = sb(es_, [128, H], F32, "ssa")
    sbc = sb(es_, [128, H], F32, "sbc")
    bsc = sb(es_, [8, 64], F32, "bsc")
    mx8 = sb(es_, [8, 8], F32, "dmx8")
    ix8 = sb(es_, [8, 8], U32, "dix8")
    ixf = sb(es_, [8, 8], F32, "dixf")
    lp = sb(es_, [8, 6], F32, "lp")
    eqt = sb(es_, [8, 128], F32, "eqt")
    phys = sb(es_, [8, 6], F32, "phys")
    physd = sb(es_, [8, 48], F32, "physd")
    gidx = sb(es_, [128, 48], I32, "gidx")
    Kg = sb(es_, [128, H, 6, 64], F32, "Kg")
    Vg = sb(es_, [128, H, 6, 65], F32, "Vg")
    tmpk = sb(es_, [128, H, 6, 64], F32, "tmpk")
    sk = sb(es_, [128, 48], F32, "sk")
    m48 = sb(es_, [48, 1], F32, "m48")
    mh = sb(es_, [1, H], F32, "mh")
    mb = sb(es_, [128, H], F32, "mb")
    Pk = sb(es_, [128, 48], F32, "Pk")
    pself = sb(es_, [128, H], F32, "pself")
    orow = sb(es_, [1, H, 65], F32, "orow")
    den = sb(es_, [1, H], F32, "den")
    arow = sb(es_, [1, AW], F32, "arow")

    S.add("pool", lambda e: e.memset(att[:, 16, :], 0.0), writes=["att"])
    S.add("sp", lambda e: e.dma_start(out=pair_sb[:], in_=pairm[:, :]), writes=["pair_sb"], dma="d_c1")
    S.add("sp", lambda e: e.dma_start(out=iota8[:], in_=iota128[:, :]), writes=["iota8"], dma="d_c2")
    S.add("sp", lambda e: e.dma_start(out=hsel_sb[:], in_=hsel[:, :]), writes=["hsel_sb"], dma="d_c3")
    S.add("sp", lambda e: e.dma_start(out=pofs_sb[:], in_=pofs[:, :]), writes=["pofs_sb"], dma="d_c4")
    for s in range(NSEQ):
        S.add("dve", lambda e, s=s: e.tensor_copy(out=selS[:, s, :], in_=ident[:, s:s + 1].to_broadcast([128, 128])), reads=["ident"], writes=["selS"])
    S.add("pool", lambda e: e.memset(Vg[:], 1.0), writes=["Vg"])
    S.add("dve", lambda e: e.tensor_tensor(out=tmpq[:], in0=qs_f[:], in1=ks_f[:], op=ALU.mult), reads=["qs_f", "ks_f"], writes=["tmpq"])
    S.add("dve", lambda e: e.tensor_reduce(out=ssa[:], in_=tmpq[:].rearrange("p (h d) -> p h d", h=H), axis=AX.X, op=ALU.add), reads=["tmpq"], writes=["ssa"])
    gi = 0
    for s in range(NSEQ):
        S.add("pe", lambda e, s=s: e.matmul(PB[0][:], selS[:, s, :], qs_f[:], start=True, stop=True), reads=["selS", "qs_f"], writes=["PB0"])
        S.add("act", lambda e: e.activation(out=qbc[:], in_=PB[0][:], func=AF.Copy), reads=["PB0"], writes=["qbc"])
        S.add("pe", lambda e, s=s: e.matmul(PB[1][:], selS[:, s, :], vs_f[:], start=True, stop=True), reads=["selS", "vs_f"], writes=["PB1"])
        S.add("act", lambda e: e.activation(out=vbc[:], in_=PB[1][:], func=AF.Copy), reads=["PB1"], writes=["vbc"])
        S.add("pe", lambda e, s=s: e.matmul(PB[2][:, 0:H], selS[:, s, :], ssa[:], start=True, stop=True), reads=["selS", "ssa"], writes=["PB2"])
        S.add("act", lambda e: e.activation(out=sbc[:], in_=PB[2][:, 0:H], func=AF.Copy), reads=["PB2"], writes=["sbc"])
        S.add("dve", lambda e, s=s: e.tensor_tensor(out=tmpq[:], in0=pgs4[:, s, :], in1=qbc[:], op=ALU.mult), reads=["pgs4", "qbc"], writes=["tmpq"])
        S.add("dve", lambda e: e.tensor_reduce(out=spg[:], in_=tmpq[:].rearrange("p (h d) -> p h d", h=H), axis=AX.X, op=ALU.add), reads=["tmpq"], writes=["spg"])
        S.add("pe", lambda e: e.matmul(PB[3][0:8, 0:64], spg[:], pair_sb[:], start=True, stop=True), reads=["spg", "pair_sb"], writes=["PB3"])
        S.add("dve", lambda e: e.tensor_copy(out=bsc[:], in_=PB[3][0:8, 0:64]), reads=["PB3"], writes=["bsc"])
        S.add("dve", lambda e: e.max(out=mx8[:], in_=bsc[:]), reads=["bsc"], writes=["dmx8"])
        S.add("dve", lambda e: e.max_index(out=ix8[:], in_max=mx8[:], in_values=bsc[:]), reads=["dmx8", "bsc"], writes=["dix8"])
        S.add("dve", lambda e: e.tensor_copy(out=ixf[:], in_=ix8[:]), reads=["dix8"], writes=["dixf"])
        lp3 = lp[:].rearrange("p (k e) -> p k e", e=2)
        for e2 in range(2):
            S.add("dve", lambda e, e2=e2: e.tensor_scalar(out=lp3[:, :, e2], in0=ixf[:, 0:3], scalar1=2.0, scalar2=float(e2), op0=ALU.mult, op1=ALU.add),
                  reads=["dixf"], writes=["lp"])
        S.add("sp", lambda e, s=s: e.dma_start(out=ptr_i[:], in_=ptab[s:s + 1, :].to_broadcast([8, NPAGE])), writes=["ptr_i"], dma="d_ptr")
        S.add("dve", lambda e: e.tensor_copy(out=ptr_f[:], in_=ptr_i[:]), reads=["ptr_i"], writes=["ptr_f"])
        for sl in range(6):
            S.add("dve", lambda e, sl=sl: e.scalar_tensor_tensor(out=eqt[:], in0=iota8[:], scalar=lp[:, sl:sl + 1], in1=ptr_f[:],
                                                                 op0=ALU.is_equal, op1=ALU.mult, accum_out=phys[:, sl:sl + 1]),
                  reads=["iota8", "lp", "ptr_f"], writes=["eqt", "phys"])
        S.add("dve", lambda e: e.tensor_tensor(out=physd[:].rearrange("p (h k) -> p h k", h=H), in0=hsel_sb[:].rearrange("p (h k) -> p h k", h=H),
                                               in1=phys[:, None, :].to_broadcast([8, H, 6]), op=ALU.mult),
              reads=["hsel_sb", "phys"], writes=["physd"])
        S.add("pe", lambda e: e.matmul(PB[4][:, 0:48], ones_f[0:8, :], physd[:], start=True, stop=True), reads=["ones_f", "physd"], writes=["PB4"])
        S.add("dve", lambda e: e.scalar_tensor_tensor(out=gidx[:], in0=PB[4][:, 0:48], scalar=1024.0, in1=pofs_sb[:], op0=ALU.mult, op1=ALU.add),
              reads=["PB4", "pofs_sb"], writes=["gidx"])
        for h in range(H):
            for sl in range(6):
                col = h * 6 + sl
                S.add("pool", lambda e, h=h, sl=sl, col=col: e.indirect_dma_start(
                    out=Kg[:, h, sl, :], out_offset=None, in_=cache_k[:, :], in_offset=bass.IndirectOffsetOnAxis(ap=gidx[:, col:col + 1], axis=0)),
                    reads=["gidx"], writes=["Kg%d" % col], dma="d_kg%d" % (col % 8))
                S.add("pool", lambda e, h=h, sl=sl, col=col: e.indirect_dma_start(
                    out=Vg[:, h, sl, 0:64], out_offset=None, in_=cache_v[:, :], in_offset=bass.IndirectOffsetOnAxis(ap=gidx[:, col:col + 1], axis=0)),
                    reads=["gidx"], writes=["Vg%d" % col], dma="d_vg%d" % (col % 8))
        kgk = ["Kg%d" % c_ for c_ in range(48)]
        vgk = ["Vg%d" % c_ for c_ in range(48)]
        S.add("dve", lambda e: e.tensor_tensor(out=tmpk[:], in0=Kg[:], in1=qbc[:].rearrange("p (h d) -> p h d", h=H)[:, :, None, :].to_broadcast([128, H, 6, 64]), op=ALU.mult),
              reads=kgk + ["qbc"], writes=["tmpk"])
        S.add("dve", lambda e: e.tensor_reduce(out=sk[:], in_=tmpk[:].rearrange("p h k d -> p (h k) d"), axis=AX.X, op=ALU.add), reads=["tmpk"], writes=["sk"])
        S.add("pe", lambda e: e.transpose(PB[5][0:48, 0:128], sk[:], ident[:]), reads=["sk", "ident"], writes=["PB5"])
        S.add("dve", lambda e: e.tensor_reduce(out=m48[:], in_=PB[5][0:48, 0:128], axis=AX.X, op=ALU.max), reads=["PB5"], writes=["m48"])
        S.add("pe", lambda e: e.transpose(PB[5][0:1, 0:48], m48[:], ident[0:48, 0:48]), reads=["m48", "ident"], writes=["PB5"])
        S.add("dve", lambda e: e.tensor_reduce(out=mh[:], in_=PB[5][0:1, 0:48].rearrange("p (h k) -> p h k", h=H), axis=AX.X, op=ALU.max), reads=["PB5"], writes=["mh"])
        S.add("dve", lambda e: e.tensor_tensor(out=mh[:], in0=mh[:], in1=sbc[0:1, :], op=ALU.max), reads=["mh", "sbc"], writes=["mh"])
        S.add("pe", lambda e: e.matmul(PB[5][:, 0:H], ones_f[0:1, :], mh[:], start=True, stop=True), reads=["ones_f", "mh"], writes=["PB5"])
        S.add("dve", lambda e: e.tensor_copy(out=mb[:], in_=PB[5][:, 0:H]), reads=["PB5"], writes=["mb"])
        S.add("dve", lambda e: e.tensor_tensor(out=sk[:].rearrange("p (h k) -> p h k", h=H), in0=sk[:].rearrange("p (h k) -> p h k", h=H),
                                               in1=mb[:, :, None].to_broadcast([128, H, 6]), op=ALU.subtract), reads=["sk", "mb"], writes=["sk"])
        S.add("act", lambda e: e.activation(out=Pk[:], in_=sk[:], func=AF.Exp, scale=0.125), reads=["sk"], writes=["Pk"])
        S.add("dve", lambda e: e.tensor_tensor(out=pself[:], in0=sbc[:], in1=mb[:], op=ALU.subtract), reads=["sbc", "mb"], writes=["pself"])
        S.add("act", lambda e: e.activation(out=pself[:], in_=pself[:], func=AF.Exp, scale=0.125), reads=["pself"], writes=["pself"])

        def pv(e):
            for h in range(H):
                for sl in range(6):
                    i = e.matmul(PX[0:1, h * 128:h * 128 + 65], Pk[:, h * 6 + sl:h * 6 + sl + 1], Vg[:, h, sl, :], start=(sl == 0), stop=(sl == 5),
                                 skip_group_check=True)
            return i
        S.add("pe", pv, reads=["Pk"] + vgk, writes=["PX"])
        o3 = PX[0:1, :].rearrange("p (h e) -> p h e", h=H)
        S.add("dve", lambda e: e.tensor_tensor(out=orow[:, :, 0:64], in0=vbc[0:1, :].rearrange("p (h d) -> p h d", h=H),
                                               in1=pself[0:1, :, None].to_broadcast([1, H, 64]), op=ALU.mult), reads=["vbc", "pself"], writes=["orow"])
        S.add("dve", lambda e: e.tensor_tensor(out=orow[:, :, 0:64], in0=orow[:, :, 0:64], in1=o3[:, :, 0:64], op=ALU.add), reads=["orow", "PX"], writes=["orow"])
        S.add("dve", lambda e: e.tensor_tensor(out=den[:], in0=pself[0:1, :], in1=o3[:, :, 64], op=ALU.add), reads=["pself", "PX"], writes=["den"])
        S.add("dve", lambda e: e.reciprocal(out=den[:], in_=den[:]), reads=["den"], writes=["den"])
        S.add("dve", lambda e: e.tensor_tensor(out=arow[:].rearrange("p (h d) -> p h d", h=H), in0=orow[:, :, 0:64],
                                               in1=den[:, :, None].to_broadcast([1, H, 64]), op=ALU.mult), reads=["orow", "den"], writes=["arow"])
        S.add("pool", lambda e, s=s: e.dma_start(out=att[s:s + 1, 16, :], in_=arow[:]), reads=["arow", "att"], writes=["att"], dma="d_att")


def _rope_tab(pos):
    half = 32
    inv = (10000.0 ** (-np.arange(half, dtype=np.float32) / half)).astype(np.float32)
    ang = pos.astype(np.float32)[:, None] * inv[None, :]
    return np.concatenate([np.cos(ang), np.sin(ang)], axis=1).astype(np.float32)


_NC = None


def kernel(x_prompt, x_sample, cache_k, cache_v, state_pool, page_table, p_prompt, p_sample,
           w_in, w_pool_mix, pool_scale, w_pool_out, w_att_out, w_o, ln_g, ln_b, w_ple, w_ple_gate):
    global _NC
    f = lambda a: np.ascontiguousarray(np.asarray(a, dtype=np.float32))
    x_prompt = f(x_prompt); x_sample = f(x_sample); p_prompt = f(p_prompt); p_sample = f(p_sample)
    ck = f(cache_k).reshape(NPHYS * 128 * 8, 64)
    cv = f(cache_v).reshape(NPHYS * 128 * 8, 64)
    state_pool = f(state_pool)
    page_table = np.ascontiguousarray(np.asarray(page_table, dtype=np.int32))
    cs_all = _rope_tab(np.arange(SEQ))
    cs_s = _rope_tab(np.full((128,), PAST))
    ident = np.eye(128, dtype=np.float32)
    oh_all = np.zeros((16, 32, SBK), np.float32)
    for s in range(15):
        oh_all[s, 2 * s, :256] = 1.0
        oh_all[s, 2 * s + 1, 256:] = 1.0
    oh_own = np.zeros((32, SBK), np.float32)
    oh_own[30, :256] = 1.0
    oh_own[31, 256:] = 1.0
    tri = np.ones((128, 4, SBK), np.float32)
    for kt in range(4):
        for k in range(128):
            kk = kt * 128 + k
            kb_, kl = kk // 256, kk % 256
            q = np.arange(SBK)
            same = (q // 256) == kb_
            tri[k, kt, :] = np.where(same & ((q % 256) < kl), 0.0, 1.0)
    tri = tri.reshape(128, 4 * SBK)
    pairm = np.zeros((128, 64), np.float32)
    pairm[np.arange(128), np.arange(128) // 2] = 1.0
    iota128 = np.tile(np.arange(128, dtype=np.float32)[None, :], (8, 1))
    hsel = np.zeros((8, 48), np.float32)
    for h in range(8):
        hsel[h, h * 6:(h + 1) * 6] = 1.0
    shared = dict(cs_all=cs_all, cs_s=cs_s, oh_all=oh_all, oh_own=oh_own, tri=tri, ident=ident,
                  w_in=f(w_in)[0], w_mix=f(w_pool_mix)[0], pscale=f(pool_scale)[0], w_po=f(w_pool_out)[0],
                  w_ao=f(w_att_out)[0], w_o=f(w_o)[0], ln_g=f(ln_g)[0], ln_b=f(ln_b)[0], w_ple=f(w_ple)[0],
                  w_pg=f(w_ple_gate)[0], cache_k=ck, cache_v=cv, pairm=pairm, iota128=iota128, hsel=hsel,
                  pofs=(np.arange(128, dtype=np.float32)[:, None] * 8 + np.repeat(np.arange(8, dtype=np.float32), 6)[None, :]).astype(np.float32))
    in_maps = []
    for c in range(NCORES):
        b, r = c // 4, c % 4
        sbs = own_sbs(r)
        tok = np.concatenate([np.arange(s * SBK, (s + 1) * SBK) for s in sbs])
        xown = x_prompt[b][tok]
        xhalo = np.zeros((16, 16, D), np.float32)
        corr = np.ones((16, 4, 16), np.float32)
        for ti in range(16):
            t0 = tok[ti * 128]
            for k in range(16):
                p = t0 - 16 + k
                if p >= 0:
                    xhalo[ti, k] = x_prompt[b, p]
            for g, w in enumerate((2, 4, 8, 16)):
                for k in range(16):
                    corr[ti, g, k] = w / min(t0 + k + 1, w)
        qblk = tok // 256
        jj = np.arange(32)[None, :]
        pastneg = np.where(jj < qblk[:, None], 0.0, -1e30).astype(np.float32)
        sbq = tok // SBK
        notown = np.where((jj // 2) == sbq[:, None], 0.0, 1.0).astype(np.float32)
        ownA = (jj == (2 * sbq)[:, None]).astype(np.float32)
        inA = ((tok % SBK) < 256)
        isAB = np.stack([inA, ~inA], axis=1).astype(np.float32)
        xs = np.zeros((128, D), np.float32); xs[:NSEQ] = x_sample[c * NSEQ:(c + 1) * NSEQ, 0]
        pss = np.zeros((128, PLE), np.float32); pss[:NSEQ] = p_sample[0, c * NSEQ:(c + 1) * NSEQ, 0]
        m = dict(shared)
        m.update(xall=x_prompt[b], xown=np.ascontiguousarray(xown), xhalo=xhalo, pown=np.ascontiguousarray(p_prompt[0, b][tok]),
                 xs=xs, pss=pss, cs_own=np.ascontiguousarray(cs_all[tok]), pastneg=pastneg, notown=notown, ownA=ownA,
                 isAB=isAB, corr=corr, ptab=np.ascontiguousarray(page_table[c * NSEQ:(c + 1) * NSEQ]),
                 spool=np.ascontiguousarray(state_pool[0, c * NSEQ:(c + 1) * NSEQ]))
        in_maps.append(m)
    if _NC is None:
        _NC = build_nc()
    res = run_bass_kernel_spmd(_NC, in_maps, core_ids=list(range(NCORES)))
    R = res.results
    global _last
    _last = R
    y_prompt = np.zeros((2, SEQ, D), np.float32)
    k_prompt = np.zeros((1, 2, SEQ, H, HD), np.float32)
    v_prompt = np.zeros((1, 2, SEQ, H, HD), np.float32)
    pool_prompt = np.zeros((1, 2, 15, 512), np.float32)
    y_sample = np.zeros((32, 1, D), np.float32)
    k_sample = np.zeros((1, 32, 1, H, HD), np.float32)
    v_sample = np.zeros((1, 32, 1, H, HD), np.float32)
    pool_sample = np.zeros((1, 32, 15, 512), np.float32)
    for c in range(NCORES):
        b, r = c // 4, c % 4
        tok = np.concatenate([np.arange(s * SBK, (s + 1) * SBK) for s in own_sbs(r)])
        y_prompt[b, tok] = R[c]["y_own"]
        k_prompt[0, b, tok] = R[c]["k_own"].reshape(2048, H, HD)
        v_prompt[0, b, tok] = R[c]["v_own"].reshape(2048, H, HD)
        if r == 0:
            pool_prompt[0, b] = R[c]["poolp"]
        y_sample[c * NSEQ:(c + 1) * NSEQ, 0] = R[c]["y_s"][:NSEQ]
        k_sample[0, c * NSEQ:(c + 1) * NSEQ, 0] = R[c]["k_s"][:NSEQ].reshape(NSEQ, H, HD)
        v_sample[0, c * NSEQ:(c + 1) * NSEQ, 0] = R[c]["v_s"][:NSEQ].reshape(NSEQ, H, HD)
        pool_sample[0, c * NSEQ:(c + 1) * NSEQ] = R[c]["pool_s"]
    return (y_prompt, y_sample, k_prompt, v_prompt, pool_prompt, k_sample, v_sample, pool_sample)


def _silu_from_psum(S, ps_ap, ps_key, th, th_key, out_ap, out_key, extra_in1=None, extra_key=None):
    S.add("act", lambda e: e.activation(out=th, in_=ps_ap, func=AF.Tanh, scale=0.5), reads=[ps_key], writes=[th_key])
    S.add("dve", lambda e: e.tensor_scalar(out=th, in0=th, scalar1=0.5, scalar2=0.5, op0=ALU.mult, op1=ALU.add),
          reads=[th_key], writes=[th_key])
    if extra_in1 is None:
        S.add("dve", lambda e: e.tensor_tensor(out=out_ap, in0=ps_ap, in1=th, op=ALU.mult),
              reads=[ps_key, th_key], writes=[out_key])
    else:
        S.add("dve", lambda e: e.tensor_tensor(out=th, in0=ps_ap, in1=th, op=ALU.mult),
              reads=[ps_key, th_key], writes=[th_key])
        S.add("pool", lambda e: e.tensor_tensor(out=out_ap, in0=th, in1=extra_in1, op=ALU.mult),
              reads=[th_key, extra_key], writes=[out_key])


def phase_front(nc, S, sb, L):
    w_in, w_mix, pscale, w_po, w_ao = L["w_in"], L["w_mix"], L["pscale"], L["w_po"], L["w_ao"]
    xown, xs, xhalo, corr, spool = L["xown"], L["xs"], L["xhalo"], L["corr"], L["spool"]
    poolp, pool_s, mg_d = L["poolp"], L["pool_s"], L["mg_d"]
    PX, PB, ident, identb, att = L["PX"], L["PB"], L["ident"], L["identb"], L["att"]
    es_ = contextlib.ExitStack()
    L["esF"] = es_

    def wload(name, src_ap, shape):
        t = sb(es_, shape, BF16, name)
        S.add("pool", lambda e: e.dma_start(out=t[:], in_=src_ap), writes=[name], dma="w_" + name)
        return t
    wzb = wload("wzb", w_in[:, 1536:2048].rearrange("(c p) n -> p c n", p=128), [128, 8, 512])
    wu = wload("wu", w_in[:, 2048:2560].rearrange("(c p) n -> p c n", p=128), [128, 8, 512])
    wza = wload("wza", w_in[:, 2560:3072].rearrange("(c p) n -> p c n", p=128), [128, 8, 512])
    wga = wload("wga", w_in[:, 3072:4096].rearrange("(c p) n -> p c n", p=128), [128, 8, 1024])
    wgb = wload("wgb", w_in[:, 4096:5120].rearrange("(c p) n -> p c n", p=128), [128, 8, 1024])
    wmix = wload("wmix", w_mix.rearrange("g c e -> c g e"), [128, 4, 128])
    wpo = wload("wpo", w_po.rearrange("(c p) n -> p c n", p=128), [128, 4, 1024])
    wao = wload("wao", w_ao.rearrange("(c p) n -> p c n", p=128), [128, 4, 1024])
    psc = sb(es_, [128, 4], F32, "psc")
    S.add("sp", lambda e: e.dma_start(out=psc[:], in_=pscale.rearrange("(g c) -> c g", g=4), allow_slow_non_contiguous=True),
          writes=["psc"], dma="psc")
    corb = sb(es_, [128, 16 * 4 * 16], F32, "corb")
    S.add("sp", lambda e: e.dma_start(out=corb[:], in_=corr.rearrange("a g k -> (a g k)")[None, :].to_broadcast([128, 1024])),
          writes=["corb"], dma="corb")
    cor4 = corb[:].rearrange("p (a g k) -> p a g k", a=16, g=4)

    xt = sb(es_, [128, D], F32, "fxt")
    xh = sb(es_, [16, D], F32, "fxh")
    xTu = sb(es_, [128, 8, SBK], BF16, "xTu")
    xTh = sb(es_, [128, 8, 16], BF16, "xTh")
    uext = sb(es_, [128, 4, 16 + SBK], F32, "uext")
    tA = sb(es_, [128, 16 + SBK], F32, "tA")
    tB = sb(es_, [128, 16 + SBK], F32, "tB")
    dT = sb(es_, [128, 4, SBK], BF16, "dT")
    th = sb(es_, [128, SBK], F32, "th")
    szT = sb(es_, [128, 4, SBK], BF16, "szT")
    pzT = sb(es_, [128, 4, SBK], BF16, "pzT")
    azb = sb(es_, [128, AW], BF16, "azb")
    azT = sb(es_, [128, 4, SBK], BF16, "azT")
    tg1 = sb(es_, [128, SBK], F32, "tg1")
    tg2 = sb(es_, [128, SBK], F32, "tg2")
    t1 = sb(es_, [128, SBK], F32, "t1")
    t2 = sb(es_, [128, SBK], F32, "t2")
    mgT = sb(es_, [128, 8, SBK], BF16, "mgT")
    hsb = sb(es_, [16, 4, 512], F32, "hsb")
    hT = sb(es_, [128, 4, 4, 16], F32, "hT")
    ssum = sb(es_, [128, 4], F32, "ssum")
    pout = sb(es_, [16, 512], F32, "pout")

    for u in range(5):
        T = SBK if u < 4 else 128
        nt = T // 128
        for t in range(nt):
            src = xown[u * SBK + t * 128:u * SBK + (t + 1) * 128, :] if u < 4 else xs[:, :]
            S.add("sp", lambda e, src=src: e.dma_start(out=xt[:], in_=src), writes=["fxt"], dma="fxt")

            def tr(e):
                for c in range(8):
                    i = e.transpose(PX[:, c * 128:(c + 1) * 128], xt[:, c * 128:(c + 1) * 128], ident[:])
                return i
            S.add("pe", tr, reads=["fxt", "ident"], writes=["PX"])
            S.add("act", lambda e, t=t: e.activation(out=xTu[:, :, t * 128:(t + 1) * 128],
                                                     in_=PX[:].rearrange("p (c t) -> p c t", c=8), func=AF.Copy),
                  reads=["PX"], writes=["xTu"])
        if u < 4:
            S.add("sp", lambda e, u=u: e.dma_start(out=xh[:], in_=xhalo[u * 4, :, :]), writes=["fxh"], dma="fxh")

            def trh(e):
                for c in range(8):
                    i = e.transpose(PX[:, c * 16:(c + 1) * 16], xh[:, c * 128:(c + 1) * 128], ident[0:16, 0:16])
                return i
            S.add("pe", trh, reads=["fxh", "ident"], writes=["PX"])
            S.add("act", lambda e: e.activation(out=xTh[:].rearrange("p c k -> p (c k)"), in_=PX[:, 0:128], func=AF.Copy),
                  reads=["PX"], writes=["xTh"])
        for g in range(4):
            pb, pbk = PB[g % 2], "PB%d" % (g % 2)

            def mu(e, g=g, pb=pb, T=T, u=u):
                for c in range(8):
                    i = e.matmul(pb[:, 0:T], wu[:, c, g * 128:(g + 1) * 128], xTu[:, c, 0:T], start=(c == 0), stop=(c == 7))
                return i
            S.add("pe", mu, reads=["wu", "xTu"], writes=[pbk])
            S.add("act", lambda e, g=g, pb=pb, T=T: e.activation(out=uext[:, g, 16:16 + T], in_=pb[:, 0:T], func=AF.Copy),
                  reads=[pbk], writes=["uext%d" % g])
            if u < 4:
                def muh(e, g=g, pb=pb):
                    for c in range(8):
                        i = e.matmul(pb[:, 0:16], wu[:, c, g * 128:(g + 1) * 128], xTh[:, c, :], start=(c == 0), stop=(c == 7))
                    return i
                S.add("pe", muh, reads=["wu", "xTh"], writes=[pbk])
                S.add("act", lambda e, g=g, pb=pb: e.activation(out=uext[:, g, 0:16], in_=pb[:, 0:16], func=AF.Copy),
                      reads=[pbk], writes=["uext%d" % g])
        if u < 4:
            Ln = 16 + T
            for g in range(4):
                w = 2 ** (g + 1)
                cur = uext[:, g, :]
                ck = "uext%d" % g
                S.add("dve", lambda e, cur=cur: e.tensor_tensor(out=tA[:, 1:Ln], in0=cur[:, 1:Ln], in1=cur[:, 0:Ln - 1], op=ALU.add),
                      reads=[ck], writes=["tA"])
                fin, fk = tA, "tA"
                if g >= 1:
                    S.add("dve", lambda e: e.tensor_tensor(out=tB[:, 3:Ln], in0=tA[:, 3:Ln], in1=tA[:, 1:Ln - 2], op=ALU.add),
                          reads=["tA"], writes=["tB"])
                    fin, fk = tB, "tB"
                if g >= 2:
                    S.add("dve", lambda e: e.tensor_tensor(out=tA[:, 7:Ln], in0=tB[:, 7:Ln], in1=tB[:, 3:Ln - 4], op=ALU.add),
                          reads=["tB"], writes=["tA"])
                    fin, fk = tA, "tA"
                if g >= 3:
                    S.add("dve", lambda e: e.tensor_tensor(out=tB[:, 15:Ln], in0=tA[:, 15:Ln], in1=tA[:, 7:Ln - 8], op=ALU.add),
                          reads=["tA"], writes=["tB"])
                    fin, fk = tB, "tB"
                S.add("dve", lambda e, fin=fin, g=g, u=u: e.tensor_tensor(out=fin[:, 16:32], in0=fin[:, 16:32], in1=cor4[:, u * 4, g, :], op=ALU.mult),
                      reads=[fk, "corb"], writes=[fk])
                S.add("dve", lambda e, fin=fin, g=g, w=w, cur=cur, T=T: e.scalar_tensor_tensor(
                    out=dT[:, g, 0:T], in0=fin[:, 16:16 + T], scalar=1.0 / w, in1=cur[:, 16:16 + T], op0=ALU.mult, op1=ALU.subtract),
                    reads=[fk, ck], writes=["dT"])
            if u == 3:
                def trp(e):
                    for g in range(4):
                        i = e.transpose(PB[2][0:15, g * 128:(g + 1) * 128], uext[:, g, 16 + 497:16 + 512], ident[:])
                    return i
                S.add("pe", trp, reads=["uext0", "uext1", "uext2", "uext3", "ident"], writes=["PB2"])
                S.add("dve", lambda e: e.tensor_copy(out=pout[0:15, :], in_=PB[2][0:15, :]), reads=["PB2"], writes=["pout"])
                S.add("sp", lambda e: e.dma_start(out=poolp[:, :], in_=pout[0:15, :]), reads=["pout"], writes=["OUTpoolp"], dma="opoolp")
        else:
            for s in range(NSEQ):
                S.add("sp", lambda e, s=s: e.dma_start(out=hsb[0:15, s, :], in_=spool[s, :, :]), writes=["hsb"], dma="hsb")
                S.add("sp", lambda e, s=s: e.dma_start(out=pool_s[s, 0:14, :], in_=spool[s, 1:15, :]), writes=["OUTps%d" % s], dma="ops%d" % s)
            for s in range(NSEQ):
                def trs(e, s=s):
                    for g in range(4):
                        i = e.transpose(PB[2][:, (s * 4 + g) * 16:(s * 4 + g) * 16 + 15], hsb[0:15, s, g * 128:(g + 1) * 128], ident[0:15, 0:15])
                    return i
                S.add("pe", trs, reads=["hsb", "ident"], writes=["PB2"])
            S.add("dve", lambda e: e.memset(hT[:], 0.0), writes=["hT"])
            S.add("dve", lambda e: e.tensor_copy(out=hT[:, :, :, 0:15], in_=PB[2][:, 0:256].rearrange("p (s g r) -> p s g r", s=4, g=4)[:, :, :, 0:15]),
                  reads=["PB2", "hT"], writes=["hT"])
            S.add("pool", lambda e: e.memset(dT[:], 0.0), writes=["dT"])
            for g in range(4):
                w = 2 ** (g + 1)
                S.add("dve", lambda e, g=g, w=w: e.tensor_reduce(out=ssum[:], in_=hT[:, :, g, 16 - w:15], axis=AX.X, op=ALU.add),
                      reads=["hT"], writes=["ssum"])
                S.add("dve", lambda e, g=g: e.tensor_tensor(out=ssum[:], in0=ssum[:], in1=uext[:, g, 16:16 + NSEQ], op=ALU.add),
                      reads=["ssum", "uext%d" % g], writes=["ssum"])
                S.add("dve", lambda e, g=g, w=w: e.scalar_tensor_tensor(
                    out=dT[:, g, 0:NSEQ], in0=ssum[:], scalar=1.0 / w, in1=uext[:, g, 16:16 + NSEQ], op0=ALU.mult, op1=ALU.subtract),
                    reads=["ssum", "uext%d" % g, "dT"], writes=["dT"])

            def tru(e):
                for g in range(4):
                    i = e.transpose(PB[3][0:NSEQ, g * 128:(g + 1) * 128], uext[:, g, 16:16 + NSEQ], ident[:])
                return i
            S.add("pe", tru, reads=["uext0", "uext1", "uext2", "uext3", "ident"], writes=["PB3"])
            S.add("dve", lambda e: e.tensor_copy(out=pout[0:NSEQ, :], in_=PB[3][0:NSEQ, :]), reads=["PB3"], writes=["pout"])
            S.add("sp", lambda e: e.dma_start(out=pool_s[:, 14, :], in_=pout[0:NSEQ, :]), reads=["pout"], writes=["OUTpsu"], dma="opsu")
        for g in range(4):
            pb, pbk = PB[g % 2], "PB%d" % (g % 2)

            def mz(e, g=g, pb=pb, T=T):
                for c in range(8):
                    i = e.matmul(pb[:, 0:T], wza[:, c, g * 128:(g + 1) * 128], xTu[:, c, 0:T], start=(c == 0), stop=(c == 7))
                return i
            S.add("pe", mz, reads=["wza", "xTu"], writes=[pbk])
            _silu_from_psum(S, pb[:, 0:T], pbk, th[:, 0:T], "th", szT[:, g, 0:T], "szT")
            S.add("pe", lambda e, g=g, pb=pb, T=T: e.matmul(pb[:, 0:T], wmix[:, g, :], dT[:, g, 0:T], start=True, stop=True),
                  reads=["wmix", "dT"], writes=[pbk])
            S.add("dve", lambda e, g=g, pb=pb, T=T: e.scalar_tensor_tensor(
                out=pzT[:, g, 0:T], in0=pb[:, 0:T], scalar=psc[:, g:g + 1], in1=szT[:, g, 0:T], op0=ALU.mult, op1=ALU.mult),
                reads=[pbk, "psc", "szT"], writes=["pzT"])
        for t in range(nt):
            def mzb(e, t=t):
                for c in range(8):
                    i = e.matmul(PB[0][:], xTu[:, c, t * 128:(t + 1) * 128], wzb[:, c, :], start=(c == 0), stop=(c == 7))
                return i
            S.add("pe", mzb, reads=["xTu", "wzb"], writes=["PB0"])
            atile = att[:, u * 4 + t, :]
            _silu_from_psum(S, PB[0][:], "PB0", th[:], "th", azb[:], "azb", extra_in1=atile, extra_key="att")
            ptb = PB[1][:].bitcast(BF16)

            def tra(e):
                for cc in range(4):
                    i = e.transpose(ptb[:, cc * 128:(cc + 1) * 128], azb[:, cc * 128:(cc + 1) * 128], identb[:])
                return i
            S.add("pe", tra, reads=["azb", "identb"], writes=["PB1"])
            S.add("act", lambda e, t=t, ptb=ptb: e.activation(out=azT[:, :, t * 128:(t + 1) * 128],
                                                              in_=ptb[:, 0:512].rearrange("p (c t) -> p c t", c=4), func=AF.Copy),
                  reads=["PB1"], writes=["azT"])
        for m in range(8):
            def myp(e, m=m, T=T):
                for cc in range(4):
                    i = e.matmul(PB[2][:, 0:T], wpo[:, cc, m * 128:(m + 1) * 128], pzT[:, cc, 0:T], start=(cc == 0), stop=(cc == 3))
                return i

            def mya(e, m=m, T=T):
                for cc in range(4):
                    i = e.matmul(PB[3][:, 0:T], wao[:, cc, m * 128:(m + 1) * 128], azT[:, cc, 0:T], start=(cc == 0), stop=(cc == 3))
                return i

            def mga(e, m=m, T=T):
                for c in range(8):
                    i = e.matmul(PB[4][:, 0:T], wga[:, c, m * 128:(m + 1) * 128], xTu[:, c, 0:T], start=(c == 0), stop=(c == 7))
                return i

            def mgb(e, m=m, T=T):
                for c in range(8):
                    i = e.matmul(PB[5][:, 0:T], wgb[:, c, m * 128:(m + 1) * 128], xTu[:, c, 0:T], start=(c == 0), stop=(c == 7))
                return i
            S.add("pe", myp, reads=["wpo", "pzT"], writes=["PB2"])
            S.add("pe", mya, reads=["wao", "azT"], writes=["PB3"])
            S.add("pe", mga, reads=["wga", "xTu"], writes=["PB4"])
            S.add("pe", mgb, reads=["wgb", "xTu"], writes=["PB5"])
            S.add("act", lambda e, T=T: e.activation(out=tg1[:, 0:T], in_=PB[4][:, 0:T], func=AF.Tanh, scale=0.5), reads=["PB4"], writes=["tg1"])
            S.add("act", lambda e, T=T: e.activation(out=tg2[:, 0:T], in_=PB[5][:, 0:T], func=AF.Tanh, scale=0.5), reads=["PB5"], writes=["tg2"])
            S.add("pool", lambda e, T=T: e.tensor_scalar(out=tg1[:, 0:T], in0=tg1[:, 0:T], scalar1=0.5, scalar2=0.5, op0=ALU.mult, op1=ALU.add),
                  reads=["tg1"], writes=["tg1"])
            S.add("pool", lambda e, T=T: e.tensor_scalar(out=tg2[:, 0:T], in0=tg2[:, 0:T], scalar1=0.5, scalar2=0.5, op0=ALU.mult, op1=ALU.add),
                  reads=["tg2"], writes=["tg2"])
            S.add("dve", lambda e, T=T: e.tensor_tensor(out=t1[:, 0:T], in0=PB[2][:, 0:T], in1=tg1[:, 0:T], op=ALU.mult),
                  reads=["PB2", "tg1"], writes=["t1"])
            S.add("dve", lambda e, T=T: e.tensor_tensor(out=t2[:, 0:T], in0=PB[3][:, 0:T], in1=tg2[:, 0:T], op=ALU.mult),
                  reads=["PB3", "tg2"], writes=["t2"])
            S.add("pool", lambda e, m=m, T=T: e.tensor_tensor(out=mgT[:, m, 0:T], in0=t1[:, 0:T], in1=t2[:, 0:T], op=ALU.add),
                  reads=["t1", "t2"], writes=["mgT"])
        S.add("sp", lambda e, u=u, T=T: e.dma_start(out=mg_d[u, :, :, 0:T], in_=mgT[:, :, 0:T]), reads=["mgT"], writes=["mgd%d" % u], dma="mgst")
    S.barrier()
    es_.close()


def phase_tail(nc, S, sb, L):
    w_o, w_pg, w_ple, ln_g, ln_b = L["w_o"], L["w_pg"], L["w_ple"], L["ln_g"], L["ln_b"]
    xown, xs, pown, pss, y_own, y_s, mg_d = L["xown"], L["xs"], L["pown"], L["pss"], L["y_own"], L["y_s"], L["mg_d"]
    PX, PB, ident, identb, neghalf = L["PX"], L["PB"], L["ident"], L["identb"], L["neghalf"]
    es_ = contextlib.ExitStack()

    def wload(name, src_ap, shape):
        t = sb(es_, shape, BF16, name)
        S.add("pool", lambda e: e.dma_start(out=t[:], in_=src_ap), writes=[name], dma="w_" + name)
        return t
    wo = wload("wo", w_o.rearrange("(c p) n -> p c n", p=128), [128, 8, D])
    wpg = wload("wpg", w_pg.rearrange("(c p) n -> p c n", p=128), [128, 8, D])
    wple = wload("wple", w_ple.rearrange("(c p) n -> p c n", p=128), [128, 2, D])
    lg = sb(es_, [128, D], F32, "lg")
    lb = sb(es_, [128, D], F32, "lb")
    S.add("sp", lambda e: e.dma_start(out=lg[:], in_=ln_g[None, :].to_broadcast([128, D])), writes=["lg"], dma="lg")
    S.add("sp", lambda e: e.dma_start(out=lb[:], in_=ln_b[None, :].to_broadcast([128, D])), writes=["lb"], dma="lb")
    mgT = [sb(es_, [128, 8, SBK], BF16, "mgTt") for _ in range(2)]
    xt = [sb(es_, [128, D], F32, "txt") for _ in range(2)]
    pt = [sb(es_, [128, PLE], F32, "tpt") for _ in range(2)]
    hp_ = [sb(es_, [128, D], F32, "hp") for _ in range(2)]
    hh_ = [sb(es_, [128, D], F32, "hh") for _ in range(2)]
    hb_ = [sb(es_, [128, D], BF16, "hb") for _ in range(2)]
    pendB = []
    hT = sb(es_, [128, 8, 128], BF16, "hTt")
    pT = sb(es_, [128, 2, 128], BF16, "pTt")
    sg = sb(es_, [128, D], F32, "sg")
    yt = [sb(es_, [128, D], F32, "yt") for _ in range(2)]
    st = sb(es_, [128, 2, 6], F32, "st")
    mv = sb(es_, [128, 2], F32, "mv")
    ve = sb(es_, [128, 1], F32, "ve")
    rstd = sb(es_, [128, 1], F32, "rstd")
    nmr = sb(es_, [128, 1], F32, "nmr")
    gi = 0
    for u in range(5):
        T = SBK if u < 4 else 128
        mj = u % 2
        S.add("sp", lambda e, u=u, mj=mj, T=T: e.dma_start(out=mgT[mj][:, :, 0:T], in_=mg_d[u, :, :, 0:T]),
              reads=["mgd%d" % u], writes=["mgTt%d" % mj], dma="mgld%d" % mj)
        for t in range(T // 128):
            j = gi % 2
            gi += 1
            hp, hh, hb = hp_[j], hh_[j], hb_[j]
            HP, HH, HB = "hp%d" % j, "hh%d" % j, "hb%d" % j
            xsrc = xown[u * SBK + t * 128:u * SBK + (t + 1) * 128, :] if u < 4 else xs[:, :]
            psrc = pown[u * SBK + t * 128:u * SBK + (t + 1) * 128, :] if u < 4 else pss[:, :]
            ydst = y_own[u * SBK + t * 128:u * SBK + (t + 1) * 128, :] if u < 4 else y_s[:, :]
            S.add("sp", lambda e, j=j, xsrc=xsrc: e.dma_start(out=xt[j][:], in_=xsrc), writes=["txt%d" % j], dma="txt%d" % j)
            S.add("sp", lambda e, j=j, psrc=psrc: e.dma_start(out=pt[j][:], in_=psrc), writes=["tpt%d" % j], dma="tpt%d" % j)

            def mho(e, t=t, mj=mj):
                for half in range(2):
                    for c in range(8):
                        i = e.matmul(PX[:, half * 512:(half + 1) * 512], mgT[mj][:, c, t * 128:(t + 1) * 128],
                                     wo[:, c, half * 512:(half + 1) * 512], start=(c == 0), stop=(c == 7))
                return i
            S.add("pe", mho, reads=["mgTt%d" % mj, "wo"], writes=["PX"])
            S.add("dve", lambda e, j=j, hp=hp: e.scalar_tensor_tensor(out=hp[:], in0=xt[j][:], scalar=ALPHA, in1=PX[:], op0=ALU.mult, op1=ALU.add),
                  reads=["txt%d" % j, "PX"], writes=[HP])

            def bst(e, hp=hp):
                for half in range(2):
                    i = e.bn_stats(out=st[:, half, :], in_=hp[:, half * 512:(half + 1) * 512])
                return i
            S.add("dve", bst, reads=[HP], writes=["st"])
            S.add("dve", lambda e: e.bn_aggr(out=mv[:], in_=st[:].rearrange("p a b -> p (a b)")), reads=["st"], writes=["mv"])
            S.add("dve", lambda e: e.tensor_scalar(out=ve[:], in0=mv[:, 1:2], scalar1=EPS, scalar2=None, op0=ALU.add), reads=["mv"], writes=["ve"])
            S.add("pool", lambda e: e.tensor_tensor(out=rstd[:], in0=ve[:], in1=neghalf[:], op=ALU.pow), reads=["ve", "neghalf"], writes=["rstd"])
            S.add("dve", lambda e: e.tensor_scalar(out=nmr[:], in0=mv[:, 0:1], scalar1=rstd[:, 0:1], scalar2=-1.0, op0=ALU.mult, op1=ALU.mult),
                  reads=["mv", "rstd"], writes=["nmr"])
            S.add("act", lambda e, hh=hh, hp=hp: e.activation(out=hh[:], in_=hp[:], func=AF.Identity, scale=rstd[:, 0:1], bias=nmr[:, 0:1]),
                  reads=[HP, "rstd", "nmr"], writes=[HH])
            S.add("dve", lambda e, hh=hh: e.tensor_tensor(out=hh[:], in0=hh[:], in1=lg[:], op=ALU.mult), reads=[HH, "lg"], writes=[HH])
            S.add("pool", lambda e, hh=hh: e.tensor_tensor(out=hh[:], in0=hh[:], in1=lb[:], op=ALU.add), reads=[HH, "lb"], writes=[HH])
            S.add("pool", lambda e, hh=hh, hb=hb: e.tensor_copy(out=hb[:], in_=hh[:]), reads=[HH], writes=[HB])
            def stageB(j=j, hh=hh, hb=hb, HH=HH, HB=HB, ydst=ydst):
                ptb = PB[0][:].bitcast(BF16)

                def trh(e):
                    for c in range(8):
                        i = e.transpose(ptb[:, c * 128:(c + 1) * 128], hb[:, c * 128:(c + 1) * 128], identb[:])
                    return i
                S.add("pe", trh, reads=[HB, "identb"], writes=["PB0"])
                S.add("act", lambda e, ptb=ptb: e.activation(out=hT[:].rearrange("p c t -> p (c t)"), in_=ptb[:, 0:1024], func=AF.Copy),
                      reads=["PB0"], writes=["hTt"])

                def trp(e, j=j):
                    for c in range(2):
                        i = e.transpose(PB[3][:, c * 128:(c + 1) * 128], pt[j][:, c * 128:(c + 1) * 128], ident[:])
                    return i
                S.add("pe", trp, reads=["tpt%d" % j, "ident"], writes=["PB3"])
                S.add("act", lambda e: e.activation(out=pT[:].rearrange("p c t -> p (c t)"), in_=PB[3][:, 0:256], func=AF.Copy),
                      reads=["PB3"], writes=["pTt"])
                for half in range(2):
                    def mg_(e, half=half):
                        for c in range(8):
                            i = e.matmul(PB[1 + half][:], hT[:, c, :], wpg[:, c, half * 512:(half + 1) * 512], start=(c == 0), stop=(c == 7))
                        return i
                    S.add("pe", mg_, reads=["hTt", "wpg"], writes=["PB%d" % (1 + half)])

                    def mp_(e, half=half):
                        for c in range(2):
                            i = e.matmul(PB[4 + half][:], pT[:, c, :], wple[:, c, half * 512:(half + 1) * 512], start=(c == 0), stop=(c == 1))
                        return i
                    S.add("pe", mp_, reads=["pTt", "wple"], writes=["PB%d" % (4 + half)])
                    hs = slice(half * 512, (half + 1) * 512)
                    S.add("act", lambda e, half=half, hs=hs: e.activation(out=sg[:, hs], in_=PB[1 + half][:], func=AF.Tanh, scale=0.5),
                          reads=["PB%d" % (1 + half)], writes=["sg%d" % half])
                    S.add("pool", lambda e, hs=hs: e.tensor_scalar(out=sg[:, hs], in0=sg[:, hs], scalar1=0.5, scalar2=0.5, op0=ALU.mult, op1=ALU.add),
                          reads=["sg%d" % half], writes=["sg%d" % half])
                    S.add("dve", lambda e, half=half, hs=hs: e.tensor_tensor(out=sg[:, hs], in0=PB[4 + half][:], in1=sg[:, hs], op=ALU.mult),
                          reads=["PB%d" % (4 + half), "sg%d" % half], writes=["sg%d" % half])
                S.add("pool", lambda e, j=j: e.tensor_tensor(out=yt[j][:], in0=sg[:], in1=hh[:], op=ALU.add),
                      reads=["sg0", "sg1", HH], writes=["yt%d" % j])
                S.add("sp", lambda e, j=j, ydst=ydst: e.dma_start(out=ydst, in_=yt[j][:]), reads=["yt%d" % j], writes=["OUTy%d" % j], dma="oy%d" % j)
            if pendB:
                pendB.pop()()
            pendB.append(stageB)
    pendB.pop()()
    S.barrier()
    es_.close()
```

```python
import contextlib
import numpy as np
import concourse.bass as bass
import concourse.mybir as mybir
from concourse.bass_utils import run_bass_kernel_spmd

F32 = mybir.dt.float32
BF16 = mybir.dt.bfloat16
I32 = mybir.dt.int32
U32 = mybir.dt.uint32
AF = mybir.ActivationFunctionType
ALU = mybir.AluOpType
AX = mybir.AxisListType

NCORES = 8
D = 1024
SEQ = 8192
H = 8
HD = 64
AW = 512
PLE = 256
PAST = 16384
NPAGE = 128
NPHYS = 5120
DEV_SKIP_DECODE = False
DEBUG = False
_last = None
SBK = 512
KMAX = [3, 7, 11, 15]
BIG = 30000.0
ALPHA = 2.0 ** 0.25
EPS = 1e-5
NSEQ = 4
QTOT = 2048 + 128


def own_sbs(r):
    return [r, 7 - r, 8 + r, 15 - r]


class Op:
    __slots__ = ("eng", "fn", "dma", "deps", "needed", "event")

    def __init__(self, eng, fn, dma):
        self.eng = eng
        self.fn = fn
        self.dma = dma
        self.deps = set()
        self.needed = False
        self.event = None


class Sched:
    ENG = ("pe", "act", "dve", "pool", "sp")

    def __init__(self, nc, es):
        self.nc = nc
        self.es = es
        self.h = {"pe": nc.tensor, "act": nc.scalar, "dve": nc.vector, "pool": nc.gpsimd, "sp": nc.sync}
        self.ops = {e: [] for e in self.ENG}
        self.res = {}
        self.bar = set()
        self.dma_since = []

    def add(self, eng, fn, reads=(), writes=(), dma=None):
        op = Op(eng, fn, dma)
        deps = set(self.bar)
        for r in reads:
            st = self.res.get(r)
            if st is not None and st[0] is not None:
                deps.add(st[0])
        for w in writes:
            st = self.res.get(w)
            if st is not None:
                if st[0] is not None:
                    deps.add(st[0])
                deps.update(st[1])
        if eng == "pe" and dma is None:
            deps = {d for d in deps if not (d.eng == "pe" and d.dma is None)}
        op.deps = deps
        for d in deps:
            d.needed = True
        for r in reads:
            self.res.setdefault(r, [None, []])[1].append(op)
        for w in writes:
            self.res[w] = [op, []]
        self.ops[eng].append(op)
        if dma is not None:
            self.dma_since.append(op)
        return op

    def barrier(self):
        b = set()
        for e in self.ENG:
            for op in reversed(self.ops[e]):
                if op.dma is None:
                    b.add(op)
                    break
        b.update(self.dma_since)
        self.dma_since = []
        for d in b:
            d.needed = True
        self.bar = b

    def emit(self, block):
        nc = self.nc
        esem = {e: self.es.enter_context(nc.semaphore("sem_" + e)) for e in self.ENG}
        dsem = {}
        dcnt = {}
        for e in self.ENG:
            cnt = 0
            for op in self.ops[e]:
                if op.dma is not None:
                    if op.dma not in dsem:
                        dsem[op.dma] = self.es.enter_context(nc.semaphore("dq_%d" % len(dsem)))
                        dcnt[op.dma] = 0
                    dcnt[op.dma] += 16
                    op.event = (dsem[op.dma], dcnt[op.dma], 16)
                elif op.needed:
                    cnt += 1
                    op.event = (esem[e], cnt, 1)

        def run(e, handle):
            waited = {}
            for op in self.ops[e]:
                need = {}
                for d in op.deps:
                    sem, val, _ = d.event
                    k = id(sem)
                    if k not in need or need[k][1] < val:
                        need[k] = (sem, val)
                for k, (sem, val) in need.items():
                    if waited.get(k, 0) < val:
                        handle.wait_ge(sem, val)
                        waited[k] = val
                ins = op.fn(handle)
                if op.event is not None:
                    ins.then_inc(op.event[0], op.event[2])

        @block.sync
        def _(eng):
            run("sp", eng)

        @block.gpsimd
        def _(eng):
            run("pool", eng)

        @block.tensor
        def _(eng):
            run("pe", eng)

        @block.vector
        def _(eng):
            run("dve", eng)

        @block.scalar
        def _(eng):
            run("act", eng)


def build_nc():
    nc = bass.Bass("TRN2", target_bir_lowering=False)

    def din(name, shape, dt=F32):
        return nc.dram_tensor(name, list(shape), dt, kind="ExternalInput").ap()

    def dout(name, shape, dt=F32):
        return nc.dram_tensor(name, list(shape), dt, kind="ExternalOutput").ap()

    xall = din("xall", [SEQ, D])
    xown = din("xown", [2048, D])
    xhalo = din("xhalo", [16, 16, D])
    pown = din("pown", [2048, PLE])
    xs = din("xs", [128, D])
    pss = din("pss", [128, PLE])
    cs_all = din("cs_all", [SEQ, 64])
    cs_own = din("cs_own", [2048, 64])
    cs_s = din("cs_s", [128, 64])
    pastneg = din("pastneg", [2048, 32])
    notown = din("notown", [2048, 32])
    ownA = din("ownA", [2048, 32])
    isAB = din("isAB", [2048, 2])
    oh_all = din("oh_all", [16, 32, SBK])
    oh_own = din("oh_own", [32, SBK])
    tri = din("tri", [128, 4 * SBK])
    corr = din("corr", [16, 4, 16])
    ident_d = din("ident", [128, 128])
    w_in = din("w_in", [D, 5120])
    w_mix = din("w_mix", [4, 128, 128])
    pscale = din("pscale", [512])
    w_po = din("w_po", [512, D])
    w_ao = din("w_ao", [512, D])
    w_o = din("w_o", [D, D])
    ln_g = din("ln_g", [D])
    ln_b = din("ln_b", [D])
    w_ple = din("w_ple", [PLE, D])
    w_pg = din("w_pg", [D, D])
    cache_k = din("cache_k", [NPHYS * 128 * 8, 64])
    cache_v = din("cache_v", [NPHYS * 128 * 8, 64])
    ptab = din("ptab", [NSEQ, NPAGE], I32)
    spool = din("spool", [NSEQ, 15, 512])
    pairm = din("pairm", [128, 64])
    iota128 = din("iota128", [8, 128])
    hsel = din("hsel", [8, 48])
    pofs = din("pofs", [128, 48])

    y_own = dout("y_own", [2048, D])
    k_own = dout("k_own", [2048, AW])
    v_own = dout("v_own", [2048, AW])
    poolp = dout("poolp", [15, 512])
    y_s = dout("y_s", [128, D])
    k_s = dout("k_s", [128, AW])
    v_s = dout("v_s", [128, AW])
    pool_s = dout("pool_s", [NSEQ, 15, 512])

    if DEBUG:
        dbg_att = dout("dbg_att", [128, 17, AW])
        dbg_mg = dout("dbg_mg", [5, 128, 8, SBK])
        dbg_qt = dout("dbg_qt", [96, H, QTOT])
    KT_d = nc.dram_tensor("KT_d", [H, 96, 20 * SBK], BF16, kind="Internal").ap()
    V_d = nc.dram_tensor("V_d", [H, 20, 128, 4, 66], BF16, kind="Internal").ap()
    mg_d = nc.dram_tensor("mg_d", [5, 128, 8, SBK], BF16, kind="Internal").ap()

    with contextlib.ExitStack() as es:
        S = Sched(nc, es)
        uid = [0]

        def sb(es_, shape, dt=F32, name=None):
            uid[0] += 1
            return es_.enter_context(nc.sbuf_tensor("%s_%d" % (name or "t", uid[0]), list(shape), dt))

        PX = es.enter_context(nc.psum_tensor("PX", [128, 1024], F32))
        PB = [es.enter_context(nc.psum_tensor("PB%d" % i, [128, 512], F32)) for i in range(6)]

        ident = sb(es, [128, 128], F32, "ident")
        identb = sb(es, [128, 128], BF16, "identb")
        att = sb(es, [128, 17, AW], BF16, "att")
        kmT = sb(es, [64, H, 32], F32, "kmT")
        kmTb = sb(es, [64, H, 32], BF16, "kmTb")
        kmx = sb(es, [128, H], F32, "kmx")
        kmxb = sb(es, [128, H], F32, "kmxb")
        trib = sb(es, [128, 4, SBK], BF16, "trib")
        ones_f = sb(es, [128, 128], F32, "ones_f")
        neghalf = sb(es, [128, 1], F32, "neghalf")
        qs_f = sb(es, [128, AW], F32, "qs_f")
        ks_f = sb(es, [128, AW], F32, "ks_f")
        vs_f = sb(es, [128, AW], F32, "vs_f")
        esQ = contextlib.ExitStack()
        QT = sb(esQ, [96, H, QTOT], BF16, "QT")

        S.add("sp", lambda e: e.dma_start(out=ident[:], in_=ident_d[:, :]), writes=["ident"], dma="c0")
        S.add("pool", lambda e: e.dma_start(out=identb[:], in_=ident_d[:, :]), writes=["identb"], dma="c1")
        S.add("pool", lambda e: e.dma_start(out=trib[:].rearrange("p a b -> p (a b)"), in_=tri[:, :]), writes=["trib"], dma="c2")
        S.add("dve", lambda e: e.memset(kmx[:], 0.0), writes=["kmx"])
        S.add("dve", lambda e: e.memset(ones_f[:], 1.0), writes=["ones_f"])
        S.add("dve", lambda e: e.memset(neghalf[:], -0.5), writes=["neghalf"])

        esA = contextlib.ExitStack()
        wq = sb(esA, [128, 8, 512], BF16, "wq")
        wk = sb(esA, [128, 8, 512], BF16, "wk")
        wv = sb(esA, [128, 8, 512], BF16, "wv")
        for nm, wt, c0 in (("wq", wq, 0), ("wk", wk, 512), ("wv", wv, 1024)):
            S.add("pool", lambda e, wt=wt, c0=c0: e.dma_start(
                out=wt[:], in_=w_in[:, c0:c0 + 512].rearrange("(c p) n -> p c n", p=128)),
                writes=[nm], dma="w" + nm)
        xt = [sb(esA, [128, D], F32, "xt") for _ in range(2)]
        cst = [sb(esA, [128, 64], F32, "cst") for _ in range(2)]
        xT = [sb(esA, [128, 8, 128], BF16, "xT") for _ in range(2)]
        r1 = sb(esA, [128, 256], F32, "r1")
        r2 = sb(esA, [128, 256], F32, "r2")
        r3 = sb(esA, [128, 256], F32, "r3")
        r4 = sb(esA, [128, 256], F32, "r4")
        kr = [sb(esA, [128, AW], F32, "kr") for _ in range(2)]
        vf = [sb(esA, [128, AW], F32, "vf") for _ in range(2)]
        qr = sb(esA, [128, AW], F32, "qr")
        kb = [sb(esA, [128, AW], BF16, "kb") for _ in range(2)]
        qb = sb(esA, [128, AW], BF16, "qb")
        sq = sb(esA, [128, AW], F32, "sq")
        ksq = sb(esA, [128, H], F32, "ksq")
        qsq = sb(esA, [128, H], F32, "qsq")
        KTs = [sb(esA, [96, H, SBK], BF16, "KTs") for _ in range(2)]
        Vs = [sb(esA, [128, H, 4, 66], BF16, "Vs") for _ in range(2)]
        for j in range(2):
            S.add("pool", lambda e, j=j: e.memset(Vs[j][:], 1.0), writes=["Vs%d" % j])
        sel_c = sb(esA, [128, 3, 32], F32, "sel_c")
        isab = sb(esA, [128, 2], F32, "isab")
        sc = sb(esA, [128, H, 32], F32, "sc")
        sel = sb(esA, [128, H, 32], F32, "sel")
        tmp3 = sb(esA, [128, H, 32], F32, "tmp3")
        mx8 = sb(esA, [128, H, 8], F32, "mx8")
        selA = sb(esA, [128, H], F32, "selA")
        tq = sb(esA, [128, H], F32, "tq")
        mbb = sb(esA, [128, H, 32], BF16, "mbb")

        PSQ, PST, PSS, PSM = ("PB4", "PB5", "PB4", "PB5")
        pq, pt_, ps_, pm_ = PB[4], PB[5], PB[4], PB[5]

        def rope(src_ps, src_key, cs_t, cs_key, dst, dst_key):
            s3 = src_ps[:].rearrange("p (h d) -> p h d", h=H)
            d3 = dst[:].rearrange("p (h d) -> p h d", h=H)
            cosb = cs_t[:, None, 0:32].to_broadcast([128, H, 32])
            sinb = cs_t[:, None, 32:64].to_broadcast([128, H, 32])
            a3 = r1[:].rearrange("p (h d) -> p h d", h=H)
            b3 = r2[:].rearrange("p (h d) -> p h d", h=H)
            c3 = r3[:].rearrange("p (h d) -> p h d", h=H)
            e3 = r4[:].rearrange("p (h d) -> p h d", h=H)
            S.add("dve", lambda e: e.tensor_tensor(out=a3, in0=s3[:, :, 0:32], in1=cosb, op=ALU.mult),
                  reads=[src_key, cs_key], writes=["r1"])
            S.add("dve", lambda e: e.tensor_tensor(out=b3, in0=s3[:, :, 32:64], in1=sinb, op=ALU.mult),
                  reads=[src_key, cs_key], writes=["r2"])
            S.add("dve", lambda e: e.tensor_tensor(out=c3, in0=s3[:, :, 32:64], in1=cosb, op=ALU.mult),
                  reads=[src_key, cs_key], writes=["r3"])
            S.add("dve", lambda e: e.tensor_tensor(out=e3, in0=s3[:, :, 0:32], in1=sinb, op=ALU.mult),
                  reads=[src_key, cs_key], writes=["r4"])
            S.add("pool", lambda e: e.tensor_tensor(out=d3[:, :, 0:32], in0=a3, in1=b3, op=ALU.subtract),
                  reads=["r1", "r2"], writes=[dst_key + "a"])
            S.add("pool", lambda e: e.tensor_tensor(out=d3[:, :, 32:64], in0=c3, in1=e3, op=ALU.add),
                  reads=["r3", "r4"], writes=[dst_key + "b"])

        gt = [0]
        late2 = [None]

        def load_tile(xsrc, cssrc):
            g = gt[0] % 2
            S.add("sp", lambda e: e.dma_start(out=xt[g][:], in_=xsrc), writes=["xt%d" % g], dma="xt%d" % g)
            S.add("sp", lambda e: e.dma_start(out=cst[g][:], in_=cssrc), writes=["cst%d" % g], dma="cst%d" % g)

        def proc_tile(t, kts_j, mode, qcol=None, orow=None, okv=None, after3=None):
            g = gt[0] % 2
            gt[0] += 1
            xk, ck, xTk = "xt%d" % g, "cst%d" % g, "xT%d" % g
            pk, pv = PB[g], PB[2 + g]
            PSK, PSV = "PB%d" % g, "PB%d" % (2 + g)

            def tr(e):
                for c in range(8):
                    i = e.transpose(PX[:, c * 128:(c + 1) * 128], xt[g][:, c * 128:(c + 1) * 128], ident[:])
                return i
            S.add("pe", tr, reads=[xk, "ident"], writes=["PX"])
            S.add("act", lambda e: e.activation(out=xT[g][:].rearrange("p c t -> p (c t)"), in_=PX[:], func=AF.Copy),
                  reads=["PX"], writes=[xTk])

            def proj(pbank, wt):
                def f(e):
                    for c in range(8):
                        i = e.matmul(pbank[:], xT[g][:, c, :], wt[:, c, :], start=(c == 0), stop=(c == 7))
                    return i
                return f
            S.add("pe", proj(pk, wk), reads=[xTk, "wk"], writes=[PSK])
            S.add("pe", proj(pv, wv), reads=[xTk, "wv"], writes=[PSV])
            if mode != "gen":
                S.add("pe", proj(pq, wq), reads=[xTk, "wq"], writes=[PSQ])
            kg = g
            krk = "kr%d" % kg

            def s2():
                rope(pk, PSK, cst[g], ck, kr[kg], krk)
                if mode != "smp":
                    vsl = Vs[kts_j][:, :, t, 0:64]
                    S.add("act", lambda e: e.activation(out=vsl, in_=pv[:].rearrange("p (h d) -> p h d", h=H), func=AF.Copy),
                          reads=[PSV], writes=["Vs%d" % kts_j])
                if mode != "gen":
                    S.add("act", lambda e: e.activation(out=vf[kg][:], in_=pv[:], func=AF.Copy),
                          reads=[PSV], writes=["vf%d" % kg])
                    S.add("sp", lambda e: e.dma_start(out=okv[0], in_=kr[kg][:]), reads=[krk + "a", krk + "b"], writes=["OUTk%d" % kg], dma="ok%d" % kg)
                    S.add("sp", lambda e: e.dma_start(out=okv[1], in_=vf[kg][:]), reads=["vf%d" % kg], writes=["OUTv%d" % kg], dma="ov%d" % kg)
                if mode == "smp":
                    return
                S.add("act", lambda e: e.activation(out=kb[kg][:], in_=kr[kg][:], func=AF.Copy), reads=[krk + "a", krk + "b"], writes=["kb%d" % kg])
                if late2[0] is not None:
                    late2[0]()
                    late2[0] = None
                S.add("pool", lambda e: e.tensor_tensor(out=sq[:], in0=kr[kg][:], in1=kr[kg][:], op=ALU.mult),
                      reads=[krk + "a", krk + "b"], writes=["sq"])

                def late():
                    S.add("dve", lambda e: e.tensor_reduce(out=ksq[:], in_=sq[:].rearrange("p (h d) -> p h d", h=H), axis=AX.X, op=ALU.add),
                          reads=["sq"], writes=["ksq"])
                    S.add("dve", lambda e: e.tensor_tensor(out=kmx[:], in0=kmx[:], in1=ksq[:], op=ALU.max),
                          reads=["ksq", "kmx"], writes=["kmx"])
                if mode == "gen":
                    late2[0] = late
                else:
                    late()

            def s3():
                if mode == "smp":
                    return
                ptb = pt_[:].bitcast(BF16)

                def trk(e):
                    for h in range(H):
                        i = e.transpose(ptb[0:64, h * 128:(h + 1) * 128], kb[kg][:, h * 64:(h + 1) * 64], identb[:])
                    return i
                S.add("pe", trk, reads=["kb%d" % kg, "identb"], writes=[PST])
                S.add("act", lambda e: e.activation(out=KTs[kts_j][0:64, :, t * 128:(t + 1) * 128],
                                                    in_=ptb[0:64, 0:1024].rearrange("p (h t) -> p h t", h=H), func=AF.Copy),
                      reads=[PST], writes=["KTs%d" % kts_j])
                if after3 is not None:
                    after3()
            if mode == "gen":
                return s2, s3
            s2()
            s3()
            return kg

        qcnt = [0]
        qsq2 = [qsq, sb(esA, [128, H], F32, "qsqb")]

        def q_tile(g_prev, qcol, selrow):
            qi = qcnt[0] % 2
            qcnt[0] += 1
            qsq = qsq2[qi]
            QSQ = "qsq%d" % qi
            rope(pq, PSQ, cst[g_prev], "cst%d" % g_prev, qr, "qr")
            S.add("pool", lambda e: e.tensor_copy(out=qb[:], in_=qr[:]), reads=["qra", "qrb"], writes=["qb"])
            S.add("pool", lambda e: e.tensor_tensor(out=sq[:], in0=qr[:], in1=qr[:], op=ALU.mult),
                  reads=["qra", "qrb"], writes=["sq"])
            S.add("dve", lambda e: e.tensor_reduce(out=qsq[:], in_=sq[:].rearrange("p (h d) -> p h d", h=H), axis=AX.X, op=ALU.add),
                  reads=["sq"], writes=[QSQ])
            ptb = pt_[:].bitcast(BF16)

            def trq(e):
                for h in range(H):
                    i = e.transpose(ptb[0:64, h * 128:(h + 1) * 128], qb[:, h * 64:(h + 1) * 64], identb[:])
                return i
            S.add("pe", trq, reads=["qb", "identb"], writes=[PST])
            S.add("act", lambda e: e.activation(out=QT[0:64, :, qcol:qcol + 128],
                                                in_=ptb[0:64, 0:1024].rearrange("p (h t) -> p h t", h=H), func=AF.Copy),
                  reads=[PST], writes=["QT"])
            if selrow is None:
                return None

            def selpart():
                S.add("sp", lambda e: e.dma_start(out=sel_c[:, 0, :], in_=pastneg[selrow:selrow + 128, :]), writes=["selc0"], dma="selc0")
                S.add("sp", lambda e: e.dma_start(out=sel_c[:, 1, :], in_=notown[selrow:selrow + 128, :]), writes=["selc1"], dma="selc1")
                S.add("sp", lambda e: e.dma_start(out=sel_c[:, 2, :], in_=ownA[selrow:selrow + 128, :]), writes=["selc2"], dma="selc2")
                S.add("sp", lambda e: e.dma_start(out=isab[:], in_=isAB[selrow:selrow + 128, :]), writes=["isab"], dma="isab")

                def scm(e):
                    for h in range(H):
                        i = e.matmul(ps_[:, h * 32:(h + 1) * 32], QT[0:64, h, qcol:qcol + 128], kmTb[:, h, :], start=True, stop=True)
                    return i
                S.add("pe", scm, reads=["QT", "kmTb"], writes=[PSS])
                S.add("dve", lambda e: e.tensor_tensor(out=sc[:], in0=ps_[:, 0:256].rearrange("p (h j) -> p h j", h=H),
                                                       in1=sel_c[:, 0:1, :].to_broadcast([128, H, 32]), op=ALU.add),
                      reads=[PSS, "selc0"], writes=["sc"])

                def mx(e):
                    for h in range(H):
                        i = e.max(out=mx8[:, h, :], in_=sc[:, h, :])
                    return i
                S.add("dve", mx, reads=["sc"], writes=["mx8"])
                S.add("dve", lambda e: e.tensor_tensor(out=sel[:], in0=sc[:], in1=mx8[:, :, 2:3].to_broadcast([128, H, 32]), op=ALU.is_ge),
                      reads=["sc", "mx8"], writes=["sel"])
                S.add("dve", lambda e: e.tensor_scalar(out=tmp3[:], in0=sc[:], scalar1=-1e29, scalar2=None, op0=ALU.is_gt),
                      reads=["sc"], writes=["tmp3"])
                S.add("dve", lambda e: e.tensor_tensor(out=sel[:], in0=sel[:], in1=tmp3[:], op=ALU.mult),
                      reads=["sel", "tmp3"], writes=["sel"])
                S.add("dve", lambda e: e.tensor_tensor(out=tmp3[:], in0=sel[:], in1=sel_c[:, 2:3, :].to_broadcast([128, H, 32]), op=ALU.mult),
                      reads=["sel", "selc2"], writes=["tmp3"])
                S.add("dve", lambda e: e.tensor_reduce(out=selA[:], in_=tmp3[:], axis=AX.X, op=ALU.add),
                      reads=["tmp3"], writes=["selA"])
                S.add("dve", lambda e: e.tensor_tensor(out=sel[:], in0=sel[:], in1=sel_c[:, 1:2, :].to_broadcast([128, H, 32]), op=ALU.mult),
                      reads=["sel", "selc1"], writes=["sel"])
                S.add("dve", lambda e: e.tensor_scalar(out=sel[:, :, 30], in0=selA[:], scalar1=isab[:, 0:1], scalar2=None, op0=ALU.add),
                      reads=["selA", "isab", "sel"], writes=["sel"])
                S.add("dve", lambda e: e.tensor_copy(out=sel[:, :, 31], in_=isab[:, 1:2].to_broadcast([128, H])),
                      reads=["isab", "sel"], writes=["sel"])
                S.add("dve", lambda e: e.tensor_tensor(out=tq[:], in0=qsq[:], in1=kmxb[:], op=ALU.add),
                      reads=[QSQ, "kmxb"], writes=["tq"])
                S.add("dve", lambda e: e.tensor_scalar(out=tq[:], in0=tq[:], scalar1=-0.5, scalar2=BIG, op0=ALU.mult, op1=ALU.add),
                      reads=["tq"], writes=["tq"])
                S.add("dve", lambda e: e.tensor_tensor(out=tmp3[:], in0=sel[:], in1=tq[:, :, None].to_broadcast([128, H, 32]), op=ALU.mult),
                      reads=["sel", "tq"], writes=["tmp3"])
                S.add("dve", lambda e: e.tensor_scalar(out=mbb[:], in0=tmp3[:], scalar1=-BIG, scalar2=None, op0=ALU.add),
                      reads=["tmp3"], writes=["mbb"])

                def trm(e):
                    for h in range(H):
                        i = e.transpose(ptb[64:96, h * 128:(h + 1) * 128], mbb[:, h, :], identb[:])
                    return i
                S.add("pe", trm, reads=["mbb", "identb"], writes=[PST])
                S.add("act", lambda e: e.activation(out=QT[64:96, :, qcol:qcol + 128],
                                                    in_=ptb[64:96, 0:1024].rearrange("p (h t) -> p h t", h=H), func=AF.Copy),
                      reads=[PST], writes=["QT"])

            return selpart

        def finish_block(kts_j, slot, oh_src, gen_s):
            S.add("pool", lambda e: e.dma_start(out=KTs[kts_j][64:96, :, :], in_=oh_src[:, None, :].to_broadcast([32, H, SBK])),
                  writes=["KTs%d" % kts_j], dma="oh%d" % kts_j)
            if gen_s is not None:
                S.add("dve", lambda e: e.tensor_reduce(out=kmT[:, :, 2 * gen_s:2 * gen_s + 2],
                                                       in_=KTs[kts_j][0:64, :, :].rearrange("p h (b k) -> p h b k", b=2),
                                                       axis=AX.X, op=ALU.add),
                      reads=["KTs%d" % kts_j], writes=["kmT"])
            S.add("sp", lambda e: e.dma_start(out=KT_d[:, :, slot * SBK:(slot + 1) * SBK].rearrange("h r k -> r h k"), in_=KTs[kts_j][:]),
                  reads=["KTs%d" % kts_j], writes=["KTd%d" % slot], dma="kst%d" % kts_j)
            S.add("sp", lambda e: e.dma_start(out=V_d[:, slot, :, :, :].rearrange("h p t e -> p h (t e)"), in_=Vs[kts_j][:].rearrange("p h t e -> p h (t e)")),
                  reads=["Vs%d" % kts_j], writes=["Vd%d" % slot], dma="vst%d" % kts_j)

        blk = 16
        tiles = [(s_, t_) for s_ in range(16) for t_ in range(4)]
        pend = []

        def ld(n_):
            s_, t_ = tiles[n_]
            r0_ = s_ * SBK + t_ * 128
            load_tile(xall[r0_:r0_ + 128, :], cs_all[r0_:r0_ + 128, :])
        ld(0)
        for n_, (s_, t_) in enumerate(tiles):
            j_ = s_ % 2
            fin = (lambda j_=j_, s_=s_: finish_block(j_, s_, oh_all[s_], s_)) if t_ == 3 else None
            st = proc_tile(t_, j_, "gen", after3=fin)
            if len(pend) >= 1:
                pend[-1][0]()
            if n_ + 1 < len(tiles):
                ld(n_ + 1)
            if len(pend) >= 2:
                pend[-2][1]()
            pend.append(st)
        pend[-1][0]()
        pend[-2][1]()
        pend[-1][1]()
        if late2[0] is not None:
            late2[0]()
            late2[0] = None

        pmx = pm_
        S.add("pe", lambda e: e.transpose(pmx[0:8, 0:128], kmx[:], ident[:]), reads=["kmx", "ident"], writes=[PSM])
        kmr = sb(esA, [8, 1], F32, "kmr")
        kmd = sb(esA, [8, 8], F32, "kmd")
        S.add("dve", lambda e: e.tensor_reduce(out=kmr[:], in_=pmx[0:8, 0:128], axis=AX.X, op=ALU.max), reads=[PSM], writes=["kmr"])
        S.add("dve", lambda e: e.tensor_scalar(out=kmd[:], in0=ident[0:8, 0:8], scalar1=kmr[:, 0:1], scalar2=None, op0=ALU.mult),
              reads=["kmr", "ident"], writes=["kmd"])
        S.add("pe", lambda e: e.matmul(pmx[:, 0:8], ones_f[0:8, :], kmd[:], start=True, stop=True), reads=["kmd", "ones_f"], writes=[PSM])
        S.add("dve", lambda e: e.tensor_copy(out=kmxb[:], in_=pmx[:, 0:8]), reads=[PSM], writes=["kmxb"])
        S.add("pool", lambda e: e.tensor_copy(out=kmTb[:], in_=kmT[:]), reads=["kmT"], writes=["kmTb"])

        pend_sel = [None]
        for i in range(4):
            j = blk % 2
            blk += 1
            for t in range(4):
                r0 = i * SBK + t * 128
                load_tile(xown[r0:r0 + 128, :], cs_own[r0:r0 + 128, :])
                g_prev = gt[0] % 2
                proc_tile(t, j, "own", okv=(k_own[r0:r0 + 128, :], v_own[r0:r0 + 128, :]))
                sp_ = q_tile(g_prev, r0, r0)
                if pend_sel[0] is not None:
                    pend_sel[0]()
                pend_sel[0] = sp_
            finish_block(j, 16 + i, oh_own, None)
        pend_sel[0]()
        load_tile(xs[:, :], cs_s[:, :])
        g_prev = gt[0] % 2
        kg_s = proc_tile(0, 0, "smp", okv=(k_s[:, :], v_s[:, :]))
        q_tile(g_prev, 2048, None)
        S.add("pool", lambda e: e.tensor_copy(out=qs_f[:], in_=qr[:]), reads=["qra", "qrb"], writes=["qs_f"])
        S.add("pool", lambda e: e.tensor_copy(out=ks_f[:], in_=kr[kg_s][:]), reads=["kr%da" % kg_s, "kr%db" % kg_s], writes=["ks_f"])
        S.add("pool", lambda e: e.tensor_copy(out=vs_f[:], in_=vf[kg_s][:]), reads=["vf%d" % kg_s], writes=["vs_f"])
        S.barrier()
        esA.close()

        esP = contextlib.ExitStack()
        ck16 = cache_k.rearrange("(n r) d -> n (r d)", r=64)
        ptT_i = sb(esP, [128, NSEQ], I32, "ptT_i")
        ptf = sb(esP, [128, NSEQ], F32, "ptf")
        idxc = sb(esP, [128, NSEQ, 16], I32, "idxc")
        pgs4 = sb(esP, [128, NSEQ, AW], F32, "pgs4")
        pg1 = [sb(esP, [128, AW], F32, "pg1") for _ in range(2)]
        chb = [sb(esP, [128, 4096], F32, "ch") for _ in range(2)]
        if not DEV_SKIP_DECODE:
            S.add("sp", lambda e: e.dma_start(out=ptT_i[:], in_=ptab.rearrange("s p -> p s"), allow_slow_non_contiguous=True), writes=["ptT_i"], dma="d_pt")
            S.add("dve", lambda e: e.tensor_copy(out=ptf[:], in_=ptT_i[:]), reads=["ptT_i"], writes=["ptf"])
            for c in range(16):
                S.add("dve", lambda e, c=c: e.tensor_scalar(out=idxc[:, :, c], in0=ptf[:], scalar1=16.0, scalar2=float(c), op0=ALU.mult, op1=ALU.add),
                      reads=["ptf"], writes=["idxc"])
            S.add("pool", lambda e: e.memset(pgs4[:], 0.0), writes=["pgs4"])

        def ps_dma(k):
            s_, c_ = k // 16, k % 16
            j = k % 2
            S.add("pool", lambda e: e.indirect_dma_start(
                out=chb[j][:], out_offset=None, in_=ck16[:, :], in_offset=bass.IndirectOffsetOnAxis(ap=idxc[:, s_, c_:c_ + 1], axis=0)),
                reads=["idxc"], writes=["ch%d" % j], dma="d_ch%d" % j)

        def ps_red(k):
            j = k % 2
            S.add("dve", lambda e: e.tensor_reduce(out=pg1[j][:], in_=chb[j][:].rearrange("p (r d) -> p d r", r=8), axis=AX.X, op=ALU.add),
                  reads=["ch%d" % j], writes=["pg1_%d" % j])

        def ps_add(k):
            s_ = k // 16
            j = k % 2
            S.add("pool", lambda e: e.tensor_tensor(out=pgs4[:, s_, :], in0=pgs4[:, s_, :], in1=pg1[j][:], op=ALU.add),
                  reads=["pgs4", "pg1_%d" % j], writes=["pgs4"])

        def ps_step(k):
            if DEV_SKIP_DECODE:
                return
            if 0 <= k < 64:
                ps_dma(k)
            if 0 <= k - 1 < 64:
                ps_red(k - 1)
            if 0 <= k - 2 < 64:
                ps_add(k - 2)

        esB = contextlib.ExitStack()
        NKB = 4
        Kc = [sb(esB, [96, SBK], BF16, "Kc") for _ in range(NKB)]
        Vc = [sb(esB, [128, 4, 66], BF16, "Vc") for _ in range(NKB)]
        Pt = [sb(esB, [128, SBK], BF16, "Pt") for _ in range(3)]
        rs = sb(esB, [128, 4], F32, "rs")
        sbanks = [(PB[0], "PB0"), (PB[1], "PB1"), (PB[2], "PB2")]
        accs = [(PX[:, 0:512], "PXa"), (PX[:, 512:1024], "PXb")]
        groups = []
        hi = 0
        for i in range(4):
            slots = list(range(KMAX[i])) + [16 + i]
            for h in range(H):
                for si, slot in enumerate(slots):
                    groups.append((i, h, si, slot, len(slots), hi))
                hi += 1

        def gload(gi_):
            i, h, si, slot, ns, hidx = groups[gi_]
            b = gi_ % NKB
            S.add("sp", lambda e: e.dma_start(out=Kc[b][:], in_=KT_d[h, :, slot * SBK:(slot + 1) * SBK]),
                  reads=["KTd%d" % slot], writes=["Kc%d" % b], dma="kc%d" % b)
            S.add("sp", lambda e: e.dma_start(out=Vc[b][:].rearrange("p t e -> p (t e)"), in_=V_d[h, slot, :, :, :].rearrange("p t e -> p (t e)")),
                  reads=["Vd%d" % slot], writes=["Vc%d" % b], dma="vc%d" % b)

        backs = []

        def front(gi_, kt, n_):
            i, h, si, slot, ns, hidx = groups[gi_]
            b = gi_ % NKB
            sbk, sbkk = sbanks[n_ % 3]
            p_i = n_ % 3
            acc, acck = accs[hidx % 2]
            acc3 = acc.rearrange("p (q e) -> p q e", q=4)
            S.add("pe", lambda e: e.matmul(sbk[:], Kc[b][:, kt * 128:(kt + 1) * 128], QT[:, h, i * SBK:(i + 1) * SBK], start=True, stop=True),
                  reads=["Kc%d" % b, "QT"], writes=[sbkk])
            S.add("act", lambda e: e.activation(out=Pt[p_i][:], in_=sbk[:], func=AF.Exp, scale=0.125),
                  reads=[sbkk], writes=["Pt%d" % p_i])
            if slot >= 16:
                S.add("pool", lambda e: e.tensor_tensor(out=Pt[p_i][:], in0=Pt[p_i][:], in1=trib[:, kt, :], op=ALU.mult),
                      reads=["Pt%d" % p_i, "trib"], writes=["Pt%d" % p_i])
            first = (si == 0 and kt == 0)
            last = (si == ns - 1 and kt == 3)

            def back():
                def pv_(e):
                    for qt in range(4):
                        i_ = e.matmul(acc3[:, qt, 0:65], Pt[p_i][:, qt * 128:(qt + 1) * 128], Vc[b][:, kt, 0:65],
                                      start=(first and qt == 0), stop=last, skip_group_check=True)
                    return i_
                S.add("pe", pv_, reads=["Pt%d" % p_i, "Vc%d" % b], writes=[acck])
                if last:
                    S.add("dve", lambda e: e.reciprocal(out=rs[:], in_=acc3[:, :, 64]), reads=[acck], writes=["rs"])
                    for qt in range(4):
                        S.add("dve", lambda e, qt=qt: e.tensor_scalar(
                            out=att[:, i * 4 + qt, h * 64:(h + 1) * 64], in0=acc3[:, qt, 0:64], scalar1=rs[:, qt:qt + 1], scalar2=None, op0=ALU.mult),
                            reads=[acck, "rs"], writes=["att"])
            return back

        gload(0)
        gload(1)
        n_ = 0
        for gi_ in range(len(groups)):
            if gi_ + 2 < len(groups):
                gload(gi_ + 2)
            if gi_ % 4 == 0:
                ps_step(gi_ // 4)
            for kt in range(4):
                backs.append(front(gi_, kt, n_))
                if n_ >= 2:
                    backs[n_ - 2]()
                n_ += 1
        backs[n_ - 2]()
        backs[n_ - 1]()
        S.barrier()
        esB.close()

        if DEBUG:
            S.add("pool", lambda e: e.dma_start(out=dbg_att[:, :, :], in_=att[:]), reads=["att"], writes=["OUTdbga"], dma="dbga")
            for hh_ in range(H):
                S.add("pool", lambda e, hh_=hh_: e.dma_start(out=dbg_qt[:, hh_, :], in_=QT[:, hh_, :]), reads=["QT"], writes=["OUTdbgq%d" % hh_], dma="dbgq")
        esD = contextlib.ExitStack()
        if not DEV_SKIP_DECODE:
            decode_attention(nc, S, esD, sb, locals())
        else:
            S.add("pool", lambda e: e.memset(att[:, 16, :], 0.0), writes=["att"])
        S.barrier()
        esD.close()
        esP.close()
        esQ.close()

        phase_front(nc, S, sb, locals())
        S.barrier()
        phase_tail(nc, S, sb, locals())

        if DEBUG:
            for u_ in range(5):
                S.add("pool", lambda e, u_=u_: e.dma_start(out=dbg_mg[u_], in_=mg_d[u_]), reads=["mgd%d" % u_], writes=["OUTdbgm%d" % u_], dma="dbgm")
        outs = [k for k in S.res if k.startswith("OUT")]
        S.add("sp", lambda e: e.nop(), reads=outs)
        with nc.Block() as block:
            S.emit(block)
    return nc


def decode_attention(nc, S, es_, sb, L):
    cache_k, cache_v, ptab = L["cache_k"], L["cache_v"], L["ptab"]
    pairm, iota128, hsel, pofs = L["pairm"], L["iota128"], L["hsel"], L["pofs"]
    PX, PB, ident, ones_f, att = L["PX"], L["PB"], L["ident"], L["ones_f"], L["att"]
    qs_f, ks_f, vs_f = L["qs_f"], L["ks_f"], L["vs_f"]
    pgs4 = L["pgs4"]
    selS = sb(es_, [128, NSEQ, 128], F32, "selS")
    pair_sb = sb(es_, [128, 64], F32, "pair_sb")
    iota8 = sb(es_, [8, 128], F32, "iota8")
    hsel_sb = sb(es_, [8, 48], F32, "hsel_sb")
    pofs_sb = sb(es_, [128, 48], F32, "pofs_sb")
    ptr_i = sb(es_, [8, 128], I32, "ptr_i")
    ptr_f = sb(es_, [8, 128], F32, "ptr_f")
    qbc = sb(es_, [128, AW], F32, "qbc")
    vbc = sb(es_, [128, AW], F32, "vbc")
    tmpq = sb(es_, [128, AW], F32, "tmpq")
    spg = sb(es_, [128, H], F32, "spg")
    ssa = sb(es_, [128, H], F32, "ssa")
    sbc = sb(es_, [128, H], F32, "sbc")
    bsc = sb(es_, [8, 64], F32, "bsc")
    mx8 = sb(es_, [8, 8], F32, "dmx8")
    ix8 = sb(es_, [8, 8], U32, "dix8")
    ixf = sb(es_, [8, 8], F32, "dixf")
    lp = sb(es_, [8, 6], F32, "lp")
    eqt = sb(es_, [8, 128], F32, "eqt")
    phys = sb(es_, [8, 6], F32, "phys")
    physd = sb(es_, [8, 48], F32, "physd")
    gidx = sb(es_, [128, 48], I32, "gidx")
    Kg = sb(es_, [128, H, 6, 64], F32, "Kg")
    Vg = sb(es_, [128, H, 6, 65], F32, "Vg")
    tmpk = sb(es_, [128, H, 6, 64], F32, "tmpk")
    sk = sb(es_, [128, 48], F32, "sk")
    m48 = sb(es_, [48, 1], F32, "m48")
    mh = sb(es_, [1, H], F32, "mh")
    mb = sb(es_, [128, H], F32, "mb")
    Pk = sb(es_, [128, 48], F32, "Pk")
    pself = sb(es_, [128, H], F32, "pself")
    orow = sb(es_, [1, H, 65], F32, "orow")
    den = sb(es_, [1, H], F32, "den")
    arow = sb(es_, [1, AW], F32, "arow")

    S.add("pool", lambda e: e.memset(att[:, 16, :], 0.0), writes=["att"])
    S.add("sp", lambda e: e.dma_start(out=pair_sb[:], in_=pairm[:, :]), writes=["pair_sb"], dma="d_c1")
    S.add("sp", lambda e: e.dma_start(out=iota8[:], in_=iota128[:, :]), writes=["iota8"], dma="d_c2")
    S.add("sp", lambda e: e.dma_start(out=hsel_sb[:], in_=hsel[:, :]), writes=["hsel_sb"], dma="d_c3")
    S.add("sp", lambda e: e.dma_start(out=pofs_sb[:], in_=pofs[:, :]), writes=["pofs_sb"], dma="d_c4")
    for s in range(NSEQ):
        S.add("dve", lambda e, s=s: e.tensor_copy(out=selS[:, s, :], in_=ident[:, s:s + 1].to_broadcast([128, 128])), reads=["ident"], writes=["selS"])
    S.add("pool", lambda e: e.memset(Vg[:], 1.0), writes=["Vg"])
    S.add("dve", lambda e: e.tensor_tensor(out=tmpq[:], in0=qs_f[:], in1=ks_f[:], op=ALU.mult), reads=["qs_f", "ks_f"], writes=["tmpq"])
    S.add("dve", lambda e: e.tensor_reduce(out=ssa[:], in_=tmpq[:].rearrange("p (h d) -> p h d", h=H), axis=AX.X, op=ALU.add), reads=["tmpq"], writes=["ssa"])
    gi = 0
    for s in range(NSEQ):
        S.add("pe", lambda e, s=s: e.matmul(PB[0][:], selS[:, s, :], qs_f[:], start=True, stop=True), reads=["selS", "qs_f"], writes=["PB0"])
        S.add("act", lambda e: e.activation(out=qbc[:], in_=PB[0][:], func=AF.Copy), reads=["PB0"], writes=["qbc"])
        S.add("pe", lambda e, s=s: e.matmul(PB[1][:], selS[:, s, :], vs_f[:], start=True, stop=True), reads=["selS", "vs_f"], writes=["PB1"])
        S.add("act", lambda e: e.activation(out=vbc[:], in_=PB[1][:], func=AF.Copy), reads=["PB1"], writes=["vbc"])
        S.add("pe", lambda e, s=s: e.matmul(PB[2][:, 0:H], selS[:, s, :], ssa[:], start=True, stop=True), reads=["selS", "ssa"], writes=["PB2"])
        S.add("act", lambda e: e.activation(out=sbc[:], in_=PB[2][:, 0:H], func=AF.Copy), reads=["PB2"], writes=["sbc"])
        S.add("dve", lambda e, s=s: e.tensor_tensor(out=tmpq[:], in0=pgs4[:, s, :], in1=qbc[:], op=ALU.mult), reads=["pgs4", "qbc"], writes=["tmpq"])
        S.add("dve", lambda e: e.tensor_reduce(out=spg[:], in_=tmpq[:].rearrange("p (h d) -> p h d", h=H), axis=AX.X, op=ALU.add), reads=["tmpq"], writes=["spg"])
        S.add("pe", lambda e: e.matmul(PB[3][0:8, 0:64], spg[:], pair_sb[:], start=True, stop=True), reads=["spg", "pair_sb"], writes=["PB3"])
        S.add("dve", lambda e: e.tensor_copy(out=bsc[:], in_=PB[3][0:8, 0:64]), reads=["PB3"], writes=["bsc"])
        S.add("dve", lambda e: e.max(out=mx8[:], in_=bsc[:]), reads=["bsc"], writes=["dmx8"])
        S.add("dve", lambda e: e.max_index(out=ix8[:], in_max=mx8[:], in_values=bsc[:]), reads=["dmx8", "bsc"], writes=["dix8"])
        S.add("dve", lambda e: e.tensor_copy(out=ixf[:], in_=ix8[:]), reads=["dix8"], writes=["dixf"])
        lp3 = lp[:].rearrange("p (k e) -> p k e", e=2)
        for e2 in range(2):
            S.add("dve", lambda e, e2=e2: e.tensor_scalar(out=lp3[:, :, e2], in0=ixf[:, 0:3], scalar1=2.0, scalar2=float(e2), op0=ALU.mult, op1=ALU.add),
                  reads=["dixf"], writes=["lp"])
        S.add("sp", lambda e, s=s: e.dma_start(out=ptr_i[:], in_=ptab[s:s + 1, :].to_broadcast([8, NPAGE])), writes=["ptr_i"], dma="d_ptr")
        S.add("dve", lambda e: e.tensor_copy(out=ptr_f[:], in_=ptr_i[:]), reads=["ptr_i"], writes=["ptr_f"])
        for sl in range(6):
            S.add("dve", lambda e, sl=sl: e.scalar_tensor_tensor(out=eqt[:], in0=iota8[:], scalar=lp[:, sl:sl + 1], in1=ptr_f[:],
                                                                 op0=ALU.is_equal, op1=ALU.mult, accum_out=phys[:, sl:sl + 1]),
                  reads=["iota8", "lp", "ptr_f"], writes=["eqt", "phys"])
        S.add("dve", lambda e: e.tensor_tensor(out=physd[:].rearrange("p (h k) -> p h k", h=H), in0=hsel_sb[:].rearrange("p (h k) -> p h k", h=H),
                                               in1=phys[:, None, :].to_broadcast([8, H, 6]), op=ALU.mult),
              reads=["hsel_sb", "phys"], writes=["physd"])
        S.add("pe", lambda e: e.matmul(PB[4][:, 0:48], ones_f[0:8, :], physd[:], start=True, stop=True), reads=["ones_f", "physd"], writes=["PB4"])
        S.add("dve", lambda e: e.scalar_tensor_tensor(out=gidx[:], in0=PB[4][:, 0:48], scalar=1024.0, in1=pofs_sb[:], op0=ALU.mult, op1=ALU.add),
              reads=["PB4", "pofs_sb"], writes=["gidx"])
        for h in range(H):
            for sl in range(6):
                col = h * 6 + sl
                S.add("pool", lambda e, h=h, sl=sl, col=col: e.indirect_dma_start(
                    out=Kg[:, h, sl, :], out_offset=None, in_=cache_k[:, :], in_offset=bass.IndirectOffsetOnAxis(ap=gidx[:, col:col + 1], axis=0)),
                    reads=["gidx"], writes=["Kg%d" % col], dma="d_kg%d" % (col % 8))
                S.add("pool", lambda e, h=h, sl=sl, col=col: e.indirect_dma_start(
                    out=Vg[:, h, sl, 0:64], out_offset=None, in_=cache_v[:, :], in_offset=bass.IndirectOffsetOnAxis(ap=gidx[:, col:col + 1], axis=0)),
                    reads=["gidx"], writes=["Vg%d" % col], dma="d_vg%d" % (col % 8))
        kgk = ["Kg%d" % c_ for c_ in range(48)]
        vgk = ["Vg%d" % c_ for c_ in range(48)]
        S.add("dve", lambda e: e.tensor_tensor(out=tmpk[:], in0=Kg[:], in1=qbc[:].rearrange("p (h d) -> p h d", h=H)[:, :, None, :].to_broadcast([128, H, 6, 64]), op=ALU.mult),
              reads=kgk + ["qbc"], writes=["tmpk"])
        S.add("dve", lambda e: e.tensor_reduce(out=sk[:], in_=tmpk[:].rearrange("p h k d -> p (h k) d"), axis=AX.X, op=ALU.add), reads=["tmpk"], writes=["sk"])
        S.add("pe", lambda e: e.transpose(PB[5][0:48, 0:128], sk[:], ident[:]), reads=["sk", "ident"], writes=["PB5"])
        S.add("dve", lambda e: e.tensor_reduce(out=m48[:], in_=PB[5][0:48, 0:128], axis=AX.X, op=ALU.max), reads=["PB5"], writes=["m48"])
        S.add("pe", lambda e: e.transpose(PB[5][0:1, 0:48], m48[:], ident[0:48, 0:48]), reads=["m48", "ident"], writes=["PB5"])
        S.add("dve", lambda e: e.tensor_reduce(out=mh[:], in_=PB[5][0:1, 0:48].rearrange("p (h k) -> p h k", h=H), axis=AX.X, op=ALU.max), reads=["PB5"], writes=["mh"])
        S.add("dve", lambda e: e.tensor_tensor(out=mh[:], in0=mh[:], in1=sbc[0:1, :], op=ALU.max), reads=["mh", "sbc"], writes=["mh"])
        S.add("pe", lambda e: e.matmul(PB[5][:, 0:H], ones_f[0:1, :], mh[:], start=True, stop=True), reads=["ones_f", "mh"], writes=["PB5"])
        S.add("dve", lambda e: e.tensor_copy(out=mb[:], in_=PB[5][:, 0:H]), reads=["PB5"], writes=["mb"])
        S.add("dve", lambda e: e.tensor_tensor(out=sk[:].rearrange("p (h k) -> p h k", h=H), in0=sk[:].rearrange("p (h k) -> p h k", h=H),
                                               in1=mb[:, :, None].to_broadcast([128, H, 6]), op=ALU.subtract), reads=["sk", "mb"], writes=["sk"])
        S.add("act", lambda e: e.activation(out=Pk[:], in_=sk[:], func=AF.Exp, scale=0.125), reads=["sk"], writes=["Pk"])
        S.add("dve", lambda e: e.tensor_tensor(out=pself[:], in0=sbc[:], in1=mb[:], op=ALU.subtract), reads=["sbc", "mb"], writes=["pself"])
        S.add("act", lambda e: e.activation(out=pself[:], in_=pself[:], func=AF.Exp, scale=0.125), reads=["pself"], writes=["pself"])

        def pv(e):
            for h in range(H):
                for sl in range(6):
                    i = e.matmul(PX[0:1, h * 128:h * 128 + 65], Pk[:, h * 6 + sl:h * 6 + sl + 1], Vg[:, h, sl, :], start=(sl == 0), stop=(sl == 5),
                                 skip_group_check=True)
            return i
        S.add("pe", pv, reads=["Pk"] + vgk, writes=["PX"])
        o3 = PX[0:1, :].rearrange("p (h e) -> p h e", h=H)
        S.add("dve", lambda e: e.tensor_tensor(out=orow[:, :, 0:64], in0=vbc[0:1, :].rearrange("p (h d) -> p h d", h=H),
                                               in1=pself[0:1, :, None].to_broadcast([1, H, 64]), op=ALU.mult), reads=["vbc", "pself"], writes=["orow"])
        S.add("dve", lambda e: e.tensor_tensor(out=orow[:, :, 0:64], in0=orow[:, :, 0:64], in1=o3[:, :, 0:64], op=ALU.add), reads=["orow", "PX"], writes=["orow"])
        S.add("dve", lambda e: e.tensor_tensor(out=den[:], in0=pself[0:1, :], in1=o3[:, :, 64], op=ALU.add), reads=["pself", "PX"], writes=["den"])
        S.add("dve", lambda e: e.reciprocal(out=den[:], in_=den[:]), reads=["den"], writes=["den"])
        S.add("dve", lambda e: e.tensor_tensor(out=arow[:].rearrange("p (h d) -> p h d", h=H), in0=orow[:, :, 0:64],
                                               in1=den[:, :, None].to_broadcast([1, H, 64]), op=ALU.mult), reads=["orow", "den"], writes=["arow"])
        S.add("pool", lambda e, s=s: e.dma_start(out=att[s:s + 1, 16, :], in_=arow[:]), reads=["arow", "att"], writes=["att"], dma="d_att")


def _rope_tab(pos):
    half = 32
    inv = (10000.0 ** (-np.arange(half, dtype=np.float32) / half)).astype(np.float32)
    ang = pos.astype(np.float32)[:, None] * inv[None, :]
    return np.concatenate([np.cos(ang), np.sin(ang)], axis=1).astype(np.float32)


_NC = None


def kernel(x_prompt, x_sample, cache_k, cache_v, state_pool, page_table, p_prompt, p_sample,
           w_in, w_pool_mix, pool_scale, w_pool_out, w_att_out, w_o, ln_g, ln_b, w_ple, w_ple_gate):
    global _NC
    f = lambda a: np.ascontiguousarray(np.asarray(a, dtype=np.float32))
    x_prompt = f(x_prompt); x_sample = f(x_sample); p_prompt = f(p_prompt); p_sample = f(p_sample)
    ck = f(cache_k).reshape(NPHYS * 128 * 8, 64)
    cv = f(cache_v).reshape(NPHYS * 128 * 8, 64)
    state_pool = f(state_pool)
    page_table = np.ascontiguousarray(np.asarray(page_table, dtype=np.int32))
    cs_all = _rope_tab(np.arange(SEQ))
    cs_s = _rope_tab(np.full((128,), PAST))
    ident = np.eye(128, dtype=np.float32)
    oh_all = np.zeros((16, 32, SBK), np.float32)
    for s in range(15):
        oh_all[s, 2 * s, :256] = 1.0
        oh_all[s, 2 * s + 1, 256:] = 1.0
    oh_own = np.zeros((32, SBK), np.float32)
    oh_own[30, :256] = 1.0
    oh_own[31, 256:] = 1.0
    tri = np.ones((128, 4, SBK), np.float32)
    for kt in range(4):
        for k in range(128):
            kk = kt * 128 + k
            kb_, kl = kk // 256, kk % 256
            q = np.arange(SBK)
            same = (q // 256) == kb_
            tri[k, kt, :] = np.where(same & ((q % 256) < kl), 0.0, 1.0)
    tri = tri.reshape(128, 4 * SBK)
    pairm = np.zeros((128, 64), np.float32)
    pairm[np.arange(128), np.arange(128) // 2] = 1.0
    iota128 = np.tile(np.arange(128, dtype=np.float32)[None, :], (8, 1))
    hsel = np.zeros((8, 48), np.float32)
    for h in range(8):
        hsel[h, h * 6:(h + 1) * 6] = 1.0
    shared = dict(cs_all=cs_all, cs_s=cs_s, oh_all=oh_all, oh_own=oh_own, tri=tri, ident=ident,
                  w_in=f(w_in)[0], w_mix=f(w_pool_mix)[0], pscale=f(pool_scale)[0], w_po=f(w_pool_out)[0],
                  w_ao=f(w_att_out)[0], w_o=f(w_o)[0], ln_g=f(ln_g)[0], ln_b=f(ln_b)[0], w_ple=f(w_ple)[0],
                  w_pg=f(w_ple_gate)[0], cache_k=ck, cache_v=cv, pairm=pairm, iota128=iota128, hsel=hsel,
                  pofs=(np.arange(128, dtype=np.float32)[:, None] * 8 + np.repeat(np.arange(8, dtype=np.float32), 6)[None, :]).astype(np.float32))
    in_maps = []
    for c in range(NCORES):
        b, r = c // 4, c % 4
        sbs = own_sbs(r)
        tok = np.concatenate([np.arange(s * SBK, (s + 1) * SBK) for s in sbs])
        xown = x_prompt[b][tok]
        xhalo = np.zeros((16, 16, D), np.float32)
        corr = np.ones((16, 4, 16), np.float32)
        for ti in range(16):
            t0 = tok[ti * 128]
            for k in range(16):
                p = t0 - 16 + k
                if p >= 0:
                    xhalo[ti, k] = x_prompt[b, p]
            for g, w in enumerate((2, 4, 8, 16)):
                for k in range(16):
                    corr[ti, g, k] = w / min(t0 + k + 1, w)
        qblk = tok // 256
        jj = np.arange(32)[None, :]
        pastneg = np.where(jj < qblk[:, None], 0.0, -1e30).astype(np.float32)
        sbq = tok // SBK
        notown = np.where((jj // 2) == sbq[:, None], 0.0, 1.0).astype(np.float32)
        ownA = (jj == (2 * sbq)[:, None]).astype(np.float32)
        inA = ((tok % SBK) < 256)
        isAB = np.stack([inA, ~inA], axis=1).astype(np.float32)
        xs = np.zeros((128, D), np.float32); xs[:NSEQ] = x_sample[c * NSEQ:(c + 1) * NSEQ, 0]
        pss = np.zeros((128, PLE), np.float32); pss[:NSEQ] = p_sample[0, c * NSEQ:(c + 1) * NSEQ, 0]
        m = dict(shared)
        m.update(xall=x_prompt[b], xown=np.ascontiguousarray(xown), xhalo=xhalo, pown=np.ascontiguousarray(p_prompt[0, b][tok]),
                 xs=xs, pss=pss, cs_own=np.ascontiguousarray(cs_all[tok]), pastneg=pastneg, notown=notown, ownA=ownA,
                 isAB=isAB, corr=corr, ptab=np.ascontiguousarray(page_table[c * NSEQ:(c + 1) * NSEQ]),
                 spool=np.ascontiguousarray(state_pool[0, c * NSEQ:(c + 1) * NSEQ]))
        in_maps.append(m)
    if _NC is None:
        _NC = build_nc()
    res = run_bass_kernel_spmd(_NC, in_maps, core_ids=list(range(NCORES)))
    R = res.results
    global _last
    _last = R
    y_prompt = np.zeros((2, SEQ, D), np.float32)
    k_prompt = np.zeros((1, 2, SEQ, H, HD), np.float32)
    v_prompt = np.zeros((1, 2, SEQ, H, HD), np.float32)
    pool_prompt = np.zeros((1, 2, 15, 512), np.float32)
    y_sample = np.zeros((32, 1, D), np.float32)
    k_sample = np.zeros((1, 32, 1, H, HD), np.float32)
    v_sample = np.zeros((1, 32, 1, H, HD), np.float32)
    pool_sample = np.zeros((1, 32, 15, 512), np.float32)
    for c in range(NCORES):
        b, r = c // 4, c % 4
        tok = np.concatenate([np.arange(s * SBK, (s + 1) * SBK) for s in own_sbs(r)])
        y_prompt[b, tok] = R[c]["y_own"]
        k_prompt[0, b, tok] = R[c]["k_own"].reshape(2048, H, HD)
        v_prompt[0, b, tok] = R[c]["v_own"].reshape(2048, H, HD)
        if r == 0:
            pool_prompt[0, b] = R[c]["poolp"]
        y_sample[c * NSEQ:(c + 1) * NSEQ, 0] = R[c]["y_s"][:NSEQ]
        k_sample[0, c * NSEQ:(c + 1) * NSEQ, 0] = R[c]["k_s"][:NSEQ].reshape(NSEQ, H, HD)
        v_sample[0, c * NSEQ:(c + 1) * NSEQ, 0] = R[c]["v_s"][:NSEQ].reshape(NSEQ, H, HD)
        pool_sample[0, c * NSEQ:(c + 1) * NSEQ] = R[c]["pool_s"]
    return (y_prompt, y_sample, k_prompt, v_prompt, pool_prompt, k_sample, v_sample, pool_sample)


def _silu_from_psum(S, ps_ap, ps_key, th, th_key, out_ap, out_key, extra_in1=None, extra_key=None):
    S.add("act", lambda e: e.activation(out=th, in_=ps_ap, func=AF.Tanh, scale=0.5), reads=[ps_key], writes=[th_key])
    S.add("dve", lambda e: e.tensor_scalar(out=th, in0=th, scalar1=0.5, scalar2=0.5, op0=ALU.mult, op1=ALU.add),
          reads=[th_key], writes=[th_key])
    if extra_in1 is None:
        S.add("dve", lambda e: e.tensor_tensor(out=out_ap, in0=ps_ap, in1=th, op=ALU.mult),
              reads=[ps_key, th_key], writes=[out_key])
    else:
        S.add("dve", lambda e: e.tensor_tensor(out=th, in0=ps_ap, in1=th, op=ALU.mult),
              reads=[ps_key, th_key], writes=[th_key])
        S.add("pool", lambda e: e.tensor_tensor(out=out_ap, in0=th, in1=extra_in1, op=ALU.mult),
              reads=[th_key, extra_key], writes=[out_key])


def phase_front(nc, S, sb, L):
    w_in, w_mix, pscale, w_po, w_ao = L["w_in"], L["w_mix"], L["pscale"], L["w_po"], L["w_ao"]
    xown, xs, xhalo, corr, spool = L["xown"], L["xs"], L["xhalo"], L["corr"], L["spool"]
    poolp, pool_s, mg_d = L["poolp"], L["pool_s"], L["mg_d"]
    PX, PB, ident, identb, att = L["PX"], L["PB"], L["ident"], L["identb"], L["att"]
    es_ = contextlib.ExitStack()
    L["esF"] = es_

    def wload(name, src_ap, shape):
        t = sb(es_, shape, BF16, name)
        S.add("pool", lambda e: e.dma_start(out=t[:], in_=src_ap), writes=[name], dma="w_" + name)
        return t
    wzb = wload("wzb", w_in[:, 1536:2048].rearrange("(c p) n -> p c n", p=128), [128, 8, 512])
    wu = wload("wu", w_in[:, 2048:2560].rearrange("(c p) n -> p c n", p=128), [128, 8, 512])
    wza = wload("wza", w_in[:, 2560:3072].rearrange("(c p) n -> p c n", p=128), [128, 8, 512])
    wga = wload("wga", w_in[:, 3072:4096].rearrange("(c p) n -> p c n", p=128), [128, 8, 1024])
    wgb = wload("wgb", w_in[:, 4096:5120].rearrange("(c p) n -> p c n", p=128), [128, 8, 1024])
    wmix = wload("wmix", w_mix.rearrange("g c e -> c g e"), [128, 4, 128])
    wpo = wload("wpo", w_po.rearrange("(c p) n -> p c n", p=128), [128, 4, 1024])
    wao = wload("wao", w_ao.rearrange("(c p) n -> p c n", p=128), [128, 4, 1024])
    psc = sb(es_, [128, 4], F32, "psc")
    S.add("sp", lambda e: e.dma_start(out=psc[:], in_=pscale.rearrange("(g c) -> c g", g=4), allow_slow_non_contiguous=True),
          writes=["psc"], dma="psc")
    corb = sb(es_, [128, 16 * 4 * 16], F32, "corb")
    S.add("sp", lambda e: e.dma_start(out=corb[:], in_=corr.rearrange("a g k -> (a g k)")[None, :].to_broadcast([128, 1024])),
          writes=["corb"], dma="corb")
    cor4 = corb[:].rearrange("p (a g k) -> p a g k", a=16, g=4)

    xt = sb(es_, [128, D], F32, "fxt")
    xh = sb(es_, [16, D], F32, "fxh")
    xTu = sb(es_, [128, 8, SBK], BF16, "xTu")
    xTh = sb(es_, [128, 8, 16], BF16, "xTh")
    uext = sb(es_, [128, 4, 16 + SBK], F32, "uext")
    tA = sb(es_, [128, 16 + SBK], F32, "tA")
    tB = sb(es_, [128, 16 + SBK], F32, "tB")
    dT = sb(es_, [128, 4, SBK], BF16, "dT")
    th = sb(es_, [128, SBK], F32, "th")
    szT = sb(es_, [128, 4, SBK], BF16, "szT")
    pzT = sb(es_, [128, 4, SBK], BF16, "pzT")
    azb = sb(es_, [128, AW], BF16, "azb")
    azT = sb(es_, [128, 4, SBK], BF16, "azT")
    tg1 = sb(es_, [128, SBK], F32, "tg1")
    tg2 = sb(es_, [128, SBK], F32, "tg2")
    t1 = sb(es_, [128, SBK], F32, "t1")
    t2 = sb(es_, [128, SBK], F32, "t2")
    mgT = sb(es_, [128, 8, SBK], BF16, "mgT")
    hsb = sb(es_, [16, 4, 512], F32, "hsb")
    hT = sb(es_, [128, 4, 4, 16], F32, "hT")
    ssum = sb(es_, [128, 4], F32, "ssum")
    pout = sb(es_, [16, 512], F32, "pout")

    for u in range(5):
        T = SBK if u < 4 else 128
        nt = T // 128
        for t in range(nt):
            src = xown[u * SBK + t * 128:u * SBK + (t + 1) * 128, :] if u < 4 else xs[:, :]
            S.add("sp", lambda e, src=src: e.dma_start(out=xt[:], in_=src), writes=["fxt"], dma="fxt")

            def tr(e):
                for c in range(8):
                    i = e.transpose(PX[:, c * 128:(c + 1) * 128], xt[:, c * 128:(c + 1) * 128], ident[:])
                return i
            S.add("pe", tr, reads=["fxt", "ident"], writes=["PX"])
            S.add("act", lambda e, t=t: e.activation(out=xTu[:, :, t * 128:(t + 1) * 128],
                                                     in_=PX[:].rearrange("p (c t) -> p c t", c=8), func=AF.Copy),
                  reads=["PX"], writes=["xTu"])
        if u < 4:
            S.add("sp", lambda e, u=u: e.dma_start(out=xh[:], in_=xhalo[u * 4, :, :]), writes=["fxh"], dma="fxh")

            def trh(e):
                for c in range(8):
                    i = e.transpose(PX[:, c * 16:(c + 1) * 16], xh[:, c * 128:(c + 1) * 128], ident[0:16, 0:16])
                return i
            S.add("pe", trh, reads=["fxh", "ident"], writes=["PX"])
            S.add("act", lambda e: e.activation(out=xTh[:].rearrange("p c k -> p (c k)"), in_=PX[:, 0:128], func=AF.Copy),
                  reads=["PX"], writes=["xTh"])
        for g in range(4):
            pb, pbk = PB[g % 2], "PB%d" % (g % 2)

            def mu(e, g=g, pb=pb, T=T, u=u):
                for c in range(8):
                    i = e.matmul(pb[:, 0:T], wu[:, c, g * 128:(g + 1) * 128], xTu[:, c, 0:T], start=(c == 0), stop=(c == 7))
                return i
            S.add("pe", mu, reads=["wu", "xTu"], writes=[pbk])
            S.add("act", lambda e, g=g, pb=pb, T=T: e.activation(out=uext[:, g, 16:16 + T], in_=pb[:, 0:T], func=AF.Copy),
                  reads=[pbk], writes=["uext%d" % g])
            if u < 4:
                def muh(e, g=g, pb=pb):
                    for c in range(8):
                        i = e.matmul(pb[:, 0:16], wu[:, c, g * 128:(g + 1) * 128], xTh[:, c, :], start=(c == 0), stop=(c == 7))
                    return i
                S.add("pe", muh, reads=["wu", "xTh"], writes=[pbk])
                S.add("act", lambda e, g=g, pb=pb: e.activation(out=uext[:, g, 0:16], in_=pb[:, 0:16], func=AF.Copy),
                      reads=[pbk], writes=["uext%d" % g])
        if u < 4:
            Ln = 16 + T
            for g in range(4):
                w = 2 ** (g + 1)
                cur = uext[:, g, :]
                ck = "uext%d" % g
                S.add("dve", lambda e, cur=cur: e.tensor_tensor(out=tA[:, 1:Ln], in0=cur[:, 1:Ln], in1=cur[:, 0:Ln - 1], op=ALU.add),
                      reads=[ck], writes=["tA"])
                fin, fk = tA, "tA"
                if g >= 1:
                    S.add("dve", lambda e: e.tensor_tensor(out=tB[:, 3:Ln], in0=tA[:, 3:Ln], in1=tA[:, 1:Ln - 2], op=ALU.add),
                          reads=["tA"], writes=["tB"])
                    fin, fk = tB, "tB"
                if g >= 2:
                    S.add("dve", lambda e: e.tensor_tensor(out=tA[:, 7:Ln], in0=tB[:, 7:Ln], in1=tB[:, 3:Ln - 4], op=ALU.add),
                          reads=["tB"], writes=["tA"])
                    fin, fk = tA, "tA"
                if g >= 3:
                    S.add("dve", lambda e: e.tensor_tensor(out=tB[:, 15:Ln], in0=tA[:, 15:Ln], in1=tA[:, 7:Ln - 8], op=ALU.add),
                          reads=["tA"], writes=["tB"])
                    fin, fk = tB, "tB"
                S.add("dve", lambda e, fin=fin, g=g, u=u: e.tensor_tensor(out=fin[:, 16:32], in0=fin[:, 16:32], in1=cor4[:, u * 4, g, :], op=ALU.mult),
                      reads=[fk, "corb"], writes=[fk])
                S.add("dve", lambda e, fin=fin, g=g, w=w, cur=cur, T=T: e.scalar_tensor_tensor(
                    out=dT[:, g, 0:T], in0=fin[:, 16:16 + T], scalar=1.0 / w, in1=cur[:, 16:16 + T], op0=ALU.mult, op1=ALU.subtract),
                    reads=[fk, ck], writes=["dT"])
            if u == 3:
                def trp(e):
                    for g in range(4):
                        i = e.transpose(PB[2][0:15, g * 128:(g + 1) * 128], uext[:, g, 16 + 497:16 + 512], ident[:])
                    return i
                S.add("pe", trp, reads=["uext0", "uext1", "uext2", "uext3", "ident"], writes=["PB2"])
                S.add("dve", lambda e: e.tensor_copy(out=pout[0:15, :], in_=PB[2][0:15, :]), reads=["PB2"], writes=["pout"])
                S.add("sp", lambda e: e.dma_start(out=poolp[:, :], in_=pout[0:15, :]), reads=["pout"], writes=["OUTpoolp"], dma="opoolp")
        else:
            for s in range(NSEQ):
                S.add("sp", lambda e, s=s: e.dma_start(out=hsb[0:15, s, :], in_=spool[s, :, :]), writes=["hsb"], dma="hsb")
                S.add("sp", lambda e, s=s: e.dma_start(out=pool_s[s, 0:14, :], in_=spool[s, 1:15, :]), writes=["OUTps%d" % s], dma="ops%d" % s)
            for s in range(NSEQ):
                def trs(e, s=s):
                    for g in range(4):
                        i = e.transpose(PB[2][:, (s * 4 + g) * 16:(s * 4 + g) * 16 + 15], hsb[0:15, s, g * 128:(g + 1) * 128], ident[0:15, 0:15])
                    return i
                S.add("pe", trs, reads=["hsb", "ident"], writes=["PB2"])
            S.add("dve", lambda e: e.memset(hT[:], 0.0), writes=["hT"])
            S.add("dve", lambda e: e.tensor_copy(out=hT[:, :, :, 0:15], in_=PB[2][:, 0:256].rearrange("p (s g r) -> p s g r", s=4, g=4)[:, :, :, 0:15]),
                  reads=["PB2", "hT"], writes=["hT"])
            S.add("pool", lambda e: e.memset(dT[:], 0.0), writes=["dT"])
            for g in range(4):
                w = 2 ** (g + 1)
                S.add("dve", lambda e, g=g, w=w: e.tensor_reduce(out=ssum[:], in_=hT[:, :, g, 16 - w:15], axis=AX.X, op=ALU.add),
                      reads=["hT"], writes=["ssum"])
                S.add("dve", lambda e, g=g: e.tensor_tensor(out=ssum[:], in0=ssum[:], in1=uext[:, g, 16:16 + NSEQ], op=ALU.add),
                      reads=["ssum", "uext%d" % g], writes=["ssum"])
                S.add("dve", lambda e, g=g, w=w: e.scalar_tensor_tensor(
                    out=dT[:, g, 0:NSEQ], in0=ssum[:], scalar=1.0 / w, in1=uext[:, g, 16:16 + NSEQ], op0=ALU.mult, op1=ALU.subtract),
                    reads=["ssum", "uext%d" % g, "dT"], writes=["dT"])

            def tru(e):
                for g in range(4):
                    i = e.transpose(PB[3][0:NSEQ, g * 128:(g + 1) * 128], uext[:, g, 16:16 + NSEQ], ident[:])
                return i
            S.add("pe", tru, reads=["uext0", "uext1", "uext2", "uext3", "ident"], writes=["PB3"])
            S.add("dve", lambda e: e.tensor_copy(out=pout[0:NSEQ, :], in_=PB[3][0:NSEQ, :]), reads=["PB3"], writes=["pout"])
            S.add("sp", lambda e: e.dma_start(out=pool_s[:, 14, :], in_=pout[0:NSEQ, :]), reads=["pout"], writes=["OUTpsu"], dma="opsu")
        for g in range(4):
            pb, pbk = PB[g % 2], "PB%d" % (g % 2)

            def mz(e, g=g, pb=pb, T=T):
                for c in range(8):
                    i = e.matmul(pb[:, 0:T], wza[:, c, g * 128:(g + 1) * 128], xTu[:, c, 0:T], start=(c == 0), stop=(c == 7))
                return i
            S.add("pe", mz, reads=["wza", "xTu"], writes=[pbk])
            _silu_from_psum(S, pb[:, 0:T], pbk, th[:, 0:T], "th", szT[:, g, 0:T], "szT")
            S.add("pe", lambda e, g=g, pb=pb, T=T: e.matmul(pb[:, 0:T], wmix[:, g, :], dT[:, g, 0:T], start=True, stop=True),
                  reads=["wmix", "dT"], writes=[pbk])
            S.add("dve", lambda e, g=g, pb=pb, T=T: e.scalar_tensor_tensor(
                out=pzT[:, g, 0:T], in0=pb[:, 0:T], scalar=psc[:, g:g + 1], in1=szT[:, g, 0:T], op0=ALU.mult, op1=ALU.mult),
                reads=[pbk, "psc", "szT"], writes=["pzT"])
        for t in range(nt):
            def mzb(e, t=t):
                for c in range(8):
                    i = e.matmul(PB[0][:], xTu[:, c, t * 128:(t + 1) * 128], wzb[:, c, :], start=(c == 0), stop=(c == 7))
                return i
            S.add("pe", mzb, reads=["xTu", "wzb"], writes=["PB0"])
            atile = att[:, u * 4 + t, :]
            _silu_from_psum(S, PB[0][:], "PB0", th[:], "th", azb[:], "azb", extra_in1=atile, extra_key="att")
            ptb = PB[1][:].bitcast(BF16)

            def tra(e):
                for cc in range(4):
                    i = e.transpose(ptb[:, cc * 128:(cc + 1) * 128], azb[:, cc * 128:(cc + 1) * 128], identb[:])
                return i
            S.add("pe", tra, reads=["azb", "identb"], writes=["PB1"])
            S.add("act", lambda e, t=t, ptb=ptb: e.activation(out=azT[:, :, t * 128:(t + 1) * 128],
                                                              in_=ptb[:, 0:512].rearrange("p (c t) -> p c t", c=4), func=AF.Copy),
                  reads=["PB1"], writes=["azT"])
        for m in range(8):
            def myp(e, m=m, T=T):
                for cc in range(4):
                    i = e.matmul(PB[2][:, 0:T], wpo[:, cc, m * 128:(m + 1) * 128], pzT[:, cc, 0:T], start=(cc == 0), stop=(cc == 3))
                return i

            def mya(e, m=m, T=T):
                for cc in range(4):
                    i = e.matmul(PB[3][:, 0:T], wao[:, cc, m * 128:(m + 1) * 128], azT[:, cc, 0:T], start=(cc == 0), stop=(cc == 3))
                return i

            def mga(e, m=m, T=T):
                for c in range(8):
                    i = e.matmul(PB[4][:, 0:T], wga[:, c, m * 128:(m + 1) * 128], xTu[:, c, 0:T], start=(c == 0), stop=(c == 7))
                return i

            def mgb(e, m=m, T=T):
                for c in range(8):
                    i = e.matmul(PB[5][:, 0:T], wgb[:, c, m * 128:(m + 1) * 128], xTu[:, c, 0:T], start=(c == 0), stop=(c == 7))
                return i
            S.add("pe", myp, reads=["wpo", "pzT"], writes=["PB2"])
            S.add("pe", mya, reads=["wao", "azT"], writes=["PB3"])
            S.add("pe", mga, reads=["wga", "xTu"], writes=["PB4"])
            S.add("pe", mgb, reads=["wgb", "xTu"], writes=["PB5"])
            S.add("act", lambda e, T=T: e.activation(out=tg1[:, 0:T], in_=PB[4][:, 0:T], func=AF.Tanh, scale=0.5), reads=["PB4"], writes=["tg1"])
            S.add("act", lambda e, T=T: e.activation(out=tg2[:, 0:T], in_=PB[5][:, 0:T], func=AF.Tanh, scale=0.5), reads=["PB5"], writes=["tg2"])
            S.add("pool", lambda e, T=T: e.tensor_scalar(out=tg1[:, 0:T], in0=tg1[:, 0:T], scalar1=0.5, scalar2=0.5, op0=ALU.mult, op1=ALU.add),
                  reads=["tg1"], writes=["tg1"])
            S.add("pool", lambda e, T=T: e.tensor_scalar(out=tg2[:, 0:T], in0=tg2[:, 0:T], scalar1=0.5, scalar2=0.5, op0=ALU.mult, op1=ALU.add),
                  reads=["tg2"], writes=["tg2"])
            S.add("dve", lambda e, T=T: e.tensor_tensor(out=t1[:, 0:T], in0=PB[2][:, 0:T], in1=tg1[:, 0:T], op=ALU.mult),
                  reads=["PB2", "tg1"], writes=["t1"])
            S.add("dve", lambda e, T=T: e.tensor_tensor(out=t2[:, 0:T], in0=PB[3][:, 0:T], in1=tg2[:, 0:T], op=ALU.mult),
                  reads=["PB3", "tg2"], writes=["t2"])
            S.add("pool", lambda e, m=m, T=T: e.tensor_tensor(out=mgT[:, m, 0:T], in0=t1[:, 0:T], in1=t2[:, 0:T], op=ALU.add),
                  reads=["t1", "t2"], writes=["mgT"])
        S.add("sp", lambda e, u=u, T=T: e.dma_start(out=mg_d[u, :, :, 0:T], in_=mgT[:, :, 0:T]), reads=["mgT"], writes=["mgd%d" % u], dma="mgst")
    S.barrier()
    es_.close()


def phase_tail(nc, S, sb, L):
    w_o, w_pg, w_ple, ln_g, ln_b = L["w_o"], L["w_pg"], L["w_ple"], L["ln_g"], L["ln_b"]
    xown, xs, pown, pss, y_own, y_s, mg_d = L["xown"], L["xs"], L["pown"], L["pss"], L["y_own"], L["y_s"], L["mg_d"]
    PX, PB, ident, identb, neghalf = L["PX"], L["PB"], L["ident"], L["identb"], L["neghalf"]
    es_ = contextlib.ExitStack()

    def wload(name, src_ap, shape):
        t = sb(es_, shape, BF16, name)
        S.add("pool", lambda e: e.dma_start(out=t[:], in_=src_ap), writes=[name], dma="w_" + name)
        return t
    wo = wload("wo", w_o.rearrange("(c p) n -> p c n", p=128), [128, 8, D])
    wpg = wload("wpg", w_pg.rearrange("(c p) n -> p c n", p=128), [128, 8, D])
    wple = wload("wple", w_ple.rearrange("(c p) n -> p c n", p=128), [128, 2, D])
    lg = sb(es_, [128, D], F32, "lg")
    lb = sb(es_, [128, D], F32, "lb")
    S.add("sp", lambda e: e.dma_start(out=lg[:], in_=ln_g[None, :].to_broadcast([128, D])), writes=["lg"], dma="lg")
    S.add("sp", lambda e: e.dma_start(out=lb[:], in_=ln_b[None, :].to_broadcast([128, D])), writes=["lb"], dma="lb")
    mgT = [sb(es_, [128, 8, SBK], BF16, "mgTt") for _ in range(2)]
    xt = [sb(es_, [128, D], F32, "txt") for _ in range(2)]
    pt = [sb(es_, [128, PLE], F32, "tpt") for _ in range(2)]
    hp_ = [sb(es_, [128, D], F32, "hp") for _ in range(2)]
    hh_ = [sb(es_, [128, D], F32, "hh") for _ in range(2)]
    hb_ = [sb(es_, [128, D], BF16, "hb") for _ in range(2)]
    pendB = []
    hT = sb(es_, [128, 8, 128], BF16, "hTt")
    pT = sb(es_, [128, 2, 128], BF16, "pTt")
    sg = sb(es_, [128, D], F32, "sg")
    yt = [sb(es_, [128, D], F32, "yt") for _ in range(2)]
    st = sb(es_, [128, 2, 6], F32, "st")
    mv = sb(es_, [128, 2], F32, "mv")
    ve = sb(es_, [128, 1], F32, "ve")
    rstd = sb(es_, [128, 1], F32, "rstd")
    nmr = sb(es_, [128, 1], F32, "nmr")
    gi = 0
    for u in range(5):
        T = SBK if u < 4 else 128
        mj = u % 2
        S.add("sp", lambda e, u=u, mj=mj, T=T: e.dma_start(out=mgT[mj][:, :, 0:T], in_=mg_d[u, :, :, 0:T]),
              reads=["mgd%d" % u], writes=["mgTt%d" % mj], dma="mgld%d" % mj)
        for t in range(T // 128):
            j = gi % 2
            gi += 1
            hp, hh, hb = hp_[j], hh_[j], hb_[j]
            HP, HH, HB = "hp%d" % j, "hh%d" % j, "hb%d" % j
            xsrc = xown[u * SBK + t * 128:u * SBK + (t + 1) * 128, :] if u < 4 else xs[:, :]
            psrc = pown[u * SBK + t * 128:u * SBK + (t + 1) * 128, :] if u < 4 else pss[:, :]
            ydst = y_own[u * SBK + t * 128:u * SBK + (t + 1) * 128, :] if u < 4 else y_s[:, :]
            S.add("sp", lambda e, j=j, xsrc=xsrc: e.dma_start(out=xt[j][:], in_=xsrc), writes=["txt%d" % j], dma="txt%d" % j)
            S.add("sp", lambda e, j=j, psrc=psrc: e.dma_start(out=pt[j][:], in_=psrc), writes=["tpt%d" % j], dma="tpt%d" % j)

            def mho(e, t=t, mj=mj):
                for half in range(2):
                    for c in range(8):
                        i = e.matmul(PX[:, half * 512:(half + 1) * 512], mgT[mj][:, c, t * 128:(t + 1) * 128],
                                     wo[:, c, half * 512:(half + 1) * 512], start=(c == 0), stop=(c == 7))
                return i
            S.add("pe", mho, reads=["mgTt%d" % mj, "wo"], writes=["PX"])
            S.add("dve", lambda e, j=j, hp=hp: e.scalar_tensor_tensor(out=hp[:], in0=xt[j][:], scalar=ALPHA, in1=PX[:], op0=ALU.mult, op1=ALU.add),
                  reads=["txt%d" % j, "PX"], writes=[HP])

            def bst(e, hp=hp):
                for half in range(2):
                    i = e.bn_stats(out=st[:, half, :], in_=hp[:, half * 512:(half + 1) * 512])
                return i
            S.add("dve", bst, reads=[HP], writes=["st"])
            S.add("dve", lambda e: e.bn_aggr(out=mv[:], in_=st[:].rearrange("p a b -> p (a b)")), reads=["st"], writes=["mv"])
            S.add("dve", lambda e: e.tensor_scalar(out=ve[:], in0=mv[:, 1:2], scalar1=EPS, scalar2=None, op0=ALU.add), reads=["mv"], writes=["ve"])
            S.add("pool", lambda e: e.tensor_tensor(out=rstd[:], in0=ve[:], in1=neghalf[:], op=ALU.pow), reads=["ve", "neghalf"], writes=["rstd"])
            S.add("dve", lambda e: e.tensor_scalar(out=nmr[:], in0=mv[:, 0:1], scalar1=rstd[:, 0:1], scalar2=-1.0, op0=ALU.mult, op1=ALU.mult),
                  reads=["mv", "rstd"], writes=["nmr"])
            S.add("act", lambda e, hh=hh, hp=hp: e.activation(out=hh[:], in_=hp[:], func=AF.Identity, scale=rstd[:, 0:1], bias=nmr[:, 0:1]),
                  reads=[HP, "rstd", "nmr"], writes=[HH])
            S.add("dve", lambda e, hh=hh: e.tensor_tensor(out=hh[:], in0=hh[:], in1=lg[:], op=ALU.mult), reads=[HH, "lg"], writes=[HH])
            S.add("pool", lambda e, hh=hh: e.tensor_tensor(out=hh[:], in0=hh[:], in1=lb[:], op=ALU.add), reads=[HH, "lb"], writes=[HH])
            S.add("pool", lambda e, hh=hh, hb=hb: e.tensor_copy(out=hb[:], in_=hh[:]), reads=[HH], writes=[HB])
            def stageB(j=j, hh=hh, hb=hb, HH=HH, HB=HB, ydst=ydst):
                ptb = PB[0][:].bitcast(BF16)

                def trh(e):
                    for c in range(8):
                        i = e.transpose(ptb[:, c * 128:(c + 1) * 128], hb[:, c * 128:(c + 1) * 128], identb[:])
                    return i
                S.add("pe", trh, reads=[HB, "identb"], writes=["PB0"])
                S.add("act", lambda e, ptb=ptb: e.activation(out=hT[:].rearrange("p c t -> p (c t)"), in_=ptb[:, 0:1024], func=AF.Copy),
                      reads=["PB0"], writes=["hTt"])

                def trp(e, j=j):
                    for c in range(2):
                        i = e.transpose(PB[3][:, c * 128:(c + 1) * 128], pt[j][:, c * 128:(c + 1) * 128], ident[:])
                    return i
                S.add("pe", trp, reads=["tpt%d" % j, "ident"], writes=["PB3"])
                S.add("act", lambda e: e.activation(out=pT[:].rearrange("p c t -> p (c t)"), in_=PB[3][:, 0:256], func=AF.Copy),
                      reads=["PB3"], writes=["pTt"])
                for half in range(2):
                    def mg_(e, half=half):
                        for c in range(8):
                            i = e.matmul(PB[1 + half][:], hT[:, c, :], wpg[:, c, half * 512:(half + 1) * 512], start=(c == 0), stop=(c == 7))
                        return i
                    S.add("pe", mg_, reads=["hTt", "wpg"], writes=["PB%d" % (1 + half)])

                    def mp_(e, half=half):
                        for c in range(2):
                            i = e.matmul(PB[4 + half][:], pT[:, c, :], wple[:, c, half * 512:(half + 1) * 512], start=(c == 0), stop=(c == 1))
                        return i
                    S.add("pe", mp_, reads=["pTt", "wple"], writes=["PB%d" % (4 + half)])
                    hs = slice(half * 512, (half + 1) * 512)
                    S.add("act", lambda e, half=half, hs=hs: e.activation(out=sg[:, hs], in_=PB[1 + half][:], func=AF.Tanh, scale=0.5),
                          reads=["PB%d" % (1 + half)], writes=["sg%d" % half])
                    S.add("pool", lambda e, hs=hs: e.tensor_scalar(out=sg[:, hs], in0=sg[:, hs], scalar1=0.5, scalar2=0.5, op0=ALU.mult, op1=ALU.add),
                          reads=["sg%d" % half], writes=["sg%d" % half])
                    S.add("dve", lambda e, half=half, hs=hs: e.tensor_tensor(out=sg[:, hs], in0=PB[4 + half][:], in1=sg[:, hs], op=ALU.mult),
                          reads=["PB%d" % (4 + half), "sg%d" % half], writes=["sg%d" % half])
                S.add("pool", lambda e, j=j: e.tensor_tensor(out=yt[j][:], in0=sg[:], in1=hh[:], op=ALU.add),
                      reads=["sg0", "sg1", HH], writes=["yt%d" % j])
                S.add("sp", lambda e, j=j, ydst=ydst: e.dma_start(out=ydst, in_=yt[j][:]), reads=["yt%d" % j], writes=["OUTy%d" % j], dma="oy%d" % j)
            if pendB:
                pendB.pop()()
            pendB.append(stageB)
    pendB.pop()()
    S.barrier()
    es_.close()
```

```python
import contextlib
import numpy as np
import concourse.bass as bass
import concourse.mybir as mybir
from concourse.bass_utils import run_bass_kernel_spmd

F32 = mybir.dt.float32
BF16 = mybir.dt.bfloat16
I32 = mybir.dt.int32
U32 = mybir.dt.uint32
AF = mybir.ActivationFunctionType
ALU = mybir.AluOpType
AX = mybir.AxisListType

NCORES = 8
D = 1024
SEQ = 8192
H = 8
HD = 64
AW = 512
PLE = 256
PAST = 16384
NPAGE = 128
NPHYS = 5120
DEV_SKIP_DECODE = False
DEBUG = False
_last = None
SBK = 512
KMAX = [3, 7, 11, 15]
BIG = 30000.0
ALPHA = 2.0 ** 0.25
EPS = 1e-5
NSEQ = 4
QTOT = 2048 + 128


def own_sbs(r):
    return [r, 7 - r, 8 + r, 15 - r]


class Op:
    __slots__ = ("eng", "fn", "dma", "deps", "needed", "event")

    def __init__(self, eng, fn, dma):
        self.eng = eng
        self.fn = fn
        self.dma = dma
        self.deps = set()
        self.needed = False
        self.event = None


class Sched:
    ENG = ("pe", "act", "dve", "pool", "sp")

    def __init__(self, nc, es):
        self.nc = nc
        self.es = es
        self.h = {"pe": nc.tensor, "act": nc.scalar, "dve": nc.vector, "pool": nc.gpsimd, "sp": nc.sync}
        self.ops = {e: [] for e in self.ENG}
        self.res = {}
        self.bar = set()
        self.dma_since = []

    def add(self, eng, fn, reads=(), writes=(), dma=None):
        op = Op(eng, fn, dma)
        deps = set(self.bar)
        for r in reads:
            st = self.res.get(r)
            if st is not None and st[0] is not None:
                deps.add(st[0])
        for w in writes:
            st = self.res.get(w)
            if st is not None:
                if st[0] is not None:
                    deps.add(st[0])
                deps.update(st[1])
        if eng == "pe" and dma is None:
            deps = {d for d in deps if not (d.eng == "pe" and d.dma is None)}
        op.deps = deps
        for d in deps:
            d.needed = True
        for r in reads:
            self.res.setdefault(r, [None, []])[1].append(op)
        for w in writes:
            self.res[w] = [op, []]
        self.ops[eng].append(op)
        if dma is not None:
            self.dma_since.append(op)
        return op

    def barrier(self):
        b = set()
        for e in self.ENG:
            for op in reversed(self.ops[e]):
                if op.dma is None:
                    b.add(op)
                    break
        b.update(self.dma_since)
        self.dma_since = []
        for d in b:
            d.needed = True
        self.bar = b

    def emit(self, block):
        nc = self.nc
        esem = {e: self.es.enter_context(nc.semaphore("sem_" + e)) for e in self.ENG}
        dsem = {}
        dcnt = {}
        for e in self.ENG:
            cnt = 0
            for op in self.ops[e]:
                if op.dma is not None:
                    if op.dma not in dsem:
                        dsem[op.dma] = self.es.enter_context(nc.semaphore("dq_%d" % len(dsem)))
                        dcnt[op.dma] = 0
                    dcnt[op.dma] += 16
                    op.event = (dsem[op.dma], dcnt[op.dma], 16)
                elif op.needed:
                    cnt += 1
                    op.event = (esem[e], cnt, 1)

        def run(e, handle):
            waited = {}
            for op in self.ops[e]:
                need = {}
                for d in op.deps:
                    sem, val, _ = d.event
                    k = id(sem)
                    if k not in need or need[k][1] < val:
                        need[k] = (sem, val)
                for k, (sem, val) in need.items():
                    if waited.get(k, 0) < val:
                        handle.wait_ge(sem, val)
                        waited[k] = val
                ins = op.fn(handle)
                if op.event is not None:
                    ins.then_inc(op.event[0], op.event[2])

        @block.sync
        def _(eng):
            run("sp", eng)

        @block.gpsimd
        def _(eng):
            run("pool", eng)

        @block.tensor
        def _(eng):
            run("pe", eng)

        @block.vector
        def _(eng):
            run("dve", eng)

        @block.scalar
        def _(eng):
            run("act", eng)


def build_nc():
    nc = bass.Bass("TRN2", target_bir_lowering=False)

    def din(name, shape, dt=F32):
        return nc.dram_tensor(name, list(shape), dt, kind="ExternalInput").ap()

    def dout(name, shape, dt=F32):
        return nc.dram_tensor(name, list(shape), dt, kind="ExternalOutput").ap()

    xall = din("xall", [SEQ, D])
    xown = din("xown", [2048, D])
    xhalo = din("xhalo", [16, 16, D])
    pown = din("pown", [2048, PLE])
    xs = din("xs", [128, D])
    pss = din("pss", [128, PLE])
    cs_all = din("cs_all", [SEQ, 64])
    cs_own = din("cs_own", [2048, 64])
    cs_s = din("cs_s", [128, 64])
    pastneg = din("pastneg", [2048, 32])
    notown = din("notown", [2048, 32])
    ownA = din("ownA", [2048, 32])
    isAB = din("isAB", [2048, 2])
    oh_all = din("oh_all", [16, 32, SBK])
    oh_own = din("oh_own", [32, SBK])
    tri = din("tri", [128, 4 * SBK])
    corr = din("corr", [16, 4, 16])
    ident_d = din("ident", [128, 128])
    w_in = din("w_in", [D, 5120])
    w_mix = din("w_mix", [4, 128, 128])
    pscale = din("pscale", [512])
    w_po = din("w_po", [512, D])
    w_ao = din("w_ao", [512, D])
    w_o = din("w_o", [D, D])
    ln_g = din("ln_g", [D])
    ln_b = din("ln_b", [D])
    w_ple = din("w_ple", [PLE, D])
    w_pg = din("w_pg", [D, D])
    cache_k = din("cache_k", [NPHYS * 128 * 8, 64])
    cache_v = din("cache_v", [NPHYS * 128 * 8, 64])
    ptab = din("ptab", [NSEQ, NPAGE], I32)
    spool = din("spool", [NSEQ, 15, 512])
    pairm = din("pairm", [128, 64])
    iota128 = din("iota128", [8, 128])
    hsel = din("hsel", [8, 48])
    pofs = din("pofs", [128, 48])

    y_own = dout("y_own", [2048, D])
    k_own = dout("k_own", [2048, AW])
    v_own = dout("v_own", [2048, AW])
    poolp = dout("poolp", [15, 512])
    y_s = dout("y_s", [128, D])
    k_s = dout("k_s", [128, AW])
    v_s = dout("v_s", [128, AW])
    pool_s = dout("pool_s", [NSEQ, 15, 512])

    if DEBUG:
        dbg_att = dout("dbg_att", [128, 17, AW])
        dbg_mg = dout("dbg_mg", [5, 128, 8, SBK])
        dbg_qt = dout("dbg_qt", [96, H, QTOT])
    KT_d = nc.dram_tensor("KT_d", [H, 96, 20 * SBK], BF16, kind="Internal").ap()
    V_d = nc.dram_tensor("V_d", [H, 20, 128, 4, 66], BF16, kind="Internal").ap()
    mg_d = nc.dram_tensor("mg_d", [5, 128, 8, SBK], BF16, kind="Internal").ap()

    with contextlib.ExitStack() as es:
        S = Sched(nc, es)
        uid = [0]

        def sb(es_, shape, dt=F32, name=None):
            uid[0] += 1
            return es_.enter_context(nc.sbuf_tensor("%s_%d" % (name or "t", uid[0]), list(shape), dt))

        PX = es.enter_context(nc.psum_tensor("PX", [128, 1024], F32))
        PB = [es.enter_context(nc.psum_tensor("PB%d" % i, [128, 512], F32)) for i in range(6)]

        ident = sb(es, [128, 128], F32, "ident")
        identb = sb(es, [128, 128], BF16, "identb")
        att = sb(es, [128, 17, AW], BF16, "att")
        kmT = sb(es, [64, H, 32], F32, "kmT")
        kmTb = sb(es, [64, H, 32], BF16, "kmTb")
        kmx = sb(es, [128, H], F32, "kmx")
        kmxb = sb(es, [128, H], F32, "kmxb")
        trib = sb(es, [128, 4, SBK], BF16, "trib")
        ones_f = sb(es, [128, 128], F32, "ones_f")
        neghalf = sb(es, [128, 1], F32, "neghalf")
        qs_f = sb(es, [128, AW], F32, "qs_f")
        ks_f = sb(es, [128, AW], F32, "ks_f")
        vs_f = sb(es, [128, AW], F32, "vs_f")
        esQ = contextlib.ExitStack()
        QT = sb(esQ, [96, H, QTOT], BF16, "QT")

        S.add("sp", lambda e: e.dma_start(out=ident[:], in_=ident_d[:, :]), writes=["ident"], dma="c0")
        S.add("pool", lambda e: e.dma_start(out=identb[:], in_=ident_d[:, :]), writes=["identb"], dma="c1")
        S.add("pool", lambda e: e.dma_start(out=trib[:].rearrange("p a b -> p (a b)"), in_=tri[:, :]), writes=["trib"], dma="c2")
        S.add("dve", lambda e: e.memset(kmx[:], 0.0), writes=["kmx"])
        S.add("dve", lambda e: e.memset(ones_f[:], 1.0), writes=["ones_f"])
        S.add("dve", lambda e: e.memset(neghalf[:], -0.5), writes=["neghalf"])

        esA = contextlib.ExitStack()
        wq = sb(esA, [128, 8, 512], BF16, "wq")
        wk = sb(esA, [128, 8, 512], BF16, "wk")
        wv = sb(esA, [128, 8, 512], BF16, "wv")
        for nm, wt, c0 in (("wq", wq, 0), ("wk", wk, 512), ("wv", wv, 1024)):
            S.add("pool", lambda e, wt=wt, c0=c0: e.dma_start(
                out=wt[:], in_=w_in[:, c0:c0 + 512].rearrange("(c p) n -> p c n", p=128)),
                writes=[nm], dma="w" + nm)
        xt = [sb(esA, [128, D], F32, "xt") for _ in range(2)]
        cst = [sb(esA, [128, 64], F32, "cst") for _ in range(2)]
        xT = [sb(esA, [128, 8, 128], BF16, "xT") for _ in range(2)]
        r1 = sb(esA, [128, 256], F32, "r1")
        r2 = sb(esA, [128, 256], F32, "r2")
        r3 = sb(esA, [128, 256], F32, "r3")
        r4 = sb(esA, [128, 256], F32, "r4")
        kr = [sb(esA, [128, AW], F32, "kr") for _ in range(2)]
        vf = [sb(esA, [128, AW], F32, "vf") for _ in range(2)]
        qr = sb(esA, [128, AW], F32, "qr")
        kb = [sb(esA, [128, AW], BF16, "kb") for _ in range(2)]
        qb = sb(esA, [128, AW], BF16, "qb")
        sq = sb(esA, [128, AW], F32, "sq")
        ksq = sb(esA, [128, H], F32, "ksq")
        qsq = sb(esA, [128, H], F32, "qsq")
        KTs = [sb(esA, [96, H, SBK], BF16, "KTs") for _ in range(2)]
        Vs = [sb(esA, [128, H, 4, 66], BF16, "Vs") for _ in range(2)]
        for j in range(2):
            S.add("pool", lambda e, j=j: e.memset(Vs[j][:], 1.0), writes=["Vs%d" % j])
        sel_c = sb(esA, [128, 3, 32], F32, "sel_c")
        isab = sb(esA, [128, 2], F32, "isab")
        sc = sb(esA, [128, H, 32], F32, "sc")
        sel = sb(esA, [128, H, 32], F32, "sel")
        tmp3 = sb(esA, [128, H, 32], F32, "tmp3")
        mx8 = sb(esA, [128, H, 8], F32, "mx8")
        selA = sb(esA, [128, H], F32, "selA")
        tq = sb(esA, [128, H], F32, "tq")
        mbb = sb(esA, [128, H, 32], BF16, "mbb")

        PSQ, PST, PSS, PSM = ("PB4", "PB5", "PB4", "PB5")
        pq, pt_, ps_, pm_ = PB[4], PB[5], PB[4], PB[5]

        def rope(src_ps, src_key, cs_t, cs_key, dst, dst_key):
            s3 = src_ps[:].rearrange("p (h d) -> p h d", h=H)
            d3 = dst[:].rearrange("p (h d) -> p h d", h=H)
            cosb = cs_t[:, None, 0:32].to_broadcast([128, H, 32])
            sinb = cs_t[:, None, 32:64].to_broadcast([128, H, 32])
            a3 = r1[:].rearrange("p (h d) -> p h d", h=H)
            b3 = r2[:].rearrange("p (h d) -> p h d", h=H)
            c3 = r3[:].rearrange("p (h d) -> p h d", h=H)
            e3 = r4[:].rearrange("p (h d) -> p h d", h=H)
            S.add("dve", lambda e: e.tensor_tensor(out=a3, in0=s3[:, :, 0:32], in1=cosb, op=ALU.mult),
                  reads=[src_key, cs_key], writes=["r1"])
            S.add("dve", lambda e: e.tensor_tensor(out=b3, in0=s3[:, :, 32:64], in1=sinb, op=ALU.mult),
                  reads=[src_key, cs_key], writes=["r2"])
            S.add("dve", lambda e: e.tensor_tensor(out=c3, in0=s3[:, :, 32:64], in1=cosb, op=ALU.mult),
                  reads=[src_key, cs_key], writes=["r3"])
            S.add("dve", lambda e: e.tensor_tensor(out=e3, in0=s3[:, :, 0:32], in1=sinb, op=ALU.mult),
                  reads=[src_key, cs_key], writes=["r4"])
            S.add("pool", lambda e: e.tensor_tensor(out=d3[:, :, 0:32], in0=a3, in1=b3, op=ALU.subtract),
                  reads=["r1", "r2"], writes=[dst_key + "a"])
            S.add("pool", lambda e: e.tensor_tensor(out=d3[:, :, 32:64], in0=c3, in1=e3, op=ALU.add),
                  reads=["r3", "r4"], writes=[dst_key + "b"])

        gt = [0]
        late2 = [None]

        def load_tile(xsrc, cssrc):
            g = gt[0] % 2
            S.add("sp", lambda e: e.dma_start(out=xt[g][:], in_=xsrc), writes=["xt%d" % g], dma="xt%d" % g)
            S.add("sp", lambda e: e.dma_start(out=cst[g][:], in_=cssrc), writes=["cst%d" % g], dma="cst%d" % g)

        def proc_tile(t, kts_j, mode, qcol=None, orow=None, okv=None, after3=None):
            g = gt[0] % 2
            gt[0] += 1
            xk, ck, xTk = "xt%d" % g, "cst%d" % g, "xT%d" % g
            pk, pv = PB[g], PB[2 + g]
            PSK, PSV = "PB%d" % g, "PB%d" % (2 + g)

            def tr(e):
                for c in range(8):
                    i = e.transpose(PX[:, c * 128:(c + 1) * 128], xt[g][:, c * 128:(c + 1) * 128], ident[:])
                return i
            S.add("pe", tr, reads=[xk, "ident"], writes=["PX"])
            S.add("act", lambda e: e.activation(out=xT[g][:].rearrange("p c t -> p (c t)"), in_=PX[:], func=AF.Copy),
                  reads=["PX"], writes=[xTk])

            def proj(pbank, wt):
                def f(e):
                    for c in range(8):
                        i = e.matmul(pbank[:], xT[g][:, c, :], wt[:, c, :], start=(c == 0), stop=(c == 7))
                    return i
                return f
            S.add("pe", proj(pk, wk), reads=[xTk, "wk"], writes=[PSK])
            S.add("pe", proj(pv, wv), reads=[xTk, "wv"], writes=[PSV])
            if mode != "gen":
                S.add("pe", proj(pq, wq), reads=[xTk, "wq"], writes=[PSQ])
            kg = g
            krk = "kr%d" % kg

            def s2():
                rope(pk, PSK, cst[g], ck, kr[kg], krk)
                if mode != "smp":
                    vsl = Vs[kts_j][:, :, t, 0:64]
                    S.add("act", lambda e: e.activation(out=vsl, in_=pv[:].rearrange("p (h d) -> p h d", h=H), func=AF.Copy),
                          reads=[PSV], writes=["Vs%d" % kts_j])
                if mode != "gen":
                    S.add("act", lambda e: e.activation(out=vf[kg][:], in_=pv[:], func=AF.Copy),
                          reads=[PSV], writes=["vf%d" % kg])
                    S.add("sp", lambda e: e.dma_start(out=okv[0], in_=kr[kg][:]), reads=[krk + "a", krk + "b"], writes=["OUTk%d" % kg], dma="ok%d" % kg)
                    S.add("sp", lambda e: e.dma_start(out=okv[1], in_=vf[kg][:]), reads=["vf%d" % kg], writes=["OUTv%d" % kg], dma="ov%d" % kg)
                if mode == "smp":
                    return
                S.add("act", lambda e: e.activation(out=kb[kg][:], in_=kr[kg][:], func=AF.Copy), reads=[krk + "a", krk + "b"], writes=["kb%d" % kg])
                if late2[0] is not None:
                    late2[0]()
                    late2[0] = None
                S.add("pool", lambda e: e.tensor_tensor(out=sq[:], in0=kr[kg][:], in1=kr[kg][:], op=ALU.mult),
                      reads=[krk + "a", krk + "b"], writes=["sq"])

                def late():
                    S.add("dve", lambda e: e.tensor_reduce(out=ksq[:], in_=sq[:].rearrange("p (h d) -> p h d", h=H), axis=AX.X, op=ALU.add),
                          reads=["sq"], writes=["ksq"])
                    S.add("dve", lambda e: e.tensor_tensor(out=kmx[:], in0=kmx[:], in1=ksq[:], op=ALU.max),
                          reads=["ksq", "kmx"], writes=["kmx"])
                if mode == "gen":
                    late2[0] = late
                else:
                    late()

            def s3():
                if mode == "smp":
                    return
                ptb = pt_[:].bitcast(BF16)

                def trk(e):
                    for h in range(H):
                        i = e.transpose(ptb[0:64, h * 128:(h + 1) * 128], kb[kg][:, h * 64:(h + 1) * 64], identb[:])
                    return i
                S.add("pe", trk, reads=["kb%d" % kg, "identb"], writes=[PST])
                S.add("act", lambda e: e.activation(out=KTs[kts_j][0:64, :, t * 128:(t + 1) * 128],
                                                    in_=ptb[0:64, 0:1024].rearrange("p (h t) -> p h t", h=H), func=AF.Copy),
                      reads=[PST], writes=["KTs%d" % kts_j])
                if after3 is not None:
                    after3()
            if mode == "gen":
                return s2, s3
            s2()
            s3()
            return kg

        qcnt = [0]
        qsq2 = [qsq, sb(esA, [128, H], F32, "qsqb")]

        def q_tile(g_prev, qcol, selrow):
            qi = qcnt[0] % 2
            qcnt[0] += 1
            qsq = qsq2[qi]
            QSQ = "qsq%d" % qi
            rope(pq, PSQ, cst[g_prev], "cst%d" % g_prev, qr, "qr")
            S.add("pool", lambda e: e.tensor_copy(out=qb[:], in_=qr[:]), reads=["qra", "qrb"], writes=["qb"])
            S.add("pool", lambda e: e.tensor_tensor(out=sq[:], in0=qr[:], in1=qr[:], op=ALU.mult),
                  reads=["qra", "qrb"], writes=["sq"])
            S.add("dve", lambda e: e.tensor_reduce(out=qsq[:], in_=sq[:].rearrange("p (h d) -> p h d", h=H), axis=AX.X, op=ALU.add),
                  reads=["sq"], writes=[QSQ])
            ptb = pt_[:].bitcast(BF16)

            def trq(e):
                for h in range(H):
                    i = e.transpose(ptb[0:64, h * 128:(h + 1) * 128], qb[:, h * 64:(h + 1) * 64], identb[:])
                return i
            S.add("pe", trq, reads=["qb", "identb"], writes=[PST])
            S.add("act", lambda e: e.activation(out=QT[0:64, :, qcol:qcol + 128],
                                                in_=ptb[0:64, 0:1024].rearrange("p (h t) -> p h t", h=H), func=AF.Copy),
                  reads=[PST], writes=["QT"])
            if selrow is None:
                return None

            def selpart():
                S.add("sp", lambda e: e.dma_start(out=sel_c[:, 0, :], in_=pastneg[selrow:selrow + 128, :]), writes=["selc0"], dma="selc0")
                S.add("sp", lambda e: e.dma_start(out=sel_c[:, 1, :], in_=notown[selrow:selrow + 128, :]), writes=["selc1"], dma="selc1")
                S.add("sp", lambda e: e.dma_start(out=sel_c[:, 2, :], in_=ownA[selrow:selrow + 128, :]), writes=["selc2"], dma="selc2")
                S.add("sp", lambda e: e.dma_start(out=isab[:], in_=isAB[selrow:selrow + 128, :]), writes=["isab"], dma="isab")

                def scm(e):
                    for h in range(H):
                        i = e.matmul(ps_[:, h * 32:(h + 1) * 32], QT[0:64, h, qcol:qcol + 128], kmTb[:, h, :], start=True, stop=True)
                    return i
                S.add("pe", scm, reads=["QT", "kmTb"], writes=[PSS])
                S.add("dve", lambda e: e.tensor_tensor(out=sc[:], in0=ps_[:, 0:256].rearrange("p (h j) -> p h j", h=H),
                                                       in1=sel_c[:, 0:1, :].to_broadcast([128, H, 32]), op=ALU.add),
                      reads=[PSS, "selc0"], writes=["sc"])

                def mx(e):
                    for h in range(H):
                        i = e.max(out=mx8[:, h, :], in_=sc[:, h, :])
                    return i
                S.add("dve", mx, reads=["sc"], writes=["mx8"])
                S.add("dve", lambda e: e.tensor_tensor(out=sel[:], in0=sc[:], in1=mx8[:, :, 2:3].to_broadcast([128, H, 32]), op=ALU.is_ge),
                      reads=["sc", "mx8"], writes=["sel"])
                S.add("dve", lambda e: e.tensor_scalar(out=tmp3[:], in0=sc[:], scalar1=-1e29, scalar2=None, op0=ALU.is_gt),
                      reads=["sc"], writes=["tmp3"])
                S.add("dve", lambda e: e.tensor_tensor(out=sel[:], in0=sel[:], in1=tmp3[:], op=ALU.mult),
                      reads=["sel", "tmp3"], writes=["sel"])
                S.add("dve", lambda e: e.tensor_tensor(out=tmp3[:], in0=sel[:], in1=sel_c[:, 2:3, :].to_broadcast([128, H, 32]), op=ALU.mult),
                      reads=["sel", "selc2"], writes=["tmp3"])
                S.add("dve", lambda e: e.tensor_reduce(out=selA[:], in_=tmp3[:], axis=AX.X, op=ALU.add),
                      reads=["tmp3"], writes=["selA"])
                S.add("dve", lambda e: e.tensor_tensor(out=sel[:], in0=sel[:], in1=sel_c[:, 1:2, :].to_broadcast([128, H, 32]), op=ALU.mult),
                      reads=["sel", "selc1"], writes=["sel"])
                S.add("dve", lambda e: e.tensor_scalar(out=sel[:, :, 30], in0=selA[:], scalar1=isab[:, 0:1], scalar2=None, op0=ALU.add),
                      reads=["selA", "isab", "sel"], writes=["sel"])
                S.add("dve", lambda e: e.tensor_copy(out=sel[:, :, 31], in_=isab[:, 1:2].to_broadcast([128, H])),
                      reads=["isab", "sel"], writes=["sel"])
                S.add("dve", lambda e: e.tensor_tensor(out=tq[:], in0=qsq[:], in1=kmxb[:], op=ALU.add),
                      reads=[QSQ, "kmxb"], writes=["tq"])
                S.add("dve", lambda e: e.tensor_scalar(out=tq[:], in0=tq[:], scalar1=-0.5, scalar2=BIG, op0=ALU.mult, op1=ALU.add),
                      reads=["tq"], writes=["tq"])
                S.add("dve", lambda e: e.tensor_tensor(out=tmp3[:], in0=sel[:], in1=tq[:, :, None].to_broadcast([128, H, 32]), op=ALU.mult),
                      reads=["sel", "tq"], writes=["tmp3"])
                S.add("dve", lambda e: e.tensor_scalar(out=mbb[:], in0=tmp3[:], scalar1=-BIG, scalar2=None, op0=ALU.add),
                      reads=["tmp3"], writes=["mbb"])

                def trm(e):
                    for h in range(H):
                        i = e.transpose(ptb[64:96, h * 128:(h + 1) * 128], mbb[:, h, :], identb[:])
                    return i
                S.add("pe", trm, reads=["mbb", "identb"], writes=[PST])
                S.add("act", lambda e: e.activation(out=QT[64:96, :, qcol:qcol + 128],
                                                    in_=ptb[64:96, 0:1024].rearrange("p (h t) -> p h t", h=H), func=AF.Copy),
                      reads=[PST], writes=["QT"])

            return selpart

        def finish_block(kts_j, slot, oh_src, gen_s):
            S.add("pool", lambda e: e.dma_start(out=KTs[kts_j][64:96, :, :], in_=oh_src[:, None, :].to_broadcast([32, H, SBK])),
                  writes=["KTs%d" % kts_j], dma="oh%d" % kts_j)
            if gen_s is not None:
                S.add("dve", lambda e: e.tensor_reduce(out=kmT[:, :, 2 * gen_s:2 * gen_s + 2],
                                                       in_=KTs[kts_j][0:64, :, :].rearrange("p h (b k) -> p h b k", b=2),
                                                       axis=AX.X, op=ALU.add),
                      reads=["KTs%d" % kts_j], writes=["kmT"])
            S.add("sp", lambda e: e.dma_start(out=KT_d[:, :, slot * SBK:(slot + 1) * SBK].rearrange("h r k -> r h k"), in_=KTs[kts_j][:]),
                  reads=["KTs%d" % kts_j], writes=["KTd%d" % slot], dma="kst%d" % kts_j)
            S.add("sp", lambda e: e.dma_start(out=V_d[:, slot, :, :, :].rearrange("h p t e -> p h (t e)"), in_=Vs[kts_j][:].rearrange("p h t e -> p h (t e)")),
                  reads=["Vs%d" % kts_j], writes=["Vd%d" % slot], dma="vst%d" % kts_j)

        blk = 16
        tiles = [(s_, t_) for s_ in range(16) for t_ in range(4)]
        pend = []

        def ld(n_):
            s_, t_ = tiles[n_]
            r0_ = s_ * SBK + t_ * 128
            load_tile(xall[r0_:r0_ + 128, :], cs_all[r0_:r0_ + 128, :])
        ld(0)
        for n_, (s_, t_) in enumerate(tiles):
            j_ = s_ % 2
            fin = (lambda j_=j_, s_=s_: finish_block(j_, s_, oh_all[s_], s_)) if t_ == 3 else None
            st = proc_tile(t_, j_, "gen", after3=fin)
            if len(pend) >= 1:
                pend[-1][0]()
            if n_ + 1 < len(tiles):
                ld(n_ + 1)
            if len(pend) >= 2:
                pend[-2][1]()
            pend.append(st)
        pend[-1][0]()
        pend[-2][1]()
        pend[-1][1]()
        if late2[0] is not None:
            late2[0]()
            late2[0] = None

        pmx = pm_
        S.add("pe", lambda e: e.transpose(pmx[0:8, 0:128], kmx[:], ident[:]), reads=["kmx", "ident"], writes=[PSM])
        kmr = sb(esA, [8, 1], F32, "kmr")
        kmd = sb(esA, [8, 8], F32, "kmd")
        S.add("dve", lambda e: e.tensor_reduce(out=kmr[:], in_=pmx[0:8, 0:128], axis=AX.X, op=ALU.max), reads=[PSM], writes=["kmr"])
        S.add("dve", lambda e: e.tensor_scalar(out=kmd[:], in0=ident[0:8, 0:8], scalar1=kmr[:, 0:1], scalar2=None, op0=ALU.mult),
              reads=["kmr", "ident"], writes=["kmd"])
        S.add("pe", lambda e: e.matmul(pmx[:, 0:8], ones_f[0:8, :], kmd[:], start=True, stop=True), reads=["kmd", "ones_f"], writes=[PSM])
        S.add("dve", lambda e: e.tensor_copy(out=kmxb[:], in_=pmx[:, 0:8]), reads=[PSM], writes=["kmxb"])
        S.add("pool", lambda e: e.tensor_copy(out=kmTb[:], in_=kmT[:]), reads=["kmT"], writes=["kmTb"])

        pend_sel = [None]
        for i in range(4):
            j = blk % 2
            blk += 1
            for t in range(4):
                r0 = i * SBK + t * 128
                load_tile(xown[r0:r0 + 128, :], cs_own[r0:r0 + 128, :])
                g_prev = gt[0] % 2
                proc_tile(t, j, "own", okv=(k_own[r0:r0 + 128, :], v_own[r0:r0 + 128, :]))
                sp_ = q_tile(g_prev, r0, r0)
                if pend_sel[0] is not None:
                    pend_sel[0]()
                pend_sel[0] = sp_
            finish_block(j, 16 + i, oh_own, None)
        pend_sel[0]()
        load_tile(xs[:, :], cs_s[:, :])
        g_prev = gt[0] % 2
        kg_s = proc_tile(0, 0, "smp", okv=(k_s[:, :], v_s[:, :]))
        q_tile(g_prev, 2048, None)
        S.add("pool", lambda e: e.tensor_copy(out=qs_f[:], in_=qr[:]), reads=["qra", "qrb"], writes=["qs_f"])
        S.add("pool", lambda e: e.tensor_copy(out=ks_f[:], in_=kr[kg_s][:]), reads=["kr%da" % kg_s, "kr%db" % kg_s], writes=["ks_f"])
        S.add("pool", lambda e: e.tensor_copy(out=vs_f[:], in_=vf[kg_s][:]), reads=["vf%d" % kg_s], writes=["vs_f"])
        S.barrier()
        esA.close()

        esP = contextlib.ExitStack()
        ck16 = cache_k.rearrange("(n r) d -> n (r d)", r=64)
        ptT_i = sb(esP, [128, NSEQ], I32, "ptT_i")
        ptf = sb(esP, [128, NSEQ], F32, "ptf")
        idxc = sb(esP, [128, NSEQ, 16], I32, "idxc")
        pgs4 = sb(esP, [128, NSEQ, AW], F32, "pgs4")
        pg1 = [sb(esP, [128, AW], F32, "pg1") for _ in range(2)]
        chb = [sb(esP, [128, 4096], F32, "ch") for _ in range(2)]
        if not DEV_SKIP_DECODE:
            S.add("sp", lambda e: e.dma_start(out=ptT_i[:], in_=ptab.rearrange("s p -> p s"), allow_slow_non_contiguous=True), writes=["ptT_i"], dma="d_pt")
            S.add("dve", lambda e: e.tensor_copy(out=ptf[:], in_=ptT_i[:]), reads=["ptT_i"], writes=["ptf"])
            for c in range(16):
                S.add("dve", lambda e, c=c: e.tensor_scalar(out=idxc[:, :, c], in0=ptf[:], scalar1=16.0, scalar2=float(c), op0=ALU.mult, op1=ALU.add),
                      reads=["ptf"], writes=["idxc"])
            S.add("pool", lambda e: e.memset(pgs4[:], 0.0), writes=["pgs4"])

        def ps_dma(k):
            s_, c_ = k // 16, k % 16
            j = k % 2
            S.add("pool", lambda e: e.indirect_dma_start(
                out=chb[j][:], out_offset=None, in_=ck16[:, :], in_offset=bass.IndirectOffsetOnAxis(ap=idxc[:, s_, c_:c_ + 1], axis=0)),
                reads=["idxc"], writes=["ch%d" % j], dma="d_ch%d" % j)

        def ps_red(k):
            return

        def ps_add(k):
            s_ = k // 16
            j = k % 2
            for r_ in range(8):
                S.add("pool", lambda e, r_=r_: e.tensor_tensor(out=pgs4[:, s_, :], in0=pgs4[:, s_, :], in1=chb[j][:, r_ * 512:(r_ + 1) * 512], op=ALU.add),
                      reads=["pgs4", "ch%d" % j], writes=["pgs4"])

        def ps_step(k):
            if DEV_SKIP_DECODE:
                return
            if 0 <= k < 64:
                ps_dma(k)
            if 0 <= k - 1 < 64:
                ps_red(k - 1)
            if 0 <= k - 1 < 64:
                ps_add(k - 1)

        esB = contextlib.ExitStack()
        NKB = 4
        Kc = [sb(esB, [96, SBK], BF16, "Kc") for _ in range(NKB)]
        Vc = [sb(esB, [128, 4, 66], BF16, "Vc") for _ in range(NKB)]
        Pt = [sb(esB, [128, SBK], BF16, "Pt") for _ in range(3)]
        rs = sb(esB, [128, 4], F32, "rs")
        sbanks = [(PB[0], "PB0"), (PB[1], "PB1"), (PB[2], "PB2")]
        accs = [(PX[:, 0:512], "PXa"), (PX[:, 512:1024], "PXb")]
        groups = []
        hi = 0
        for i in range(4):
            slots = list(range(KMAX[i])) + [16 + i]
            for h in range(H):
                for si, slot in enumerate(slots):
                    groups.append((i, h, si, slot, len(slots), hi))
                hi += 1

        def gload(gi_):
            i, h, si, slot, ns, hidx = groups[gi_]
            b = gi_ % NKB
            S.add("sp", lambda e: e.dma_start(out=Kc[b][:], in_=KT_d[h, :, slot * SBK:(slot + 1) * SBK]),
                  reads=["KTd%d" % slot], writes=["Kc%d" % b], dma="kc%d" % b)
            S.add("sp", lambda e: e.dma_start(out=Vc[b][:].rearrange("p t e -> p (t e)"), in_=V_d[h, slot, :, :, :].rearrange("p t e -> p (t e)")),
                  reads=["Vd%d" % slot], writes=["Vc%d" % b], dma="vc%d" % b)

        backs = []

        def front(gi_, kt, n_):
            i, h, si, slot, ns, hidx = groups[gi_]
            b = gi_ % NKB
            sbk, sbkk = sbanks[n_ % 3]
            p_i = n_ % 3
            acc, acck = accs[hidx % 2]
            acc3 = acc.rearrange("p (q e) -> p q e", q=4)
            S.add("pe", lambda e: e.matmul(sbk[:], Kc[b][:, kt * 128:(kt + 1) * 128], QT[:, h, i * SBK:(i + 1) * SBK], start=True, stop=True),
                  reads=["Kc%d" % b, "QT"], writes=[sbkk])
            S.add("act", lambda e: e.activation(out=Pt[p_i][:], in_=sbk[:], func=AF.Exp, scale=0.125),
                  reads=[sbkk], writes=["Pt%d" % p_i])
            if slot >= 16:
                S.add("dve", lambda e: e.tensor_tensor(out=Pt[p_i][:], in0=Pt[p_i][:], in1=trib[:, kt, :], op=ALU.mult),
                      reads=["Pt%d" % p_i, "trib"], writes=["Pt%d" % p_i])
            first = (si == 0 and kt == 0)
            last = (si == ns - 1 and kt == 3)

            def back():
                def pv_(e):
                    for qt in range(4):
                        i_ = e.matmul(acc3[:, qt, 0:65], Pt[p_i][:, qt * 128:(qt + 1) * 128], Vc[b][:, kt, 0:65],
                                      start=(first and qt == 0), stop=last, skip_group_check=True)
                    return i_
                S.add("pe", pv_, reads=["Pt%d" % p_i, "Vc%d" % b], writes=[acck])
                if last:
                    S.add("dve", lambda e: e.reciprocal(out=rs[:], in_=acc3[:, :, 64]), reads=[acck], writes=["rs"])
                    for qt in range(4):
                        S.add("dve", lambda e, qt=qt: e.tensor_scalar(
                            out=att[:, i * 4 + qt, h * 64:(h + 1) * 64], in0=acc3[:, qt, 0:64], scalar1=rs[:, qt:qt + 1], scalar2=None, op0=ALU.mult),
                            reads=[acck, "rs"], writes=["att"])
            return back

        gload(0)
        gload(1)
        n_ = 0
        for gi_ in range(len(groups)):
            if gi_ + 2 < len(groups):
                gload(gi_ + 2)
            if gi_ % 4 == 0:
                ps_step(gi_ // 4)
            for kt in range(4):
                backs.append(front(gi_, kt, n_))
                if n_ >= 2:
                    backs[n_ - 2]()
                n_ += 1
        backs[n_ - 2]()
        backs[n_ - 1]()
        S.barrier()
        esB.close()

        if DEBUG:
            S.add("pool", lambda e: e.dma_start(out=dbg_att[:, :, :], in_=att[:]), reads=["att"], writes=["OUTdbga"], dma="dbga")
            for hh_ in range(H):
                S.add("pool", lambda e, hh_=hh_: e.dma_start(out=dbg_qt[:, hh_, :], in_=QT[:, hh_, :]), reads=["QT"], writes=["OUTdbgq%d" % hh_], dma="dbgq")
        esD = contextlib.ExitStack()
        if not DEV_SKIP_DECODE:
            decode_attention(nc, S, esD, sb, locals())
        else:
            S.add("pool", lambda e: e.memset(att[:, 16, :], 0.0), writes=["att"])
        S.barrier()
        esD.close()
        esP.close()
        esQ.close()

        phase_front(nc, S, sb, locals())
        S.barrier()
        phase_tail(nc, S, sb, locals())

        if DEBUG:
            for u_ in range(5):
                S.add("pool", lambda e, u_=u_: e.dma_start(out=dbg_mg[u_], in_=mg_d[u_]), reads=["mgd%d" % u_], writes=["OUTdbgm%d" % u_], dma="dbgm")
        outs = [k for k in S.res if k.startswith("OUT")]
        S.add("sp", lambda e: e.nop(), reads=outs)
        with nc.Block() as block:
            S.emit(block)
    return nc


def decode_attention(nc, S, es_, sb, L):
    cache_k, cache_v, ptab = L["cache_k"], L["cache_v"], L["ptab"]
    pairm, iota128, hsel, pofs = L["pairm"], L["iota128"], L["hsel"], L["pofs"]
    PX, PB, ident, ones_f, att = L["PX"], L["PB"], L["ident"], L["ones_f"], L["att"]
    qs_f, ks_f, vs_f = L["qs_f"], L["ks_f"], L["vs_f"]
    pgs4 = L["pgs4"]
    selS = sb(es_, [128, NSEQ, 128], F32, "selS")
    pair_sb = sb(es_, [128, 64], F32, "pair_sb")
    iota8 = sb(es_, [8, 128], F32, "iota8")
    hsel_sb = sb(es_, [8, 48], F32, "hsel_sb")
    pofs_sb = sb(es_, [128, 48], F32, "pofs_sb")
    ptr_i = sb(es_, [8, 128], I32, "ptr_i")
    ptr_f = sb(es_, [8, 128], F32, "ptr_f")
    qbc = sb(es_, [128, AW], F32, "qbc")
    vbc = sb(es_, [128, AW], F32, "vbc")
    tmpq = sb(es_, [128, AW], F32, "tmpq")
    spg = sb(es_, [128, H], F32, "spg")
    ssa = sb(es_, [128, H], F32, "ssa")
    sbc = sb(es_, [128, H], F32, "sbc")
    bsc = sb(es_, [8, 64], F32, "bsc")
    mx8 = sb(es_, [8, 8], F32, "dmx8")
    ix8 = sb(es_, [8, 8], U32, "dix8")
    ixf = sb(es_, [8, 8], F32, "dixf")
    lp = sb(es_, [8, 6], F32, "lp")
    eqt = sb(es_, [8, 128], F32, "eqt")
    phys = sb(es_, [8, 6], F32, "phys")
    physd = sb(es_, [8, 48], F32, "physd")
    gidx = sb(es_, [128, 48], I32, "gidx")
    Kg = sb(es_, [128, H, 6, 64], F32, "Kg")
    Vg = sb(es_, [128, H, 6, 65], F32, "Vg")
    tmpk = sb(es_, [128, H, 6, 64], F32, "tmpk")
    sk = sb(es_, [128, 48], F32, "sk")
    m48 = sb(es_, [48, 1], F32, "m48")
    mh = sb(es_, [1, H], F32, "mh")
    mb = sb(es_, [128, H], F32, "mb")
    Pk = sb(es_, [128, 48], F32, "Pk")
    pself = sb(es_, [128, H], F32, "pself")
    orow = sb(es_, [1, H, 65], F32, "orow")
    den = sb(es_, [1, H], F32, "den")
    arow = sb(es_, [1, AW], F32, "arow")

    S.add("pool", lambda e: e.memset(att[:, 16, :], 0.0), writes=["att"])
    S.add("sp", lambda e: e.dma_start(out=pair_sb[:], in_=pairm[:, :]), writes=["pair_sb"], dma="d_c1")
    S.add("sp", lambda e: e.dma_start(out=iota8[:], in_=iota128[:, :]), writes=["iota8"], dma="d_c2")
    S.add("sp", lambda e: e.dma_start(out=hsel_sb[:], in_=hsel[:, :]), writes=["hsel_sb"], dma="d_c3")
    S.add("sp", lambda e: e.dma_start(out=pofs_sb[:], in_=pofs[:, :]), writes=["pofs_sb"], dma="d_c4")
    for s in range(NSEQ):
        S.add("dve", lambda e, s=s: e.tensor_copy(out=selS[:, s, :], in_=ident[:, s:s + 1].to_broadcast([128, 128])), reads=["ident"], writes=["selS"])
    S.add("pool", lambda e: e.memset(Vg[:], 1.0), writes=["Vg"])
    S.add("dve", lambda e: e.tensor_tensor(out=tmpq[:], in0=qs_f[:], in1=ks_f[:], op=ALU.mult), reads=["qs_f", "ks_f"], writes=["tmpq"])
    S.add("dve", lambda e: e.tensor_reduce(out=ssa[:], in_=tmpq[:].rearrange("p (h d) -> p h d", h=H), axis=AX.X, op=ALU.add), reads=["tmpq"], writes=["ssa"])
    gi = 0
    for s in range(NSEQ):
        S.add("pe", lambda e, s=s: e.matmul(PB[0][:], selS[:, s, :], qs_f[:], start=True, stop=True), reads=["selS", "qs_f"], writes=["PB0"])
        S.add("act", lambda e: e.activation(out=qbc[:], in_=PB[0][:], func=AF.Copy), reads=["PB0"], writes=["qbc"])
        S.add("pe", lambda e, s=s: e.matmul(PB[1][:], selS[:, s, :], vs_f[:], start=True, stop=True), reads=["selS", "vs_f"], writes=["PB1"])
        S.add("act", lambda e: e.activation(out=vbc[:], in_=PB[1][:], func=AF.Copy), reads=["PB1"], writes=["vbc"])
        S.add("pe", lambda e, s=s: e.matmul(PB[2][:, 0:H], selS[:, s, :], ssa[:], start=True, stop=True), reads=["selS", "ssa"], writes=["PB2"])
        S.add("act", lambda e: e.activation(out=sbc[:], in_=PB[2][:, 0:H], func=AF.Copy), reads=["PB2"], writes=["sbc"])
        S.add("dve", lambda e, s=s: e.tensor_tensor(out=tmpq[:], in0=pgs4[:, s, :], in1=qbc[:], op=ALU.mult), reads=["pgs4", "qbc"], writes=["tmpq"])
        S.add("dve", lambda e: e.tensor_reduce(out=spg[:], in_=tmpq[:].rearrange("p (h d) -> p h d", h=H), axis=AX.X, op=ALU.add), reads=["tmpq"], writes=["spg"])
        S.add("pe", lambda e: e.matmul(PB[3][0:8, 0:64], spg[:], pair_sb[:], start=True, stop=True), reads=["spg", "pair_sb"], writes=["PB3"])
        S.add("dve", lambda e: e.tensor_copy(out=bsc[:], in_=PB[3][0:8, 0:64]), reads=["PB3"], writes=["bsc"])
        S.add("dve", lambda e: e.max(out=mx8[:], in_=bsc[:]), reads=["bsc"], writes=["dmx8"])
        S.add("dve", lambda e: e.max_index(out=ix8[:], in_max=mx8[:], in_values=bsc[:]), reads=["dmx8", "bsc"], writes=["dix8"])
        S.add("dve", lambda e: e.tensor_copy(out=ixf[:], in_=ix8[:]), reads=["dix8"], writes=["dixf"])
        lp3 = lp[:].rearrange("p (k e) -> p k e", e=2)
        for e2 in range(2):
            S.add("dve", lambda e, e2=e2: e.tensor_scalar(out=lp3[:, :, e2], in0=ixf[:, 0:3], scalar1=2.0, scalar2=float(e2), op0=ALU.mult, op1=ALU.add),
                  reads=["dixf"], writes=["lp"])
        S.add("sp", lambda e, s=s: e.dma_start(out=ptr_i[:], in_=ptab[s:s + 1, :].to_broadcast([8, NPAGE])), writes=["ptr_i"], dma="d_ptr")
        S.add("dve", lambda e: e.tensor_copy(out=ptr_f[:], in_=ptr_i[:]), reads=["ptr_i"], writes=["ptr_f"])
        for sl in range(6):
            S.add("dve", lambda e, sl=sl: e.scalar_tensor_tensor(out=eqt[:], in0=iota8[:], scalar=lp[:, sl:sl + 1], in1=ptr_f[:],
                                                                 op0=ALU.is_equal, op1=ALU.mult, accum_out=phys[:, sl:sl + 1]),
                  reads=["iota8", "lp", "ptr_f"], writes=["eqt", "phys"])
        S.add("dve", lambda e: e.tensor_tensor(out=physd[:].rearrange("p (h k) -> p h k", h=H), in0=hsel_sb[:].rearrange("p (h k) -> p h k", h=H),
                                               in1=phys[:, None, :].to_broadcast([8, H, 6]), op=ALU.mult),
              reads=["hsel_sb", "phys"], writes=["physd"])
        S.add("pe", lambda e: e.matmul(PB[4][:, 0:48], ones_f[0:8, :], physd[:], start=True, stop=True), reads=["ones_f", "physd"], writes=["PB4"])
        S.add("dve", lambda e: e.scalar_tensor_tensor(out=gidx[:], in0=PB[4][:, 0:48], scalar=1024.0, in1=pofs_sb[:], op0=ALU.mult, op1=ALU.add),
              reads=["PB4", "pofs_sb"], writes=["gidx"])
        for h in range(H):
            for sl in range(6):
                col = h * 6 + sl
                S.add("pool", lambda e, h=h, sl=sl, col=col: e.indirect_dma_start(
                    out=Kg[:, h, sl, :], out_offset=None, in_=cache_k[:, :], in_offset=bass.IndirectOffsetOnAxis(ap=gidx[:, col:col + 1], axis=0)),
                    reads=["gidx"], writes=["Kg%d" % col], dma="d_kg%d" % (col % 8))
                S.add("pool", lambda e, h=h, sl=sl, col=col: e.indirect_dma_start(
                    out=Vg[:, h, sl, 0:64], out_offset=None, in_=cache_v[:, :], in_offset=bass.IndirectOffsetOnAxis(ap=gidx[:, col:col + 1], axis=0)),
                    reads=["gidx"], writes=["Vg%d" % col], dma="d_vg%d" % (col % 8))
        kgk = ["Kg%d" % c_ for c_ in range(48)]
        vgk = ["Vg%d" % c_ for c_ in range(48)]
        S.add("dve", lambda e: e.tensor_tensor(out=tmpk[:], in0=Kg[:], in1=qbc[:].rearrange("p (h d) -> p h d", h=H)[:, :, None, :].to_broadcast([128, H, 6, 64]), op=ALU.mult),
              reads=kgk + ["qbc"], writes=["tmpk"])
        S.add("dve", lambda e: e.tensor_reduce(out=sk[:], in_=tmpk[:].rearrange("p h k d -> p (h k) d"), axis=AX.X, op=ALU.add), reads=["tmpk"], writes=["sk"])
        S.add("pe", lambda e: e.transpose(PB[5][0:48, 0:128], sk[:], ident[:]), reads=["sk", "ident"], writes=["PB5"])
        S.add("dve", lambda e: e.tensor_reduce(out=m48[:], in_=PB[5][0:48, 0:128], axis=AX.X, op=ALU.max), reads=["PB5"], writes=["m48"])
        S.add("pe", lambda e: e.transpose(PB[5][0:1, 0:48], m48[:], ident[0:48, 0:48]), reads=["m48", "ident"], writes=["PB5"])
        S.add("dve", lambda e: e.tensor_reduce(out=mh[:], in_=PB[5][0:1, 0:48].rearrange("p (h k) -> p h k", h=H), axis=AX.X, op=ALU.max), reads=["PB5"], writes=["mh"])
        S.add("dve", lambda e: e.tensor_tensor(out=mh[:], in0=mh[:], in1=sbc[0:1, :], op=ALU.max), reads=["mh", "sbc"], writes=["mh"])
        S.add("pe", lambda e: e.matmul(PB[5][:, 0:H], ones_f[0:1, :], mh[:], start=True, stop=True), reads=["ones_f", "mh"], writes=["PB5"])
        S.add("dve", lambda e: e.tensor_copy(out=mb[:], in_=PB[5][:, 0:H]), reads=["PB5"], writes=["mb"])
        S.add("dve", lambda e: e.tensor_tensor(out=sk[:].rearrange("p (h k) -> p h k", h=H), in0=sk[:].rearrange("p (h k) -> p h k", h=H),
                                               in1=mb[:, :, None].to_broadcast([128, H, 6]), op=ALU.subtract), reads=["sk", "mb"], writes=["sk"])
        S.add("act", lambda e: e.activation(out=Pk[:], in_=sk[:], func=AF.Exp, scale=0.125), reads=["sk"], writes=["Pk"])
        S.add("dve", lambda e: e.tensor_tensor(out=pself[:], in0=sbc[:], in1=mb[:], op=ALU.subtract), reads=["sbc", "mb"], writes=["pself"])
        S.add("act", lambda e: e.activation(out=pself[:], in_=pself[:], func=AF.Exp, scale=0.125), reads=["pself"], writes=["pself"])

        def pv(e):
            for h in range(H):
                for sl in range(6):
                    i = e.matmul(PX[0:1, h * 128:h * 128 + 65], Pk[:, h * 6 + sl:h * 6 + sl + 1], Vg[:, h, sl, :], start=(sl == 0), stop=(sl == 5),
                                 skip_group_check=True)
            return i
        S.add("pe", pv, reads=["Pk"] + vgk, writes=["PX"])
        o3 = PX[0:1, :].rearrange("p (h e) -> p h e", h=H)
        S.add("dve", lambda e: e.tensor_tensor(out=orow[:, :, 0:64], in0=vbc[0:1, :].rearrange("p (h d) -> p h d", h=H),
                                               in1=pself[0:1, :, None].to_broadcast([1, H, 64]), op=ALU.mult), reads=["vbc", "pself"], writes=["orow"])
        S.add("dve", lambda e: e.tensor_tensor(out=orow[:, :, 0:64], in0=orow[:, :, 0:64], in1=o3[:, :, 0:64], op=ALU.add), reads=["orow", "PX"], writes=["orow"])
        S.add("dve", lambda e: e.tensor_tensor(out=den[:], in0=pself[0:1, :], in1=o3[:, :, 64], op=ALU.add), reads=["pself", "PX"], writes=["den"])
        S.add("dve", lambda e: e.reciprocal(out=den[:], in_=den[:]), reads=["den"], writes=["den"])
        S.add("dve", lambda e: e.tensor_tensor(out=arow[:].rearrange("p (h d) -> p h d", h=H), in0=orow[:, :, 0:64],
                                               in1=den[:, :, None].to_broadcast([1, H, 64]), op=ALU.mult), reads=["orow", "den"], writes=["arow"])
        S.add("pool", lambda e, s=s: e.dma_start(out=att[s:s + 1, 16, :], in_=arow[:]), reads=["arow", "att"], writes=["att"], dma="d_att")


def _rope_tab(pos):
    half = 32
    inv = (10000.0 ** (-np.arange(half, dtype=np.float32) / half)).astype(np.float32)
    ang = pos.astype(np.float32)[:, None] * inv[None, :]
    return np.concatenate([np.cos(ang), np.sin(ang)], axis=1).astype(np.float32)


_NC = None


def kernel(x_prompt, x_sample, cache_k, cache_v, state_pool, page_table, p_prompt, p_sample,
           w_in, w_pool_mix, pool_scale, w_pool_out, w_att_out, w_o, ln_g, ln_b, w_ple, w_ple_gate):
    global _NC
    f = lambda a: np.ascontiguousarray(np.asarray(a, dtype=np.float32))
    x_prompt = f(x_prompt); x_sample = f(x_sample); p_prompt = f(p_prompt); p_sample = f(p_sample)
    ck = f(cache_k).reshape(NPHYS * 128 * 8, 64)
    cv = f(cache_v).reshape(NPHYS * 128 * 8, 64)
    state_pool = f(state_pool)
    page_table = np.ascontiguousarray(np.asarray(page_table, dtype=np.int32))
    cs_all = _rope_tab(np.arange(SEQ))
    cs_s = _rope_tab(np.full((128,), PAST))
    ident = np.eye(128, dtype=np.float32)
    oh_all = np.zeros((16, 32, SBK), np.float32)
    for s in range(15):
        oh_all[s, 2 * s, :256] = 1.0
        oh_all[s, 2 * s + 1, 256:] = 1.0
    oh_own = np.zeros((32, SBK), np.float32)
    oh_own[30, :256] = 1.0
    oh_own[31, 256:] = 1.0
    tri = np.ones((128, 4, SBK), np.float32)
    for kt in range(4):
        for k in range(128):
            kk = kt * 128 + k
            kb_, kl = kk // 256, kk % 256
            q = np.arange(SBK)
            same = (q // 256) == kb_
            tri[k, kt, :] = np.where(same & ((q % 256) < kl), 0.0, 1.0)
    tri = tri.reshape(128, 4 * SBK)
    pairm = np.zeros((128, 64), np.float32)
    pairm[np.arange(128), np.arange(128) // 2] = 1.0
    iota128 = np.tile(np.arange(128, dtype=np.float32)[None, :], (8, 1))
    hsel = np.zeros((8, 48), np.float32)
    for h in range(8):
        hsel[h, h * 6:(h + 1) * 6] = 1.0
    shared = dict(cs_all=cs_all, cs_s=cs_s, oh_all=oh_all, oh_own=oh_own, tri=tri, ident=ident,
                  w_in=f(w_in)[0], w_mix=f(w_pool_mix)[0], pscale=f(pool_scale)[0], w_po=f(w_pool_out)[0],
                  w_ao=f(w_att_out)[0], w_o=f(w_o)[0], ln_g=f(ln_g)[0], ln_b=f(ln_b)[0], w_ple=f(w_ple)[0],
                  w_pg=f(w_ple_gate)[0], cache_k=ck, cache_v=cv, pairm=pairm, iota128=iota128, hsel=hsel,
                  pofs=(np.arange(128, dtype=np.float32)[:, None] * 8 + np.repeat(np.arange(8, dtype=np.float32), 6)[None, :]).astype(np.float32))
    in_maps = []
    for c in range(NCORES):
        b, r = c // 4, c % 4
        sbs = own_sbs(r)
        tok = np.concatenate([np.arange(s * SBK, (s + 1) * SBK) for s in sbs])
        xown = x_prompt[b][tok]
        xhalo = np.zeros((16, 16, D), np.float32)
        corr = np.ones((16, 4, 16), np.float32)
        for ti in range(16):
            t0 = tok[ti * 128]
            for k in range(16):
                p = t0 - 16 + k
                if p >= 0:
                    xhalo[ti, k] = x_prompt[b, p]
            for g, w in enumerate((2, 4, 8, 16)):
                for k in range(16):
                    corr[ti, g, k] = w / min(t0 + k + 1, w)
        qblk = tok // 256
        jj = np.arange(32)[None, :]
        pastneg = np.where(jj < qblk[:, None], 0.0, -1e30).astype(np.float32)
        sbq = tok // SBK
        notown = np.where((jj // 2) == sbq[:, None], 0.0, 1.0).astype(np.float32)
        ownA = (jj == (2 * sbq)[:, None]).astype(np.float32)
        inA = ((tok % SBK) < 256)
        isAB = np.stack([inA, ~inA], axis=1).astype(np.float32)
        xs = np.zeros((128, D), np.float32); xs[:NSEQ] = x_sample[c * NSEQ:(c + 1) * NSEQ, 0]
        pss = np.zeros((128, PLE), np.float32); pss[:NSEQ] = p_sample[0, c * NSEQ:(c + 1) * NSEQ, 0]
        m = dict(shared)
        m.update(xall=x_prompt[b], xown=np.ascontiguousarray(xown), xhalo=xhalo, pown=np.ascontiguousarray(p_prompt[0, b][tok]),
                 xs=xs, pss=pss, cs_own=np.ascontiguousarray(cs_all[tok]), pastneg=pastneg, notown=notown, ownA=ownA,
                 isAB=isAB, corr=corr, ptab=np.ascontiguousarray(page_table[c * NSEQ:(c + 1) * NSEQ]),
                 spool=np.ascontiguousarray(state_pool[0, c * NSEQ:(c + 1) * NSEQ]))
        in_maps.append(m)
    if _NC is None:
        _NC = build_nc()
    res = run_bass_kernel_spmd(_NC, in_maps, core_ids=list(range(NCORES)))
    R = res.results
    global _last
    _last = R
    y_prompt = np.zeros((2, SEQ, D), np.float32)
    k_prompt = np.zeros((1, 2, SEQ, H, HD), np.float32)
    v_prompt = np.zeros((1, 2, SEQ, H, HD), np.float32)
    pool_prompt = np.zeros((1, 2, 15, 512), np.float32)
    y_sample = np.zeros((32, 1, D), np.float32)
    k_sample = np.zeros((1, 32, 1, H, HD), np.float32)
    v_sample = np.zeros((1, 32, 1, H, HD), np.float32)
    pool_sample = np.zeros((1, 32, 15, 512), np.float32)
    for c in range(NCORES):
        b, r = c // 4, c % 4
        tok = np.concatenate([np.arange(s * SBK, (s + 1) * SBK) for s in own_sbs(r)])
        y_prompt[b, tok] = R[c]["y_own"]
        k_prompt[0, b, tok] = R[c]["k_own"].reshape(2048, H, HD)
        v_prompt[0, b, tok] = R[c]["v_own"].reshape(2048, H, HD)
        if r == 0:
            pool_prompt[0, b] = R[c]["poolp"]
        y_sample[c * NSEQ:(c + 1) * NSEQ, 0] = R[c]["y_s"][:NSEQ]
        k_sample[0, c * NSEQ:(c + 1) * NSEQ, 0] = R[c]["k_s"][:NSEQ].reshape(NSEQ, H, HD)
        v_sample[0, c * NSEQ:(c + 1) * NSEQ, 0] = R[c]["v_s"][:NSEQ].reshape(NSEQ, H, HD)
        pool_sample[0, c * NSEQ:(c + 1) * NSEQ] = R[c]["pool_s"]
    return (y_prompt, y_sample, k_prompt, v_prompt, pool_prompt, k_sample, v_sample, pool_sample)


def _silu_from_psum(S, ps_ap, ps_key, th, th_key, out_ap, out_key, extra_in1=None, extra_key=None):
    S.add("act", lambda e: e.activation(out=th, in_=ps_ap, func=AF.Tanh, scale=0.5), reads=[ps_key], writes=[th_key])
    S.add("dve", lambda e: e.tensor_scalar(out=th, in0=th, scalar1=0.5, scalar2=0.5, op0=ALU.mult, op1=ALU.add),
          reads=[th_key], writes=[th_key])
    if extra_in1 is None:
        S.add("dve", lambda e: e.tensor_tensor(out=out_ap, in0=ps_ap, in1=th, op=ALU.mult),
              reads=[ps_key, th_key], writes=[out_key])
    else:
        S.add("dve", lambda e: e.tensor_tensor(out=th, in0=ps_ap, in1=th, op=ALU.mult),
              reads=[ps_key, th_key], writes=[th_key])
        S.add("pool", lambda e: e.tensor_tensor(out=out_ap, in0=th, in1=extra_in1, op=ALU.mult),
              reads=[th_key, extra_key], writes=[out_key])


def phase_front(nc, S, sb, L):
    w_in, w_mix, pscale, w_po, w_ao = L["w_in"], L["w_mix"], L["pscale"], L["w_po"], L["w_ao"]
    xown, xs, xhalo, corr, spool = L["xown"], L["xs"], L["xhalo"], L["corr"], L["spool"]
    poolp, pool_s, mg_d = L["poolp"], L["pool_s"], L["mg_d"]
    PX, PB, ident, identb, att = L["PX"], L["PB"], L["ident"], L["identb"], L["att"]
    es_ = contextlib.ExitStack()
    L["esF"] = es_

    def wload(name, src_ap, shape):
        t = sb(es_, shape, BF16, name)
        S.add("pool", lambda e: e.dma_start(out=t[:], in_=src_ap), writes=[name], dma="w_" + name)
        return t
    wzb = wload("wzb", w_in[:, 1536:2048].rearrange("(c p) n -> p c n", p=128), [128, 8, 512])
    wu = wload("wu", w_in[:, 2048:2560].rearrange("(c p) n -> p c n", p=128), [128, 8, 512])
    wza = wload("wza", w_in[:, 2560:3072].rearrange("(c p) n -> p c n", p=128), [128, 8, 512])
    wga = wload("wga", w_in[:, 3072:4096].rearrange("(c p) n -> p c n", p=128), [128, 8, 1024])
    wgb = wload("wgb", w_in[:, 4096:5120].rearrange("(c p) n -> p c n", p=128), [128, 8, 1024])
    wmix = wload("wmix", w_mix.rearrange("g c e -> c g e"), [128, 4, 128])
    wpo = wload("wpo", w_po.rearrange("(c p) n -> p c n", p=128), [128, 4, 1024])
    wao = wload("wao", w_ao.rearrange("(c p) n -> p c n", p=128), [128, 4, 1024])
    psc = sb(es_, [128, 4], F32, "psc")
    S.add("sp", lambda e: e.dma_start(out=psc[:], in_=pscale.rearrange("(g c) -> c g", g=4), allow_slow_non_contiguous=True),
          writes=["psc"], dma="psc")
    corb = sb(es_, [128, 16 * 4 * 16], F32, "corb")
    S.add("sp", lambda e: e.dma_start(out=corb[:], in_=corr.rearrange("a g k -> (a g k)")[None, :].to_broadcast([128, 1024])),
          writes=["corb"], dma="corb")
    cor4 = corb[:].rearrange("p (a g k) -> p a g k", a=16, g=4)

    xt = sb(es_, [128, D], F32, "fxt")
    xh = sb(es_, [16, D], F32, "fxh")
    xTu = sb(es_, [128, 8, SBK], BF16, "xTu")
    xTh = sb(es_, [128, 8, 16], BF16, "xTh")
    uext = sb(es_, [128, 4, 16 + SBK], F32, "uext")
    tA = sb(es_, [128, 16 + SBK], F32, "tA")
    tB = sb(es_, [128, 16 + SBK], F32, "tB")
    dT = sb(es_, [128, 4, SBK], BF16, "dT")
    th = sb(es_, [128, SBK], F32, "th")
    szT = sb(es_, [128, 4, SBK], BF16, "szT")
    pzT = sb(es_, [128, 4, SBK], BF16, "pzT")
    azb = sb(es_, [128, AW], BF16, "azb")
    azT = sb(es_, [128, 4, SBK], BF16, "azT")
    tg1 = sb(es_, [128, SBK], F32, "tg1")
    tg2 = sb(es_, [128, SBK], F32, "tg2")
    t1 = sb(es_, [128, SBK], F32, "t1")
    t2 = sb(es_, [128, SBK], F32, "t2")
    mgT = sb(es_, [128, 8, SBK], BF16, "mgT")
    hsb = sb(es_, [16, 4, 512], F32, "hsb")
    hT = sb(es_, [128, 4, 4, 16], F32, "hT")
    ssum = sb(es_, [128, 4], F32, "ssum")
    pout = sb(es_, [16, 512], F32, "pout")

    for u in range(5):
        T = SBK if u < 4 else 128
        nt = T // 128
        for t in range(nt):
            src = xown[u * SBK + t * 128:u * SBK + (t + 1) * 128, :] if u < 4 else xs[:, :]
            S.add("sp", lambda e, src=src: e.dma_start(out=xt[:], in_=src), writes=["fxt"], dma="fxt")

            def tr(e):
                for c in range(8):
                    i = e.transpose(PX[:, c * 128:(c + 1) * 128], xt[:, c * 128:(c + 1) * 128], ident[:])
                return i
            S.add("pe", tr, reads=["fxt", "ident"], writes=["PX"])
            S.add("act", lambda e, t=t: e.activation(out=xTu[:, :, t * 128:(t + 1) * 128],
                                                     in_=PX[:].rearrange("p (c t) -> p c t", c=8), func=AF.Copy),
                  reads=["PX"], writes=["xTu"])
        if u < 4:
            S.add("sp", lambda e, u=u: e.dma_start(out=xh[:], in_=xhalo[u * 4, :, :]), writes=["fxh"], dma="fxh")

            def trh(e):
                for c in range(8):
                    i = e.transpose(PX[:, c * 16:(c + 1) * 16], xh[:, c * 128:(c + 1) * 128], ident[0:16, 0:16])
                return i
            S.add("pe", trh, reads=["fxh", "ident"], writes=["PX"])
            S.add("act", lambda e: e.activation(out=xTh[:].rearrange("p c k -> p (c k)"), in_=PX[:, 0:128], func=AF.Copy),
                  reads=["PX"], writes=["xTh"])
        for g in range(4):
            pb, pbk = PB[g % 2], "PB%d" % (g % 2)

            def mu(e, g=g, pb=pb, T=T, u=u):
                for c in range(8):
                    i = e.matmul(pb[:, 0:T], wu[:, c, g * 128:(g + 1) * 128], xTu[:, c, 0:T], start=(c == 0), stop=(c == 7))
                return i
            S.add("pe", mu, reads=["wu", "xTu"], writes=[pbk])
            S.add("act", lambda e, g=g, pb=pb, T=T: e.activation(out=uext[:, g, 16:16 + T], in_=pb[:, 0:T], func=AF.Copy),
                  reads=[pbk], writes=["uext%d" % g])
            if u < 4:
                def muh(e, g=g, pb=pb):
                    for c in range(8):
                        i = e.matmul(pb[:, 0:16], wu[:, c, g * 128:(g + 1) * 128], xTh[:, c, :], start=(c == 0), stop=(c == 7))
                    return i
                S.add("pe", muh, reads=["wu", "xTh"], writes=[pbk])
                S.add("act", lambda e, g=g, pb=pb: e.activation(out=uext[:, g, 0:16], in_=pb[:, 0:16], func=AF.Copy),
                      reads=[pbk], writes=["uext%d" % g])
        if u < 4:
            Ln = 16 + T
            for g in range(4):
                w = 2 ** (g + 1)
                cur = uext[:, g, :]
                ck = "uext%d" % g
                S.add("dve", lambda e, cur=cur: e.tensor_tensor(out=tA[:, 1:Ln], in0=cur[:, 1:Ln], in1=cur[:, 0:Ln - 1], op=ALU.add),
                      reads=[ck], writes=["tA"])
                fin, fk = tA, "tA"
                if g >= 1:
                    S.add("dve", lambda e: e.tensor_tensor(out=tB[:, 3:Ln], in0=tA[:, 3:Ln], in1=tA[:, 1:Ln - 2], op=ALU.add),
                          reads=["tA"], writes=["tB"])
                    fin, fk = tB, "tB"
                if g >= 2:
                    S.add("dve", lambda e: e.tensor_tensor(out=tA[:, 7:Ln], in0=tB[:, 7:Ln], in1=tB[:, 3:Ln - 4], op=ALU.add),
                          reads=["tB"], writes=["tA"])
                    fin, fk = tA, "tA"
                if g >= 3:
                    S.add("dve", lambda e: e.tensor_tensor(out=tB[:, 15:Ln], in0=tA[:, 15:Ln], in1=tA[:, 7:Ln - 8], op=ALU.add),
                          reads=["tA"], writes=["tB"])
                    fin, fk = tB, "tB"
                S.add("dve", lambda e, fin=fin, g=g, u=u: e.tensor_tensor(out=fin[:, 16:32], in0=fin[:, 16:32], in1=cor4[:, u * 4, g, :], op=ALU.mult),
                      reads=[fk, "corb"], writes=[fk])
                S.add("dve", lambda e, fin=fin, g=g, w=w, cur=cur, T=T: e.scalar_tensor_tensor(
                    out=dT[:, g, 0:T], in0=fin[:, 16:16 + T], scalar=1.0 / w, in1=cur[:, 16:16 + T], op0=ALU.mult, op1=ALU.subtract),
                    reads=[fk, ck], writes=["dT"])
            if u == 3:
                def trp(e):
                    for g in range(4):
                        i = e.transpose(PB[2][0:15, g * 128:(g + 1) * 128], uext[:, g, 16 + 497:16 + 512], ident[:])
                    return i
                S.add("pe", trp, reads=["uext0", "uext1", "uext2", "uext3", "ident"], writes=["PB2"])
                S.add("dve", lambda e: e.tensor_copy(out=pout[0:15, :], in_=PB[2][0:15, :]), reads=["PB2"], writes=["pout"])
                S.add("sp", lambda e: e.dma_start(out=poolp[:, :], in_=pout[0:15, :]), reads=["pout"], writes=["OUTpoolp"], dma="opoolp")
        else:
            for s in range(NSEQ):
                S.add("sp", lambda e, s=s: e.dma_start(out=hsb[0:15, s, :], in_=spool[s, :, :]), writes=["hsb"], dma="hsb")
                S.add("sp", lambda e, s=s: e.dma_start(out=pool_s[s, 0:14, :], in_=spool[s, 1:15, :]), writes=["OUTps%d" % s], dma="ops%d" % s)
            for s in range(NSEQ):
                def trs(e, s=s):
                    for g in range(4):
                        i = e.transpose(PB[2][:, (s * 4 + g) * 16:(s * 4 + g) * 16 + 15], hsb[0:15, s, g * 128:(g + 1) * 128], ident[0:15, 0:15])
                    return i
                S.add("pe", trs, reads=["hsb", "ident"], writes=["PB2"])
            S.add("dve", lambda e: e.memset(hT[:], 0.0), writes=["hT"])
            S.add("dve", lambda e: e.tensor_copy(out=hT[:, :, :, 0:15], in_=PB[2][:, 0:256].rearrange("p (s g r) -> p s g r", s=4, g=4)[:, :, :, 0:15]),
                  reads=["PB2", "hT"], writes=["hT"])
            S.add("pool", lambda e: e.memset(dT[:], 0.0), writes=["dT"])
            for g in range(4):
                w = 2 ** (g + 1)
                S.add("dve", lambda e, g=g, w=w: e.tensor_reduce(out=ssum[:], in_=hT[:, :, g, 16 - w:15], axis=AX.X, op=ALU.add),
                      reads=["hT"], writes=["ssum"])
                S.add("dve", lambda e, g=g: e.tensor_tensor(out=ssum[:], in0=ssum[:], in1=uext[:, g, 16:16 + NSEQ], op=ALU.add),
                      reads=["ssum", "uext%d" % g], writes=["ssum"])
                S.add("dve", lambda e, g=g, w=w: e.scalar_tensor_tensor(
                    out=dT[:, g, 0:NSEQ], in0=ssum[:], scalar=1.0 / w, in1=uext[:, g, 16:16 + NSEQ], op0=ALU.mult, op1=ALU.subtract),
                    reads=["ssum", "uext%d" % g, "dT"], writes=["dT"])

            def tru(e):
                for g in range(4):
                    i = e.transpose(PB[3][0:NSEQ, g * 128:(g + 1) * 128], uext[:, g, 16:16 + NSEQ], ident[:])
                return i
            S.add("pe", tru, reads=["uext0", "uext1", "uext2", "uext3", "ident"], writes=["PB3"])
            S.add("dve", lambda e: e.tensor_copy(out=pout[0:NSEQ, :], in_=PB[3][0:NSEQ, :]), reads=["PB3"], writes=["pout"])
            S.add("sp", lambda e: e.dma_start(out=pool_s[:, 14, :], in_=pout[0:NSEQ, :]), reads=["pout"], writes=["OUTpsu"], dma="opsu")
        for g in range(4):
            pb, pbk = PB[g % 2], "PB%d" % (g % 2)

            def mz(e, g=g, pb=pb, T=T):
                for c in range(8):
                    i = e.matmul(pb[:, 0:T], wza[:, c, g * 128:(g + 1) * 128], xTu[:, c, 0:T], start=(c == 0), stop=(c == 7))
                return i
            S.add("pe", mz, reads=["wza", "xTu"], writes=[pbk])
            _silu_from_psum(S, pb[:, 0:T], pbk, th[:, 0:T], "th", szT[:, g, 0:T], "szT")
            S.add("pe", lambda e, g=g, pb=pb, T=T: e.matmul(pb[:, 0:T], wmix[:, g, :], dT[:, g, 0:T], start=True, stop=True),
                  reads=["wmix", "dT"], writes=[pbk])
            S.add("dve", lambda e, g=g, pb=pb, T=T: e.scalar_tensor_tensor(
                out=pzT[:, g, 0:T], in0=pb[:, 0:T], scalar=psc[:, g:g + 1], in1=szT[:, g, 0:T], op0=ALU.mult, op1=ALU.mult),
                reads=[pbk, "psc", "szT"], writes=["pzT"])
        for t in range(nt):
            def mzb(e, t=t):
                for c in range(8):
                    i = e.matmul(PB[0][:], xTu[:, c, t * 128:(t + 1) * 128], wzb[:, c, :], start=(c == 0), stop=(c == 7))
                return i
            S.add("pe", mzb, reads=["xTu", "wzb"], writes=["PB0"])
            atile = att[:, u * 4 + t, :]
            _silu_from_psum(S, PB[0][:], "PB0", th[:], "th", azb[:], "azb", extra_in1=atile, extra_key="att")
            ptb = PB[1][:].bitcast(BF16)

            def tra(e):
                for cc in range(4):
                    i = e.transpose(ptb[:, cc * 128:(cc + 1) * 128], azb[:, cc * 128:(cc + 1) * 128], identb[:])
                return i
            S.add("pe", tra, reads=["azb", "identb"], writes=["PB1"])
            S.add("act", lambda e, t=t, ptb=ptb: e.activation(out=azT[:, :, t * 128:(t + 1) * 128],
                                                              in_=ptb[:, 0:512].rearrange("p (c t) -> p c t", c=4), func=AF.Copy),
                  reads=["PB1"], writes=["azT"])
        for m in range(8):
            def myp(e, m=m, T=T):
                for cc in range(4):
                    i = e.matmul(PB[2][:, 0:T], wpo[:, cc, m * 128:(m + 1) * 128], pzT[:, cc, 0:T], start=(cc == 0), stop=(cc == 3))
                return i

            def mya(e, m=m, T=T):
                for cc in range(4):
                    i = e.matmul(PB[3][:, 0:T], wao[:, cc, m * 128:(m + 1) * 128], azT[:, cc, 0:T], start=(cc == 0), stop=(cc == 3))
                return i

            def mga(e, m=m, T=T):
                for c in range(8):
                    i = e.matmul(PB[4][:, 0:T], wga[:, c, m * 128:(m + 1) * 128], xTu[:, c, 0:T], start=(c == 0), stop=(c == 7))
                return i

            def mgb(e, m=m, T=T):
                for c in range(8):
                    i = e.matmul(PB[5][:, 0:T], wgb[:, c, m * 128:(m + 1) * 128], xTu[:, c, 0:T], start=(c == 0), stop=(c == 7))
                return i
            S.add("pe", myp, reads=["wpo", "pzT"], writes=["PB2"])
            S.add("pe", mya, reads=["wao", "azT"], writes=["PB3"])
            S.add("pe", mga, reads=["wga", "xTu"], writes=["PB4"])
            S.add("pe", mgb, reads=["wgb", "xTu"], writes=["PB5"])
            S.add("act", lambda e, T=T: e.activation(out=tg1[:, 0:T], in_=PB[4][:, 0:T], func=AF.Tanh, scale=0.5), reads=["PB4"], writes=["tg1"])
            S.add("act", lambda e, T=T: e.activation(out=tg2[:, 0:T], in_=PB[5][:, 0:T], func=AF.Tanh, scale=0.5), reads=["PB5"], writes=["tg2"])
            S.add("pool", lambda e, T=T: e.tensor_scalar(out=tg1[:, 0:T], in0=tg1[:, 0:T], scalar1=0.5, scalar2=0.5, op0=ALU.mult, op1=ALU.add),
                  reads=["tg1"], writes=["tg1"])
            S.add("pool", lambda e, T=T: e.tensor_scalar(out=tg2[:, 0:T], in0=tg2[:, 0:T], scalar1=0.5, scalar2=0.5, op0=ALU.mult, op1=ALU.add),
                  reads=["tg2"], writes=["tg2"])
            S.add("dve", lambda e, T=T: e.tensor_tensor(out=t1[:, 0:T], in0=PB[2][:, 0:T], in1=tg1[:, 0:T], op=ALU.mult),
                  reads=["PB2", "tg1"], writes=["t1"])
            S.add("dve", lambda e, T=T: e.tensor_tensor(out=t2[:, 0:T], in0=PB[3][:, 0:T], in1=tg2[:, 0:T], op=ALU.mult),
                  reads=["PB3", "tg2"], writes=["t2"])
            S.add("pool", lambda e, m=m, T=T: e.tensor_tensor(out=mgT[:, m, 0:T], in0=t1[:, 0:T], in1=t2[:, 0:T], op=ALU.add),
                  reads=["t1", "t2"], writes=["mgT"])
        S.add("sp", lambda e, u=u, T=T: e.dma_start(out=mg_d[u, :, :, 0:T], in_=mgT[:, :, 0:T]), reads=["mgT"], writes=["mgd%d" % u], dma="mgst")
    S.barrier()
    es_.close()


def phase_tail(nc, S, sb, L):
    w_o, w_pg, w_ple, ln_g, ln_b = L["w_o"], L["w_pg"], L["w_ple"], L["ln_g"], L["ln_b"]
    xown, xs, pown, pss, y_own, y_s, mg_d = L["xown"], L["xs"], L["pown"], L["pss"], L["y_own"], L["y_s"], L["mg_d"]
    PX, PB, ident, identb, neghalf = L["PX"], L["PB"], L["ident"], L["identb"], L["neghalf"]
    es_ = contextlib.ExitStack()

    def wload(name, src_ap, shape):
        t = sb(es_, shape, BF16, name)
        S.add("pool", lambda e: e.dma_start(out=t[:], in_=src_ap), writes=[name], dma="w_" + name)
        return t
    wo = wload("wo", w_o.rearrange("(c p) n -> p c n", p=128), [128, 8, D])
    wpg = wload("wpg", w_pg.rearrange("(c p) n -> p c n", p=128), [128, 8, D])
    wple = wload("wple", w_ple.rearrange("(c p) n -> p c n", p=128), [128, 2, D])
    lg = sb(es_, [128, D], F32, "lg")
    lb = sb(es_, [128, D], F32, "lb")
    S.add("sp", lambda e: e.dma_start(out=lg[:], in_=ln_g[None, :].to_broadcast([128, D])), writes=["lg"], dma="lg")
    S.add("sp", lambda e: e.dma_start(out=lb[:], in_=ln_b[None, :].to_broadcast([128, D])), writes=["lb"], dma="lb")
    mgT = [sb(es_, [128, 8, SBK], BF16, "mgTt") for _ in range(2)]
    xt = [sb(es_, [128, D], F32, "txt") for _ in range(2)]
    pt = [sb(es_, [128, PLE], F32, "tpt") for _ in range(2)]
    hp_ = [sb(es_, [128, D], F32, "hp") for _ in range(2)]
    hh_ = [sb(es_, [128, D], F32, "hh") for _ in range(2)]
    hb_ = [sb(es_, [128, D], BF16, "hb") for _ in range(2)]
    pendB = []
    hT = sb(es_, [128, 8, 128], BF16, "hTt")
    pT = sb(es_, [128, 2, 128], BF16, "pTt")
    sg = sb(es_, [128, D], F32, "sg")
    yt = [sb(es_, [128, D], F32, "yt") for _ in range(2)]
    st = sb(es_, [128, 2, 6], F32, "st")
    mv = sb(es_, [128, 2], F32, "mv")
    ve = sb(es_, [128, 1], F32, "ve")
    rstd = sb(es_, [128, 1], F32, "rstd")
    nmr = sb(es_, [128, 1], F32, "nmr")
    gi = 0
    for u in range(5):
        T = SBK if u < 4 else 128
        mj = u % 2
        S.add("sp", lambda e, u=u, mj=mj, T=T: e.dma_start(out=mgT[mj][:, :, 0:T], in_=mg_d[u, :, :, 0:T]),
              reads=["mgd%d" % u], writes=["mgTt%d" % mj], dma="mgld%d" % mj)
        for t in range(T // 128):
            j = gi % 2
            gi += 1
            hp, hh, hb = hp_[j], hh_[j], hb_[j]
            HP, HH, HB = "hp%d" % j, "hh%d" % j, "hb%d" % j
            xsrc = xown[u * SBK + t * 128:u * SBK + (t + 1) * 128, :] if u < 4 else xs[:, :]
            psrc = pown[u * SBK + t * 128:u * SBK + (t + 1) * 128, :] if u < 4 else pss[:, :]
            ydst = y_own[u * SBK + t * 128:u * SBK + (t + 1) * 128, :] if u < 4 else y_s[:, :]
            S.add("sp", lambda e, j=j, xsrc=xsrc: e.dma_start(out=xt[j][:], in_=xsrc), writes=["txt%d" % j], dma="txt%d" % j)
            S.add("sp", lambda e, j=j, psrc=psrc: e.dma_start(out=pt[j][:], in_=psrc), writes=["tpt%d" % j], dma="tpt%d" % j)

            def mho(e, t=t, mj=mj):
                for half in range(2):
                    for c in range(8):
                        i = e.matmul(PX[:, half * 512:(half + 1) * 512], mgT[mj][:, c, t * 128:(t + 1) * 128],
                                     wo[:, c, half * 512:(half + 1) * 512], start=(c == 0), stop=(c == 7))
                return i
            S.add("pe", mho, reads=["mgTt%d" % mj, "wo"], writes=["PX"])
            S.add("dve", lambda e, j=j, hp=hp: e.scalar_tensor_tensor(out=hp[:], in0=xt[j][:], scalar=ALPHA, in1=PX[:], op0=ALU.mult, op1=ALU.add),
                  reads=["txt%d" % j, "PX"], writes=[HP])

            def bst(e, hp=hp):
                for half in range(2):
                    i = e.bn_stats(out=st[:, half, :], in_=hp[:, half * 512:(half + 1) * 512])
                return i
            S.add("dve", bst, reads=[HP], writes=["st"])
            S.add("dve", lambda e: e.bn_aggr(out=mv[:], in_=st[:].rearrange("p a b -> p (a b)")), reads=["st"], writes=["mv"])
            S.add("dve", lambda e: e.tensor_scalar(out=ve[:], in0=mv[:, 1:2], scalar1=EPS, scalar2=None, op0=ALU.add), reads=["mv"], writes=["ve"])
            S.add("pool", lambda e: e.tensor_tensor(out=rstd[:], in0=ve[:], in1=neghalf[:], op=ALU.pow), reads=["ve", "neghalf"], writes=["rstd"])
            S.add("dve", lambda e: e.tensor_scalar(out=nmr[:], in0=mv[:, 0:1], scalar1=rstd[:, 0:1], scalar2=-1.0, op0=ALU.mult, op1=ALU.mult),
                  reads=["mv", "rstd"], writes=["nmr"])
            S.add("act", lambda e, hh=hh, hp=hp: e.activation(out=hh[:], in_=hp[:], func=AF.Identity, scale=rstd[:, 0:1], bias=nmr[:, 0:1]),
                  reads=[HP, "rstd", "nmr"], writes=[HH])
            S.add("dve", lambda e, hh=hh: e.tensor_tensor(out=hh[:], in0=hh[:], in1=lg[:], op=ALU.mult), reads=[HH, "lg"], writes=[HH])
            S.add("pool", lambda e, hh=hh: e.tensor_tensor(out=hh[:], in0=hh[:], in1=lb[:], op=ALU.add), reads=[HH, "lb"], writes=[HH])
            S.add("pool", lambda e, hh=hh, hb=hb: e.tensor_copy(out=hb[:], in_=hh[:]), reads=[HH], writes=[HB])
            def stageB(j=j, hh=hh, hb=hb, HH=HH, HB=HB, ydst=ydst):
                ptb = PB[0][:].bitcast(BF16)

                def trh(e):
                    for c in range(8):
                        i = e.transpose(ptb[:, c * 128:(c + 1) * 128], hb[:, c * 128:(c + 1) * 128], identb[:])
                    return i
                S.add("pe", trh, reads=[HB, "identb"], writes=["PB0"])
                S.add("act", lambda e, ptb=ptb: e.activation(out=hT[:].rearrange("p c t -> p (c t)"), in_=ptb[:, 0:1024], func=AF.Copy),
                      reads=["PB0"], writes=["hTt"])

                def trp(e, j=j):
                    for c in range(2):
                        i = e.transpose(PB[3][:, c * 128:(c + 1) * 128], pt[j][:, c * 128:(c + 1) * 128], ident[:])
                    return i
                S.add("pe", trp, reads=["tpt%d" % j, "ident"], writes=["PB3"])
                S.add("act", lambda e: e.activation(out=pT[:].rearrange("p c t -> p (c t)"), in_=PB[3][:, 0:256], func=AF.Copy),
                      reads=["PB3"], writes=["pTt"])
                for half in range(2):
                    def mg_(e, half=half):
                        for c in range(8):
                            i = e.matmul(PB[1 + half][:], hT[:, c, :], wpg[:, c, half * 512:(half + 1) * 512], start=(c == 0), stop=(c == 7))
                        return i
                    S.add("pe", mg_, reads=["hTt", "wpg"], writes=["PB%d" % (1 + half)])

                    def mp_(e, half=half):
                        for c in range(2):
                            i = e.matmul(PB[4 + half][:], pT[:, c, :], wple[:, c, half * 512:(half + 1) * 512], start=(c == 0), stop=(c == 1))
                        return i
                    S.add("pe", mp_, reads=["pTt", "wple"], writes=["PB%d" % (4 + half)])
                    hs = slice(half * 512, (half + 1) * 512)
                    S.add("act", lambda e, half=half, hs=hs: e.activation(out=sg[:, hs], in_=PB[1 + half][:], func=AF.Tanh, scale=0.5),
                          reads=["PB%d" % (1 + half)], writes=["sg%d" % half])
                    S.add("pool", lambda e, hs=hs: e.tensor_scalar(out=sg[:, hs], in0=sg[:, hs], scalar1=0.5, scalar2=0.5, op0=ALU.mult, op1=ALU.add),
                          reads=["sg%d" % half], writes=["sg%d" % half])
                    S.add("dve", lambda e, half=half, hs=hs: e.tensor_tensor(out=sg[:, hs], in0=PB[4 + half][:], in1=sg[:, hs], op=ALU.mult),
                          reads=["PB%d" % (4 + half), "sg%d" % half], writes=["sg%d" % half])
                S.add("pool", lambda e, j=j: e.tensor_tensor(out=yt[j][:], in0=sg[:], in1=hh[:], op=ALU.add),
                      reads=["sg0", "sg1", HH], writes=["yt%d" % j])
                S.add("sp", lambda e, j=j, ydst=ydst: e.dma_start(out=ydst, in_=yt[j][:]), reads=["yt%d" % j], writes=["OUTy%d" % j], dma="oy%d" % j)
            if pendB:
                pendB.pop()()
            pendB.append(stageB)
    pendB.pop()()
    S.barrier()
    es_.close()
```

```python
import contextlib
import numpy as np
import concourse.bass as bass
import concourse.mybir as mybir
from concourse.bass_utils import run_bass_kernel_spmd

F32 = mybir.dt.float32
BF16 = mybir.dt.bfloat16
I32 = mybir.dt.int32
U32 = mybir.dt.uint32
AF = mybir.ActivationFunctionType
ALU = mybir.AluOpType
AX = mybir.AxisListType

NCORES = 8
D = 1024
SEQ = 8192
H = 8
HD = 64
AW = 512
PLE = 256
PAST = 16384
NPAGE = 128
NPHYS = 5120
DEV_SKIP_DECODE = False
DEBUG = False
_last = None
SBK = 512
KMAX = [3, 7, 11, 15]
BIG = 30000.0
ALPHA = 2.0 ** 0.25
EPS = 1e-5
NSEQ = 4
QTOT = 2048 + 128


def own_sbs(r):
    return [r, 7 - r, 8 + r, 15 - r]


class Op:
    __slots__ = ("eng", "fn", "dma", "deps", "needed", "event")

    def __init__(self, eng, fn, dma):
        self.eng = eng
        self.fn = fn
        self.dma = dma
        self.deps = set()
        self.needed = False
        self.event = None


class Sched:
    ENG = ("pe", "act", "dve", "pool", "sp")

    def __init__(self, nc, es):
        self.nc = nc
        self.es = es
        self.h = {"pe": nc.tensor, "act": nc.scalar, "dve": nc.vector, "pool": nc.gpsimd, "sp": nc.sync}
        self.ops = {e: [] for e in self.ENG}
        self.res = {}
        self.bar = set()
        self.dma_since = []

    def add(self, eng, fn, reads=(), writes=(), dma=None):
        op = Op(eng, fn, dma)
        deps = set(self.bar)
        for r in reads:
            st = self.res.get(r)
            if st is not None and st[0] is not None:
                deps.add(st[0])
        for w in writes:
            st = self.res.get(w)
            if st is not None:
                if st[0] is not None:
                    deps.add(st[0])
                deps.update(st[1])
        if eng == "pe" and dma is None:
            deps = {d for d in deps if not (d.eng == "pe" and d.dma is None)}
        op.deps = deps
        for d in deps:
            d.needed = True
        for r in reads:
            self.res.setdefault(r, [None, []])[1].append(op)
        for w in writes:
            self.res[w] = [op, []]
        self.ops[eng].append(op)
        if dma is not None:
            self.dma_since.append(op)
        return op

    def barrier(self):
        b = set()
        for e in self.ENG:
            for op in reversed(self.ops[e]):
                if op.dma is None:
                    b.add(op)
                    break
        b.update(self.dma_since)
        self.dma_since = []
        for d in b:
            d.needed = True
        self.bar = b

    def emit(self, block):
        nc = self.nc
        esem = {e: self.es.enter_context(nc.semaphore("sem_" + e)) for e in self.ENG}
        dsem = {}
        dcnt = {}
        for e in self.ENG:
            cnt = 0
            for op in self.ops[e]:
                if op.dma is not None:
                    if op.dma not in dsem:
                        dsem[op.dma] = self.es.enter_context(nc.semaphore("dq_%d" % len(dsem)))
                        dcnt[op.dma] = 0
                    dcnt[op.dma] += 16
                    op.event = (dsem[op.dma], dcnt[op.dma], 16)
                elif op.needed:
                    cnt += 1
                    op.event = (esem[e], cnt, 1)

        def run(e, handle):
            waited = {}
            for op in self.ops[e]:
                need = {}
                for d in op.deps:
                    sem, val, _ = d.event
                    k = id(sem)
                    if k not in need or need[k][1] < val:
                        need[k] = (sem, val)
                for k, (sem, val) in need.items():
                    if waited.get(k, 0) < val:
                        handle.wait_ge(sem, val)
                        waited[k] = val
                ins = op.fn(handle)
                if op.event is not None:
                    ins.then_inc(op.event[0], op.event[2])

        @block.sync
        def _(eng):
            run("sp", eng)

        @block.gpsimd
        def _(eng):
            run("pool", eng)

        @block.tensor
        def _(eng):
            run("pe", eng)

        @block.vector
        def _(eng):
            run("dve", eng)

        @block.scalar
        def _(eng):
            run("act", eng)


def build_nc():
    nc = bass.Bass("TRN2", target_bir_lowering=False)

    def din(name, shape, dt=F32):
        return nc.dram_tensor(name, list(shape), dt, kind="ExternalInput").ap()

    def dout(name, shape, dt=F32):
        return nc.dram_tensor(name, list(shape), dt, kind="ExternalOutput").ap()

    xall = din("xall", [SEQ, D])
    xown = din("xown", [2048, D])
    xhalo = din("xhalo", [16, 16, D])
    pown = din("pown", [2048, PLE])
    xs = din("xs", [128, D])
    pss = din("pss", [128, PLE])
    cs_all = din("cs_all", [SEQ, 64])
    cs_own = din("cs_own", [2048, 64])
    cs_s = din("cs_s", [128, 64])
    pastneg = din("pastneg", [2048, 32])
    notown = din("notown", [2048, 32])
    ownA = din("ownA", [2048, 32])
    isAB = din("isAB", [2048, 2])
    oh_all = din("oh_all", [16, 32, SBK])
    oh_own = din("oh_own", [32, SBK])
    tri = din("tri", [128, 4 * SBK])
    corr = din("corr", [16, 4, 16])
    ident_d = din("ident", [128, 128])
    w_in = din("w_in", [D, 5120])
    w_mix = din("w_mix", [4, 128, 128])
    pscale = din("pscale", [512])
    w_po = din("w_po", [512, D])
    w_ao = din("w_ao", [512, D])
    w_o = din("w_o", [D, D])
    ln_g = din("ln_g", [D])
    ln_b = din("ln_b", [D])
    w_ple = din("w_ple", [PLE, D])
    w_pg = din("w_pg", [D, D])
    cache_k = din("cache_k", [NPHYS * 128 * 8, 64])
    cache_v = din("cache_v", [NPHYS * 128 * 8, 64])
    ptab = din("ptab", [NSEQ, NPAGE], I32)
    spool = din("spool", [NSEQ, 15, 512])
    pairm = din("pairm", [128, 64])
    iota128 = din("iota128", [8, 128])
    hsel = din("hsel", [8, 48])
    pofs = din("pofs", [128, 48])

    y_own = dout("y_own", [2048, D])
    k_own = dout("k_own", [2048, AW])
    v_own = dout("v_own", [2048, AW])
    poolp = dout("poolp", [15, 512])
    y_s = dout("y_s", [128, D])
    k_s = dout("k_s", [128, AW])
    v_s = dout("v_s", [128, AW])
    pool_s = dout("pool_s", [NSEQ, 15, 512])

    if DEBUG:
        dbg_att = dout("dbg_att", [128, 17, AW])
        dbg_mg = dout("dbg_mg", [5, 128, 8, SBK])
        dbg_qt = dout("dbg_qt", [96, H, QTOT])
    KT_d = nc.dram_tensor("KT_d", [H, 96, 20 * SBK], BF16, kind="Internal").ap()
    V_d = nc.dram_tensor("V_d", [H, 20, 128, 4, 66], BF16, kind="Internal").ap()
    mg_d = nc.dram_tensor("mg_d", [5, 128, 8, SBK], BF16, kind="Internal").ap()

    with contextlib.ExitStack() as es:
        S = Sched(nc, es)
        uid = [0]

        def sb(es_, shape, dt=F32, name=None):
            uid[0] += 1
            return es_.enter_context(nc.sbuf_tensor("%s_%d" % (name or "t", uid[0]), list(shape), dt))

        PX = es.enter_context(nc.psum_tensor("PX", [128, 1024], F32))
        PB = [es.enter_context(nc.psum_tensor("PB%d" % i, [128, 512], F32)) for i in range(6)]

        ident = sb(es, [128, 128], F32, "ident")
        identb = sb(es, [128, 128], BF16, "identb")
        att = sb(es, [128, 17, AW], BF16, "att")
        kmT = sb(es, [64, H, 32], F32, "kmT")
        kmTb = sb(es, [64, H, 32], BF16, "kmTb")
        kmx = sb(es, [128, H], F32, "kmx")
        kmxb = sb(es, [128, H], F32, "kmxb")
        trib = sb(es, [128, 4, SBK], BF16, "trib")
        ones_f = sb(es, [128, 128], F32, "ones_f")
        neghalf = sb(es, [128, 1], F32, "neghalf")
        qs_f = sb(es, [128, AW], F32, "qs_f")
        ks_f = sb(es, [128, AW], F32, "ks_f")
        vs_f = sb(es, [128, AW], F32, "vs_f")
        esQ = contextlib.ExitStack()
        QT = sb(esQ, [96, H, QTOT], BF16, "QT")

        S.add("sp", lambda e: e.dma_start(out=ident[:], in_=ident_d[:, :]), writes=["ident"], dma="c0")
        S.add("pool", lambda e: e.dma_start(out=identb[:], in_=ident_d[:, :]), writes=["identb"], dma="c1")
        S.add("pool", lambda e: e.dma_start(out=trib[:].rearrange("p a b -> p (a b)"), in_=tri[:, :]), writes=["trib"], dma="c2")
        S.add("dve", lambda e: e.memset(kmx[:], 0.0), writes=["kmx"])
        S.add("dve", lambda e: e.memset(ones_f[:], 1.0), writes=["ones_f"])
        S.add("dve", lambda e: e.memset(neghalf[:], -0.5), writes=["neghalf"])

        esA = contextlib.ExitStack()
        wq = sb(esA, [128, 8, 512], BF16, "wq")
        wk = sb(esA, [128, 8, 512], BF16, "wk")
        wv = sb(esA, [128, 8, 512], BF16, "wv")
        for nm, wt, c0 in (("wq", wq, 0), ("wk", wk, 512), ("wv", wv, 1024)):
            S.add("pool", lambda e, wt=wt, c0=c0: e.dma_start(
                out=wt[:], in_=w_in[:, c0:c0 + 512].rearrange("(c p) n -> p c n", p=128)),
                writes=[nm], dma="w" + nm)
        xt = [sb(esA, [128, D], F32, "xt") for _ in range(2)]
        cst = [sb(esA, [128, 64], F32, "cst") for _ in range(2)]
        xT = [sb(esA, [128, 8, 128], BF16, "xT") for _ in range(2)]
        r1 = sb(esA, [128, 256], F32, "r1")
        r2 = sb(esA, [128, 256], F32, "r2")
        r3 = sb(esA, [128, 256], F32, "r3")
        r4 = sb(esA, [128, 256], F32, "r4")
        kr = [sb(esA, [128, AW], F32, "kr") for _ in range(2)]
        vf = [sb(esA, [128, AW], F32, "vf") for _ in range(2)]
        qr = sb(esA, [128, AW], F32, "qr")
        kb = [sb(esA, [128, AW], BF16, "kb") for _ in range(2)]
        qb = sb(esA, [128, AW], BF16, "qb")
        sq = sb(esA, [128, AW], F32, "sq")
        ksq = sb(esA, [128, H], F32, "ksq")
        qsq = sb(esA, [128, H], F32, "qsq")
        KTs = [sb(esA, [96, H, SBK], BF16, "KTs") for _ in range(2)]
        Vs = [sb(esA, [128, H, 4, 66], BF16, "Vs") for _ in range(2)]
        for j in range(2):
            S.add("pool", lambda e, j=j: e.memset(Vs[j][:], 1.0), writes=["Vs%d" % j])
        sel_c = sb(esA, [128, 3, 32], F32, "sel_c")
        isab = sb(esA, [128, 2], F32, "isab")
        sc = sb(esA, [128, H, 32], F32, "sc")
        sel = sb(esA, [128, H, 32], F32, "sel")
        tmp3 = sb(esA, [128, H, 32], F32, "tmp3")
        mx8 = sb(esA, [128, H, 8], F32, "mx8")
        selA = sb(esA, [128, H], F32, "selA")
        tq = sb(esA, [128, H], F32, "tq")
        mbb = sb(esA, [128, H, 32], BF16, "mbb")

        PSQ, PST, PSS, PSM = ("PB4", "PB5", "PB4", "PB5")
        pq, pt_, ps_, pm_ = PB[4], PB[5], PB[4], PB[5]

        def rope(src_ps, src_key, cs_t, cs_key, dst, dst_key):
            s3 = src_ps[:].rearrange("p (h d) -> p h d", h=H)
            d3 = dst[:].rearrange("p (h d) -> p h d", h=H)
            cosb = cs_t[:, None, 0:32].to_broadcast([128, H, 32])
            sinb = cs_t[:, None, 32:64].to_broadcast([128, H, 32])
            a3 = r1[:].rearrange("p (h d) -> p h d", h=H)
            b3 = r2[:].rearrange("p (h d) -> p h d", h=H)
            c3 = r3[:].rearrange("p (h d) -> p h d", h=H)
            e3 = r4[:].rearrange("p (h d) -> p h d", h=H)
            S.add("dve", lambda e: e.tensor_tensor(out=a3, in0=s3[:, :, 0:32], in1=cosb, op=ALU.mult),
                  reads=[src_key, cs_key], writes=["r1"])
            S.add("dve", lambda e: e.tensor_tensor(out=b3, in0=s3[:, :, 32:64], in1=sinb, op=ALU.mult),
                  reads=[src_key, cs_key], writes=["r2"])
            S.add("dve", lambda e: e.tensor_tensor(out=c3, in0=s3[:, :, 32:64], in1=cosb, op=ALU.mult),
                  reads=[src_key, cs_key], writes=["r3"])
            S.add("dve", lambda e: e.tensor_tensor(out=e3, in0=s3[:, :, 0:32], in1=sinb, op=ALU.mult),
                  reads=[src_key, cs_key], writes=["r4"])
            S.add("pool", lambda e: e.tensor_tensor(out=d3[:, :, 0:32], in0=a3, in1=b3, op=ALU.subtract),
                  reads=["r1", "r2"], writes=[dst_key + "a"])
            S.add("pool", lambda e: e.tensor_tensor(out=d3[:, :, 32:64], in0=c3, in1=e3, op=ALU.add),
                  reads=["r3", "r4"], writes=[dst_key + "b"])

        gt = [0]
        late2 = [None]

        def load_tile(xsrc, cssrc):
            g = gt[0] % 2
            S.add("sp", lambda e: e.dma_start(out=xt[g][:], in_=xsrc), writes=["xt%d" % g], dma="xt%d" % g)
            S.add("sp", lambda e: e.dma_start(out=cst[g][:], in_=cssrc), writes=["cst%d" % g], dma="cst%d" % g)

        def proc_tile(t, kts_j, mode, qcol=None, orow=None, okv=None, after3=None):
            g = gt[0] % 2
            gt[0] += 1
            xk, ck, xTk = "xt%d" % g, "cst%d" % g, "xT%d" % g
            pk, pv = PB[g], PB[2 + g]
            PSK, PSV = "PB%d" % g, "PB%d" % (2 + g)

            def tr(e):
                for c in range(8):
                    i = e.transpose(PX[:, c * 128:(c + 1) * 128], xt[g][:, c * 128:(c + 1) * 128], ident[:])
                return i
            S.add("pe", tr, reads=[xk, "ident"], writes=["PX"])
            S.add("act", lambda e: e.activation(out=xT[g][:].rearrange("p c t -> p (c t)"), in_=PX[:], func=AF.Copy),
                  reads=["PX"], writes=[xTk])

            def proj(pbank, wt):
                def f(e):
                    for c in range(8):
                        i = e.matmul(pbank[:], xT[g][:, c, :], wt[:, c, :], start=(c == 0), stop=(c == 7))
                    return i
                return f
            S.add("pe", proj(pk, wk), reads=[xTk, "wk"], writes=[PSK])
            S.add("pe", proj(pv, wv), reads=[xTk, "wv"], writes=[PSV])
            if mode != "gen":
                S.add("pe", proj(pq, wq), reads=[xTk, "wq"], writes=[PSQ])
            kg = g
            krk = "kr%d" % kg

            def s2():
                rope(pk, PSK, cst[g], ck, kr[kg], krk)
                if mode != "smp":
                    vsl = Vs[kts_j][:, :, t, 0:64]
                    S.add("act", lambda e: e.activation(out=vsl, in_=pv[:].rearrange("p (h d) -> p h d", h=H), func=AF.Copy),
                          reads=[PSV], writes=["Vs%d" % kts_j])
                if mode != "gen":
                    S.add("act", lambda e: e.activation(out=vf[kg][:], in_=pv[:], func=AF.Copy),
                          reads=[PSV], writes=["vf%d" % kg])
                    S.add("sp", lambda e: e.dma_start(out=okv[0], in_=kr[kg][:]), reads=[krk + "a", krk + "b"], writes=["OUTk%d" % kg], dma="ok%d" % kg)
                    S.add("sp", lambda e: e.dma_start(out=okv[1], in_=vf[kg][:]), reads=["vf%d" % kg], writes=["OUTv%d" % kg], dma="ov%d" % kg)
                if mode == "smp":
                    return
                S.add("act", lambda e: e.activation(out=kb[kg][:], in_=kr[kg][:], func=AF.Copy), reads=[krk + "a", krk + "b"], writes=["kb%d" % kg])
                if late2[0] is not None:
                    late2[0]()
                    late2[0] = None
                S.add("pool", lambda e: e.tensor_tensor(out=sq[:], in0=kr[kg][:], in1=kr[kg][:], op=ALU.mult),
                      reads=[krk + "a", krk + "b"], writes=["sq"])

                def late():
                    S.add("dve", lambda e: e.tensor_reduce(out=ksq[:], in_=sq[:].rearrange("p (h d) -> p h d", h=H), axis=AX.X, op=ALU.add),
                          reads=["sq"], writes=["ksq"])
                    S.add("dve", lambda e: e.tensor_tensor(out=kmx[:], in0=kmx[:], in1=ksq[:], op=ALU.max),
                          reads=["ksq", "kmx"], writes=["kmx"])
                if mode == "gen":
                    late2[0] = late
                else:
                    late()

            def s3():
                if mode == "smp":
                    return
                ptb = pt_[:].bitcast(BF16)

                def trk(e):
                    for h in range(H):
                        i = e.transpose(ptb[0:64, h * 128:(h + 1) * 128], kb[kg][:, h * 64:(h + 1) * 64], identb[:])
                    return i
                S.add("pe", trk, reads=["kb%d" % kg, "identb"], writes=[PST])
                S.add("act", lambda e: e.activation(out=KTs[kts_j][0:64, :, t * 128:(t + 1) * 128],
                                                    in_=ptb[0:64, 0:1024].rearrange("p (h t) -> p h t", h=H), func=AF.Copy),
                      reads=[PST], writes=["KTs%d" % kts_j])
                if after3 is not None:
                    after3()
            if mode == "gen":
                return s2, s3
            s2()
            s3()
            return kg

        qcnt = [0]
        qsq2 = [qsq, sb(esA, [128, H], F32, "qsqb")]

        def q_tile(g_prev, qcol, selrow):
            qi = qcnt[0] % 2
            qcnt[0] += 1
            qsq = qsq2[qi]
            QSQ = "qsq%d" % qi
            rope(pq, PSQ, cst[g_prev], "cst%d" % g_prev, qr, "qr")
            S.add("pool", lambda e: e.tensor_copy(out=qb[:], in_=qr[:]), reads=["qra", "qrb"], writes=["qb"])
            S.add("pool", lambda e: e.tensor_tensor(out=sq[:], in0=qr[:], in1=qr[:], op=ALU.mult),
                  reads=["qra", "qrb"], writes=["sq"])
            S.add("dve", lambda e: e.tensor_reduce(out=qsq[:], in_=sq[:].rearrange("p (h d) -> p h d", h=H), axis=AX.X, op=ALU.add),
                  reads=["sq"], writes=[QSQ])
            ptb = pt_[:].bitcast(BF16)

            def trq(e):
                for h in range(H):
                    i = e.transpose(ptb[0:64, h * 128:(h + 1) * 128], qb[:, h * 64:(h + 1) * 64], identb[:])
                return i
            S.add("pe", trq, reads=["qb", "identb"], writes=[PST])
            S.add("act", lambda e: e.activation(out=QT[0:64, :, qcol:qcol + 128],
                                                in_=ptb[0:64, 0:1024].rearrange("p (h t) -> p h t", h=H), func=AF.Copy),
                  reads=[PST], writes=["QT"])
            if selrow is None:
                return None

            def selpart():
                S.add("sp", lambda e: e.dma_start(out=sel_c[:, 0, :], in_=pastneg[selrow:selrow + 128, :]), writes=["selc0"], dma="selc0")
                S.add("sp", lambda e: e.dma_start(out=sel_c[:, 1, :], in_=notown[selrow:selrow + 128, :]), writes=["selc1"], dma="selc1")
                S.add("sp", lambda e: e.dma_start(out=sel_c[:, 2, :], in_=ownA[selrow:selrow + 128, :]), writes=["selc2"], dma="selc2")
                S.add("sp", lambda e: e.dma_start(out=isab[:], in_=isAB[selrow:selrow + 128, :]), writes=["isab"], dma="isab")

                def scm(e):
                    for h in range(H):
                        i = e.matmul(ps_[:, h * 32:(h + 1) * 32], QT[0:64, h, qcol:qcol + 128], kmTb[:, h, :], start=True, stop=True)
                    return i
                S.add("pe", scm, reads=["QT", "kmTb"], writes=[PSS])
                S.add("dve", lambda e: e.tensor_tensor(out=sc[:], in0=ps_[:, 0:256].rearrange("p (h j) -> p h j", h=H),
                                                       in1=sel_c[:, 0:1, :].to_broadcast([128, H, 32]), op=ALU.add),
                      reads=[PSS, "selc0"], writes=["sc"])

                def mx(e):
                    for h in range(H):
                        i = e.max(out=mx8[:, h, :], in_=sc[:, h, :])
                    return i
                S.add("dve", mx, reads=["sc"], writes=["mx8"])
                S.add("dve", lambda e: e.tensor_tensor(out=sel[:], in0=sc[:], in1=mx8[:, :, 2:3].to_broadcast([128, H, 32]), op=ALU.is_ge),
                      reads=["sc", "mx8"], writes=["sel"])
                S.add("dve", lambda e: e.tensor_scalar(out=tmp3[:], in0=sc[:], scalar1=-1e29, scalar2=None, op0=ALU.is_gt),
                      reads=["sc"], writes=["tmp3"])
                S.add("dve", lambda e: e.tensor_tensor(out=sel[:], in0=sel[:], in1=tmp3[:], op=ALU.mult),
                      reads=["sel", "tmp3"], writes=["sel"])
                S.add("dve", lambda e: e.tensor_tensor(out=tmp3[:], in0=sel[:], in1=sel_c[:, 2:3, :].to_broadcast([128, H, 32]), op=ALU.mult),
                      reads=["sel", "selc2"], writes=["tmp3"])
                S.add("dve", lambda e: e.tensor_reduce(out=selA[:], in_=tmp3[:], axis=AX.X, op=ALU.add),
                      reads=["tmp3"], writes=["selA"])
                S.add("dve", lambda e: e.tensor_tensor(out=sel[:], in0=sel[:], in1=sel_c[:, 1:2, :].to_broadcast([128, H, 32]), op=ALU.mult),
                      reads=["sel", "selc1"], writes=["sel"])
                S.add("dve", lambda e: e.tensor_scalar(out=sel[:, :, 30], in0=selA[:], scalar1=isab[:, 0:1], scalar2=None, op0=ALU.add),
                      reads=["selA", "isab", "sel"], writes=["sel"])
                S.add("dve", lambda e: e.tensor_copy(out=sel[:, :, 31], in_=isab[:, 1:2].to_broadcast([128, H])),
                      reads=["isab", "sel"], writes=["sel"])
                S.add("dve", lambda e: e.tensor_tensor(out=tq[:], in0=qsq[:], in1=kmxb[:], op=ALU.add),
                      reads=[QSQ, "kmxb"], writes=["tq"])
                S.add("dve", lambda e: e.tensor_scalar(out=tq[:], in0=tq[:], scalar1=-0.5, scalar2=BIG, op0=ALU.mult, op1=ALU.add),
                      reads=["tq"], writes=["tq"])
                S.add("dve", lambda e: e.tensor_tensor(out=tmp3[:], in0=sel[:], in1=tq[:, :, None].to_broadcast([128, H, 32]), op=ALU.mult),
                      reads=["sel", "tq"], writes=["tmp3"])
                S.add("dve", lambda e: e.tensor_scalar(out=mbb[:], in0=tmp3[:], scalar1=-BIG, scalar2=None, op0=ALU.add),
                      reads=["tmp3"], writes=["mbb"])

                def trm(e):
                    for h in range(H):
                        i = e.transpose(ptb[64:96, h * 128:(h + 1) * 128], mbb[:, h, :], identb[:])
                    return i
                S.add("pe", trm, reads=["mbb", "identb"], writes=[PST])
                S.add("act", lambda e: e.activation(out=QT[64:96, :, qcol:qcol + 128],
                                                    in_=ptb[64:96, 0:1024].rearrange("p (h t) -> p h t", h=H), func=AF.Copy),
                      reads=[PST], writes=["QT"])

            return selpart

        def finish_block(kts_j, slot, oh_src, gen_s):
            S.add("pool", lambda e: e.dma_start(out=KTs[kts_j][64:96, :, :], in_=oh_src[:, None, :].to_broadcast([32, H, SBK])),
                  writes=["KTs%d" % kts_j], dma="oh%d" % kts_j)
            if gen_s is not None:
                S.add("dve", lambda e: e.tensor_reduce(out=kmT[:, :, 2 * gen_s:2 * gen_s + 2],
                                                       in_=KTs[kts_j][0:64, :, :].rearrange("p h (b k) -> p h b k", b=2),
                                                       axis=AX.X, op=ALU.add),
                      reads=["KTs%d" % kts_j], writes=["kmT"])
            S.add("sp", lambda e: e.dma_start(out=KT_d[:, :, slot * SBK:(slot + 1) * SBK].rearrange("h r k -> r h k"), in_=KTs[kts_j][:]),
                  reads=["KTs%d" % kts_j], writes=["KTd%d" % slot], dma="kst%d" % kts_j)
            S.add("sp", lambda e: e.dma_start(out=V_d[:, slot, :, :, :].rearrange("h p t e -> p h (t e)"), in_=Vs[kts_j][:].rearrange("p h t e -> p h (t e)")),
                  reads=["Vs%d" % kts_j], writes=["Vd%d" % slot], dma="vst%d" % kts_j)

        blk = 16
        tiles = [(s_, t_) for s_ in range(16) for t_ in range(4)]
        pend = []

        def ld(n_):
            s_, t_ = tiles[n_]
            r0_ = s_ * SBK + t_ * 128
            load_tile(xall[r0_:r0_ + 128, :], cs_all[r0_:r0_ + 128, :])
        ld(0)
        for n_, (s_, t_) in enumerate(tiles):
            j_ = s_ % 2
            fin = (lambda j_=j_, s_=s_: finish_block(j_, s_, oh_all[s_], s_)) if t_ == 3 else None
            st = proc_tile(t_, j_, "gen", after3=fin)
            if len(pend) >= 1:
                pend[-1][0]()
            if n_ + 1 < len(tiles):
                ld(n_ + 1)
            if len(pend) >= 2:
                pend[-2][1]()
            pend.append(st)
        pend[-1][0]()
        pend[-2][1]()
        pend[-1][1]()
        if late2[0] is not None:
            late2[0]()
            late2[0] = None

        pmx = pm_
        S.add("pe", lambda e: e.transpose(pmx[0:8, 0:128], kmx[:], ident[:]), reads=["kmx", "ident"], writes=[PSM])
        kmr = sb(esA, [8, 1], F32, "kmr")
        kmd = sb(esA, [8, 8], F32, "kmd")
        S.add("dve", lambda e: e.tensor_reduce(out=kmr[:], in_=pmx[0:8, 0:128], axis=AX.X, op=ALU.max), reads=[PSM], writes=["kmr"])
        S.add("dve", lambda e: e.tensor_scalar(out=kmd[:], in0=ident[0:8, 0:8], scalar1=kmr[:, 0:1], scalar2=None, op0=ALU.mult),
              reads=["kmr", "ident"], writes=["kmd"])
        S.add("pe", lambda e: e.matmul(pmx[:, 0:8], ones_f[0:8, :], kmd[:], start=True, stop=True), reads=["kmd", "ones_f"], writes=[PSM])
        S.add("dve", lambda e: e.tensor_copy(out=kmxb[:], in_=pmx[:, 0:8]), reads=[PSM], writes=["kmxb"])
        S.add("pool", lambda e: e.tensor_copy(out=kmTb[:], in_=kmT[:]), reads=["kmT"], writes=["kmTb"])

        pend_sel = [None]
        for i in range(4):
            j = blk % 2
            blk += 1
            for t in range(4):
                r0 = i * SBK + t * 128
                load_tile(xown[r0:r0 + 128, :], cs_own[r0:r0 + 128, :])
                g_prev = gt[0] % 2
                proc_tile(t, j, "own", okv=(k_own[r0:r0 + 128, :], v_own[r0:r0 + 128, :]))
                sp_ = q_tile(g_prev, r0, r0)
                if pend_sel[0] is not None:
                    pend_sel[0]()
                pend_sel[0] = sp_
            finish_block(j, 16 + i, oh_own, None)
        pend_sel[0]()
        load_tile(xs[:, :], cs_s[:, :])
        g_prev = gt[0] % 2
        kg_s = proc_tile(0, 0, "smp", okv=(k_s[:, :], v_s[:, :]))
        q_tile(g_prev, 2048, None)
        S.add("pool", lambda e: e.tensor_copy(out=qs_f[:], in_=qr[:]), reads=["qra", "qrb"], writes=["qs_f"])
        S.add("pool", lambda e: e.tensor_copy(out=ks_f[:], in_=kr[kg_s][:]), reads=["kr%da" % kg_s, "kr%db" % kg_s], writes=["ks_f"])
        S.add("pool", lambda e: e.tensor_copy(out=vs_f[:], in_=vf[kg_s][:]), reads=["vf%d" % kg_s], writes=["vs_f"])
        S.barrier()
        esA.close()

        esP = contextlib.ExitStack()
        ck16 = cache_k.rearrange("(n r) d -> n (r d)", r=64)
        ptT_i = sb(esP, [128, NSEQ], I32, "ptT_i")
        ptf = sb(esP, [128, NSEQ], F32, "ptf")
        idxc = sb(esP, [128, NSEQ, 16], I32, "idxc")
        pgs4 = sb(esP, [128, NSEQ, AW], F32, "pgs4")
        pg1 = [sb(esP, [128, AW], F32, "pg1") for _ in range(2)]
        chb = [sb(esP, [128, 4096], F32, "ch") for _ in range(2)]
        if not DEV_SKIP_DECODE:
            S.add("sp", lambda e: e.dma_start(out=ptT_i[:], in_=ptab.rearrange("s p -> p s"), allow_slow_non_contiguous=True), writes=["ptT_i"], dma="d_pt")
            S.add("dve", lambda e: e.tensor_copy(out=ptf[:], in_=ptT_i[:]), reads=["ptT_i"], writes=["ptf"])
            for c in range(16):
                S.add("dve", lambda e, c=c: e.tensor_scalar(out=idxc[:, :, c], in0=ptf[:], scalar1=16.0, scalar2=float(c), op0=ALU.mult, op1=ALU.add),
                      reads=["ptf"], writes=["idxc"])
            S.add("pool", lambda e: e.memset(pgs4[:], 0.0), writes=["pgs4"])

        def ps_dma(k):
            s_, c_ = k // 16, k % 16
            j = k % 2
            S.add("pool", lambda e: e.indirect_dma_start(
                out=chb[j][:], out_offset=None, in_=ck16[:, :], in_offset=bass.IndirectOffsetOnAxis(ap=idxc[:, s_, c_:c_ + 1], axis=0)),
                reads=["idxc"], writes=["ch%d" % j], dma="d_ch%d" % j)

        def ps_red(k):
            return

        def ps_add(k):
            s_ = k // 16
            j = k % 2
            for r_ in range(8):
                S.add("pool", lambda e, r_=r_: e.tensor_tensor(out=pgs4[:, s_, :], in0=pgs4[:, s_, :], in1=chb[j][:, r_ * 512:(r_ + 1) * 512], op=ALU.add),
                      reads=["pgs4", "ch%d" % j], writes=["pgs4"])

        def ps_step(k):
            if DEV_SKIP_DECODE:
                return
            if 0 <= k < 64:
                ps_dma(k)
            if 0 <= k - 1 < 64:
                ps_red(k - 1)
            if 0 <= k - 1 < 64:
                ps_add(k - 1)

        esB = contextlib.ExitStack()
        NKB = 4
        Kc = [sb(esB, [96, SBK], BF16, "Kc") for _ in range(NKB)]
        Vc = [sb(esB, [128, 4, 66], BF16, "Vc") for _ in range(NKB)]
        Pt = [sb(esB, [128, SBK], BF16, "Pt") for _ in range(3)]
        rs = sb(esB, [128, 4], F32, "rs")
        sbanks = [(PB[0], "PB0"), (PB[1], "PB1"), (PB[2], "PB2")]
        accs = [(PX[:, 0:512], "PXa"), (PX[:, 512:1024], "PXb")]
        groups = []
        hi = 0
        for i in range(4):
            slots = list(range(KMAX[i])) + [16 + i]
            for h in range(H):
                for si, slot in enumerate(slots):
                    groups.append((i, h, si, slot, len(slots), hi))
                hi += 1

        def gload(gi_):
            i, h, si, slot, ns, hidx = groups[gi_]
            b = gi_ % NKB
            S.add("sp", lambda e: e.dma_start(out=Kc[b][:], in_=KT_d[h, :, slot * SBK:(slot + 1) * SBK]),
                  reads=["KTd%d" % slot], writes=["Kc%d" % b], dma="kc%d" % b)
            S.add("sp", lambda e: e.dma_start(out=Vc[b][:].rearrange("p t e -> p (t e)"), in_=V_d[h, slot, :, :, :].rearrange("p t e -> p (t e)")),
                  reads=["Vd%d" % slot], writes=["Vc%d" % b], dma="vc%d" % b)

        backs = []

        def front(gi_, kt, n_):
            i, h, si, slot, ns, hidx = groups[gi_]
            b = gi_ % NKB
            sbk, sbkk = sbanks[n_ % 3]
            p_i = n_ % 3
            acc, acck = accs[hidx % 2]
            acc3 = acc.rearrange("p (q e) -> p q e", q=4)
            S.add("pe", lambda e: e.matmul(sbk[:], Kc[b][:, kt * 128:(kt + 1) * 128], QT[:, h, i * SBK:(i + 1) * SBK], start=True, stop=True),
                  reads=["Kc%d" % b, "QT"], writes=[sbkk])
            S.add("act", lambda e: e.activation(out=Pt[p_i][:], in_=sbk[:], func=AF.Exp, scale=0.125),
                  reads=[sbkk], writes=["Pt%d" % p_i])
            if slot >= 16:
                S.add("dve", lambda e: e.tensor_tensor(out=Pt[p_i][:], in0=Pt[p_i][:], in1=trib[:, kt, :], op=ALU.mult),
                      reads=["Pt%d" % p_i, "trib"], writes=["Pt%d" % p_i])
            first = (si == 0 and kt == 0)
            last = (si == ns - 1 and kt == 3)

            def back():
                def pv_(e):
                    for qt in range(4):
                        i_ = e.matmul(acc3[:, qt, 0:65], Pt[p_i][:, qt * 128:(qt + 1) * 128], Vc[b][:, kt, 0:65],
                                      start=(first and qt == 0), stop=last, skip_group_check=True)
                    return i_
                S.add("pe", pv_, reads=["Pt%d" % p_i, "Vc%d" % b], writes=[acck])
                if last:
                    S.add("dve", lambda e: e.reciprocal(out=rs[:], in_=acc3[:, :, 64]), reads=[acck], writes=["rs"])
                    for qt in range(4):
                        S.add("dve", lambda e, qt=qt: e.tensor_scalar(
                            out=att[:, i * 4 + qt, h * 64:(h + 1) * 64], in0=acc3[:, qt, 0:64], scalar1=rs[:, qt:qt + 1], scalar2=None, op0=ALU.mult),
                            reads=[acck, "rs"], writes=["att"])
            return back

        gload(0)
        gload(1)
        n_ = 0
        for gi_ in range(len(groups)):
            if gi_ + 2 < len(groups):
                gload(gi_ + 2)
            if gi_ % 4 == 0:
                ps_step(gi_ // 4)
            for kt in range(4):
                backs.append(front(gi_, kt, n_))
                if n_ >= 2:
                    backs[n_ - 2]()
                n_ += 1
        backs[n_ - 2]()
        backs[n_ - 1]()
        S.barrier()
        esB.close()

        if DEBUG:
            S.add("pool", lambda e: e.dma_start(out=dbg_att[:, :, :], in_=att[:]), reads=["att"], writes=["OUTdbga"], dma="dbga")
            for hh_ in range(H):
                S.add("pool", lambda e, hh_=hh_: e.dma_start(out=dbg_qt[:, hh_, :], in_=QT[:, hh_, :]), reads=["QT"], writes=["OUTdbgq%d" % hh_], dma="dbgq")
        esD = contextlib.ExitStack()
        if not DEV_SKIP_DECODE:
            decode_attention(nc, S, esD, sb, locals())
        else:
            S.add("pool", lambda e: e.memset(att[:, 16, :], 0.0), writes=["att"])
        S.barrier()
        esD.close()
        esP.close()
        esQ.close()

        phase_front(nc, S, sb, locals())
        S.barrier()
        phase_tail(nc, S, sb, locals())

        if DEBUG:
            for u_ in range(5):
                S.add("pool", lambda e, u_=u_: e.dma_start(out=dbg_mg[u_], in_=mg_d[u_]), reads=["mgd%d" % u_], writes=["OUTdbgm%d" % u_], dma="dbgm")
        outs = [k for k in S.res if k.startswith("OUT")]
        S.add("sp", lambda e: e.nop(), reads=outs)
        with nc.Block() as block:
            S.emit(block)
    return nc


def decode_attention(nc, S, es_, sb, L):
    cache_k, cache_v, ptab = L["cache_k"], L["cache_v"], L["ptab"]
    pairm, iota128, hsel, pofs = L["pairm"], L["iota128"], L["hsel"], L["pofs"]
    PX, PB, ident, ones_f, att = L["PX"], L["PB"], L["ident"], L["ones_f"], L["att"]
    qs_f, ks_f, vs_f = L["qs_f"], L["ks_f"], L["vs_f"]
    pgs4 = L["pgs4"]
    selS = sb(es_, [128, NSEQ, 128], F32, "selS")
    pair_sb = sb(es_, [128, 64], F32, "pair_sb")
    iota8 = sb(es_, [8, 128], F32, "iota8")
    hsel_sb = sb(es_, [8, 48], F32, "hsel_sb")
    pofs_sb = sb(es_, [128, 48], F32, "pofs_sb")
    ptr_i = sb(es_, [8, 128], I32, "ptr_i")
    ptr_f = sb(es_, [8, 128], F32, "ptr_f")
    qbc = sb(es_, [128, AW], F32, "qbc")
    vbc = sb(es_, [128, AW], F32, "vbc")
    tmpq = sb(es_, [128, AW], F32, "tmpq")
    spg = sb(es_, [128, H], F32, "spg")
    ssa = sb(es_, [128, H], F32, "ssa")
    sbc = sb(es_, [128, H], F32, "sbc")
    bsc = sb(es_, [8, 64], F32, "bsc")
    mx8 = sb(es_, [8, 8], F32, "dmx8")
    ix8 = sb(es_, [8, 8], U32, "dix8")
    ixf = sb(es_, [8, 8], F32, "dixf")
    lp = sb(es_, [8, 6], F32, "lp")
    eqt = sb(es_, [8, 128], F32, "eqt")
    phys = sb(es_, [8, 6], F32, "phys")
    physd = sb(es_, [8, 48], F32, "physd")
    gidx = sb(es_, [128, 48], I32, "gidx")
    Kg = sb(es_, [128, H, 6, 64], F32, "Kg")
    Vg = sb(es_, [128, H, 6, 65], F32, "Vg")
    tmpk = sb(es_, [128, H, 6, 64], F32, "tmpk")
    sk = sb(es_, [128, 48], F32, "sk")
    m48 = sb(es_, [48, 1], F32, "m48")
    mh = sb(es_, [1, H], F32, "mh")
    mb = sb(es_, [128, H], F32, "mb")
    Pk = sb(es_, [128, 48], F32, "Pk")
    pself = sb(es_, [128, H], F32, "pself")
    orow = sb(es_, [1, H, 65], F32, "orow")
    den = sb(es_, [1, H], F32, "den")
    arow = sb(es_, [1, AW], F32, "arow")

    S.add("pool", lambda e: e.memset(att[:, 16, :], 0.0), writes=["att"])
    S.add("sp", lambda e: e.dma_start(out=pair_sb[:], in_=pairm[:, :]), writes=["pair_sb"], dma="d_c1")
    S.add("sp", lambda e: e.dma_start(out=iota8[:], in_=iota128[:, :]), writes=["iota8"], dma="d_c2")
    S.add("sp", lambda e: e.dma_start(out=hsel_sb[:], in_=hsel[:, :]), writes=["hsel_sb"], dma="d_c3")
    S.add("sp", lambda e: e.dma_start(out=pofs_sb[:], in_=pofs[:, :]), writes=["pofs_sb"], dma="d_c4")
    for s in range(NSEQ):
        S.add("dve", lambda e, s=s: e.tensor_copy(out=selS[:, s, :], in_=ident[:, s:s + 1].to_broadcast([128, 128])), reads=["ident"], writes=["selS"])
    S.add("pool", lambda e: e.memset(Vg[:], 1.0), writes=["Vg"])
    S.add("dve", lambda e: e.tensor_tensor(out=tmpq[:], in0=qs_f[:], in1=ks_f[:], op=ALU.mult), reads=["qs_f", "ks_f"], writes=["tmpq"])
    S.add("dve", lambda e: e.tensor_reduce(out=ssa[:], in_=tmpq[:].rearrange("p (h d) -> p h d", h=H), axis=AX.X, op=ALU.add), reads=["tmpq"], writes=["ssa"])
    gi = 0
    for s in range(NSEQ):
        S.add("pe", lambda e, s=s: e.matmul(PB[0][:], selS[:, s, :], qs_f[:], start=True, stop=True), reads=["selS", "qs_f"], writes=["PB0"])
        S.add("act", lambda e: e.activation(out=qbc[:], in_=PB[0][:], func=AF.Copy), reads=["PB0"], writes=["qbc"])
        S.add("pe", lambda e, s=s: e.matmul(PB[1][:], selS[:, s, :], vs_f[:], start=True, stop=True), reads=["selS", "vs_f"], writes=["PB1"])
        S.add("act", lambda e: e.activation(out=vbc[:], in_=PB[1][:], func=AF.Copy), reads=["PB1"], writes=["vbc"])
        S.add("pe", lambda e, s=s: e.matmul(PB[2][:, 0:H], selS[:, s, :], ssa[:], start=True, stop=True), reads=["selS", "ssa"], writes=["PB2"])
        S.add("act", lambda e: e.activation(out=sbc[:], in_=PB[2][:, 0:H], func=AF.Copy), reads=["PB2"], writes=["sbc"])
        S.add("dve", lambda e, s=s: e.tensor_tensor(out=tmpq[:], in0=pgs4[:, s, :], in1=qbc[:], op=ALU.mult), reads=["pgs4", "qbc"], writes=["tmpq"])
        S.add("dve", lambda e: e.tensor_reduce(out=spg[:], in_=tmpq[:].rearrange("p (h d) -> p h d", h=H), axis=AX.X, op=ALU.add), reads=["tmpq"], writes=["spg"])
        S.add("pe", lambda e: e.matmul(PB[3][0:8, 0:64], spg[:], pair_sb[:], start=True, stop=True), reads=["spg", "pair_sb"], writes=["PB3"])
        S.add("dve", lambda e: e.tensor_copy(out=bsc[:], in_=PB[3][0:8, 0:64]), reads=["PB3"], writes=["bsc"])
        S.add("dve", lambda e: e.max(out=mx8[:], in_=bsc[:]), reads=["bsc"], writes=["dmx8"])
        S.add("dve", lambda e: e.max_index(out=ix8[:], in_max=mx8[:], in_values=bsc[:]), reads=["dmx8", "bsc"], writes=["dix8"])
        S.add("dve", lambda e: e.tensor_copy(out=ixf[:], in_=ix8[:]), reads=["dix8"], writes=["dixf"])
        lp3 = lp[:].rearrange("p (k e) -> p k e", e=2)
        for e2 in range(2):
            S.add("dve", lambda e, e2=e2: e.tensor_scalar(out=lp3[:, :, e2], in0=ixf[:, 0:3], scalar1=2.0, scalar2=float(e2), op0=ALU.mult, op1=ALU.add),
                  reads=["dixf"], writes=["lp"])
        S.add("sp", lambda e, s=s: e.dma_start(out=ptr_i[:], in_=ptab[s:s + 1, :].to_broadcast([8, NPAGE])), writes=["ptr_i"], dma="d_ptr")
        S.add("dve", lambda e: e.tensor_copy(out=ptr_f[:], in_=ptr_i[:]), reads=["ptr_i"], writes=["ptr_f"])
        for sl in range(6):
            S.add("dve", lambda e, sl=sl: e.scalar_tensor_tensor(out=eqt[:], in0=iota8[:], scalar=lp[:, sl:sl + 1], in1=ptr_f[:],
                                                                 op0=ALU.is_equal, op1=ALU.mult, accum_out=phys[:, sl:sl + 1]),
                  reads=["iota8", "lp", "ptr_f"], writes=["eqt", "phys"])
        S.add("dve", lambda e: e.tensor_tensor(out=physd[:].rearrange("p (h k) -> p h k", h=H), in0=hsel_sb[:].rearrange("p (h k) -> p h k", h=H),
                                               in1=phys[:, None, :].to_broadcast([8, H, 6]), op=ALU.mult),
              reads=["hsel_sb", "phys"], writes=["physd"])
        S.add("pe", lambda e: e.matmul(PB[4][:, 0:48], ones_f[0:8, :], physd[:], start=True, stop=True), reads=["ones_f", "physd"], writes=["PB4"])
        S.add("dve", lambda e: e.scalar_tensor_tensor(out=gidx[:], in0=PB[4][:, 0:48], scalar=1024.0, in1=pofs_sb[:], op0=ALU.mult, op1=ALU.add),
              reads=["PB4", "pofs_sb"], writes=["gidx"])
        for h in range(H):
            for sl in range(6):
                col = h * 6 + sl
                S.add("pool", lambda e, h=h, sl=sl, col=col: e.indirect_dma_start(
                    out=Kg[:, h, sl, :], out_offset=None, in_=cache_k[:, :], in_offset=bass.IndirectOffsetOnAxis(ap=gidx[:, col:col + 1], axis=0)),
                    reads=["gidx"], writes=["Kg%d" % col], dma="d_kg%d" % (col % 8))
        for h in range(H):
            for sl in range(6):
                col = h * 6 + sl
                S.add("pool", lambda e, h=h, sl=sl, col=col: e.indirect_dma_start(
                    out=Vg[:, h, sl, 0:64], out_offset=None, in_=cache_v[:, :], in_offset=bass.IndirectOffsetOnAxis(ap=gidx[:, col:col + 1], axis=0)),
                    reads=["gidx"], writes=["Vg%d" % col], dma="d_vg%d" % (col % 8))
        kgk = ["Kg%d" % c_ for c_ in range(48)]
        vgk = ["Vg%d" % c_ for c_ in range(48)]
        S.add("dve", lambda e: e.tensor_tensor(out=tmpk[:], in0=Kg[:], in1=qbc[:].rearrange("p (h d) -> p h d", h=H)[:, :, None, :].to_broadcast([128, H, 6, 64]), op=ALU.mult),
              reads=kgk + ["qbc"], writes=["tmpk"])
        S.add("dve", lambda e: e.tensor_reduce(out=sk[:], in_=tmpk[:].rearrange("p h k d -> p (h k) d"), axis=AX.X, op=ALU.add), reads=["tmpk"], writes=["sk"])
        S.add("pe", lambda e: e.transpose(PB[5][0:48, 0:128], sk[:], ident[:]), reads=["sk", "ident"], writes=["PB5"])
        S.add("dve", lambda e: e.tensor_reduce(out=m48[:], in_=PB[5][0:48, 0:128], axis=AX.X, op=ALU.max), reads=["PB5"], writes=["m48"])
        S.add("pe", lambda e: e.transpose(PB[5][0:1, 0:48], m48[:], ident[0:48, 0:48]), reads=["m48", "ident"], writes=["PB5"])
        S.add("dve", lambda e: e.tensor_reduce(out=mh[:], in_=PB[5][0:1, 0:48].rearrange("p (h k) -> p h k", h=H), axis=AX.X, op=ALU.max), reads=["PB5"], writes=["mh"])
        S.add("dve", lambda e: e.tensor_tensor(out=mh[:], in0=mh[:], in1=sbc[0:1, :], op=ALU.max), reads=["mh", "sbc"], writes=["mh"])
        S.add("pe", lambda e: e.matmul(PB[5][:, 0:H], ones_f[0:1, :], mh[:], start=True, stop=True), reads=["ones_f", "mh"], writes=["PB5"])
        S.add("dve", lambda e: e.tensor_copy(out=mb[:], in_=PB[5][:, 0:H]), reads=["PB5"], writes=["mb"])
        S.add("dve", lambda e: e.tensor_tensor(out=sk[:].rearrange("p (h k) -> p h k", h=H), in0=sk[:].rearrange("p (h k) -> p h k", h=H),
                                               in1=mb[:, :, None].to_broadcast([128, H, 6]), op=ALU.subtract), reads=["sk", "mb"], writes=["sk"])
        S.add("act", lambda e: e.activation(out=Pk[:], in_=sk[:], func=AF.Exp, scale=0.125), reads=["sk"], writes=["Pk"])
        S.add("dve", lambda e: e.tensor_tensor(out=pself[:], in0=sbc[:], in1=mb[:], op=ALU.subtract), reads=["sbc", "mb"], writes=["pself"])
        S.add("act", lambda e: e.activation(out=pself[:], in_=pself[:], func=AF.Exp, scale=0.125), reads=["pself"], writes=["pself"])

        def pv(e):
            for h in range(H):
                for sl in range(6):
                    i = e.matmul(PX[0:1, h * 128:h * 128 + 65], Pk[:, h * 6 + sl:h * 6 + sl + 1], Vg[:, h, sl, :], start=(sl == 0), stop=(sl == 5),
                                 skip_group_check=True)
            return i
        S.add("pe", pv, reads=["Pk"] + vgk, writes=["PX"])
        o3 = PX[0:1, :].rearrange("p (h e) -> p h e", h=H)
        S.add("dve", lambda e: e.tensor_tensor(out=orow[:, :, 0:64], in0=vbc[0:1, :].rearrange("p (h d) -> p h d", h=H),
                                               in1=pself[0:1, :, None].to_broadcast([1, H, 64]), op=ALU.mult), reads=["vbc", "pself"], writes=["orow"])
        S.add("dve", lambda e: e.tensor_tensor(out=orow[:, :, 0:64], in0=orow[:, :, 0:64], in1=o3[:, :, 0:64], op=ALU.add), reads=["orow", "PX"], writes=["orow"])
        S.add("dve", lambda e: e.tensor_tensor(out=den[:], in0=pself[0:1, :], in1=o3[:, :, 64], op=ALU.add), reads=["pself", "PX"], writes=["den"])
        S.add("dve", lambda e: e.reciprocal(out=den[:], in_=den[:]), reads=["den"], writes=["den"])
        S.add("dve", lambda e: e.tensor_tensor(out=arow[:].rearrange("p (h d) -> p h d", h=H), in0=orow[:, :, 0:64],
                                               in1=den[:, :, None].to_broadcast([1, H, 64]), op=ALU.mult), reads=["orow", "den"], writes=["arow"])
        S.add("pool", lambda e, s=s: e.dma_start(out=att[s:s + 1, 16, :], in_=arow[:]), reads=["arow", "att"], writes=["att"], dma="d_att")


def _rope_tab(pos):
    half = 32
    inv = (10000.0 ** (-np.arange(half, dtype=np.float32) / half)).astype(np.float32)
    ang = pos.astype(np.float32)[:, None] * inv[None, :]
    return np.concatenate([np.cos(ang), np.sin(ang)], axis=1).astype(np.float32)


_NC = None


def kernel(x_prompt, x_sample, cache_k, cache_v, state_pool, page_table, p_prompt, p_sample,
           w_in, w_pool_mix, pool_scale, w_pool_out, w_att_out, w_o, ln_g, ln_b, w_ple, w_ple_gate):
    global _NC
    f = lambda a: np.ascontiguousarray(np.asarray(a, dtype=np.float32))
    x_prompt = f(x_prompt); x_sample = f(x_sample); p_prompt = f(p_prompt); p_sample = f(p_sample)
    ck = f(cache_k).reshape(NPHYS * 128 * 8, 64)
    cv = f(cache_v).reshape(NPHYS * 128 * 8, 64)
    state_pool = f(state_pool)
    page_table = np.ascontiguousarray(np.asarray(page_table, dtype=np.int32))
    cs_all = _rope_tab(np.arange(SEQ))
    cs_s = _rope_tab(np.full((128,), PAST))
    ident = np.eye(128, dtype=np.float32)
    oh_all = np.zeros((16, 32, SBK), np.float32)
    for s in range(15):
        oh_all[s, 2 * s, :256] = 1.0
        oh_all[s, 2 * s + 1, 256:] = 1.0
    oh_own = np.zeros((32, SBK), np.float32)
    oh_own[30, :256] = 1.0
    oh_own[31, 256:] = 1.0
    tri = np.ones((128, 4, SBK), np.float32)
    for kt in range(4):
        for k in range(128):
            kk = kt * 128 + k
            kb_, kl = kk // 256, kk % 256
            q = np.arange(SBK)
            same = (q // 256) == kb_
            tri[k, kt, :] = np.where(same & ((q % 256) < kl), 0.0, 1.0)
    tri = tri.reshape(128, 4 * SBK)
    pairm = np.zeros((128, 64), np.float32)
    pairm[np.arange(128), np.arange(128) // 2] = 1.0
    iota128 = np.tile(np.arange(128, dtype=np.float32)[None, :], (8, 1))
    hsel = np.zeros((8, 48), np.float32)
    for h in range(8):
        hsel[h, h * 6:(h + 1) * 6] = 1.0
    shared = dict(cs_all=cs_all, cs_s=cs_s, oh_all=oh_all, oh_own=oh_own, tri=tri, ident=ident,
                  w_in=f(w_in)[0], w_mix=f(w_pool_mix)[0], pscale=f(pool_scale)[0], w_po=f(w_pool_out)[0],
                  w_ao=f(w_att_out)[0], w_o=f(w_o)[0], ln_g=f(ln_g)[0], ln_b=f(ln_b)[0], w_ple=f(w_ple)[0],
                  w_pg=f(w_ple_gate)[0], cache_k=ck, cache_v=cv, pairm=pairm, iota128=iota128, hsel=hsel,
                  pofs=(np.arange(128, dtype=np.float32)[:, None] * 8 + np.repeat(np.arange(8, dtype=np.float32), 6)[None, :]).astype(np.float32))
    in_maps = []
    for c in range(NCORES):
        b, r = c // 4, c % 4
        sbs = own_sbs(r)
        tok = np.concatenate([np.arange(s * SBK, (s + 1) * SBK) for s in sbs])
        xown = x_prompt[b][tok]
        xhalo = np.zeros((16, 16, D), np.float32)
        corr = np.ones((16, 4, 16), np.float32)
        for ti in range(16):
            t0 = tok[ti * 128]
            for k in range(16):
                p = t0 - 16 + k
                if p >= 0:
                    xhalo[ti, k] = x_prompt[b, p]
            for g, w in enumerate((2, 4, 8, 16)):
                for k in range(16):
                    corr[ti, g, k] = w / min(t0 + k + 1, w)
        qblk = tok // 256
        jj = np.arange(32)[None, :]
        pastneg = np.where(jj < qblk[:, None], 0.0, -1e30).astype(np.float32)
        sbq = tok // SBK
        notown = np.where((jj // 2) == sbq[:, None], 0.0, 1.0).astype(np.float32)
        ownA = (jj == (2 * sbq)[:, None]).astype(np.float32)
        inA = ((tok % SBK) < 256)
        isAB = np.stack([inA, ~inA], axis=1).astype(np.float32)
        xs = np.zeros((128, D), np.float32); xs[:NSEQ] = x_sample[c * NSEQ:(c + 1) * NSEQ, 0]
        pss = np.zeros((128, PLE), np.float32); pss[:NSEQ] = p_sample[0, c * NSEQ:(c + 1) * NSEQ, 0]
        m = dict(shared)
        m.update(xall=x_prompt[b], xown=np.ascontiguousarray(xown), xhalo=xhalo, pown=np.ascontiguousarray(p_prompt[0, b][tok]),
                 xs=xs, pss=pss, cs_own=np.ascontiguousarray(cs_all[tok]), pastneg=pastneg, notown=notown, ownA=ownA,
                 isAB=isAB, corr=corr, ptab=np.ascontiguousarray(page_table[c * NSEQ:(c + 1) * NSEQ]),
                 spool=np.ascontiguousarray(state_pool[0, c * NSEQ:(c + 1) * NSEQ]))
        in_maps.append(m)
    if _NC is None:
        _NC = build_nc()
    res = run_bass_kernel_spmd(_NC, in_maps, core_ids=list(range(NCORES)))
    R = res.results
    global _last
    _last = R
    y_prompt = np.zeros((2, SEQ, D), np.float32)
    k_prompt = np.zeros((1, 2, SEQ, H, HD), np.float32)
    v_prompt = np.zeros((1, 2, SEQ, H, HD), np.float32)
    pool_prompt = np.zeros((1, 2, 15, 512), np.float32)
    y_sample = np.zeros((32, 1, D), np.float32)
    k_sample = np.zeros((1, 32, 1, H, HD), np.float32)
    v_sample = np.zeros((1, 32, 1, H, HD), np.float32)
    pool_sample = np.zeros((1, 32, 15, 512), np.float32)
    for c in range(NCORES):
        b, r = c // 4, c % 4
        tok = np.concatenate([np.arange(s * SBK, (s + 1) * SBK) for s in own_sbs(r)])
        y_prompt[b, tok] = R[c]["y_own"]
        k_prompt[0, b, tok] = R[c]["k_own"].reshape(2048, H, HD)
        v_prompt[0, b, tok] = R[c]["v_own"].reshape(2048, H, HD)
        if r == 0:
            pool_prompt[0, b] = R[c]["poolp"]
        y_sample[c * NSEQ:(c + 1) * NSEQ, 0] = R[c]["y_s"][:NSEQ]
        k_sample[0, c * NSEQ:(c + 1) * NSEQ, 0] = R[c]["k_s"][:NSEQ].reshape(NSEQ, H, HD)
        v_sample[0, c * NSEQ:(c + 1) * NSEQ, 0] = R[c]["v_s"][:NSEQ].reshape(NSEQ, H, HD)
        pool_sample[0, c * NSEQ:(c + 1) * NSEQ] = R[c]["pool_s"]
    return (y_prompt, y_sample, k_prompt, v_prompt, pool_prompt, k_sample, v_sample, pool_sample)


def _silu_from_psum(S, ps_ap, ps_key, th, th_key, out_ap, out_key, extra_in1=None, extra_key=None):
    S.add("act", lambda e: e.activation(out=th, in_=ps_ap, func=AF.Tanh, scale=0.5), reads=[ps_key], writes=[th_key])
    S.add("dve", lambda e: e.tensor_scalar(out=th, in0=th, scalar1=0.5, scalar2=0.5, op0=ALU.mult, op1=ALU.add),
          reads=[th_key], writes=[th_key])
    if extra_in1 is None:
        S.add("dve", lambda e: e.tensor_tensor(out=out_ap, in0=ps_ap, in1=th, op=ALU.mult),
              reads=[ps_key, th_key], writes=[out_key])
    else:
        S.add("dve", lambda e: e.tensor_tensor(out=th, in0=ps_ap, in1=th, op=ALU.mult),
              reads=[ps_key, th_key], writes=[th_key])
        S.add("pool", lambda e: e.tensor_tensor(out=out_ap, in0=th, in1=extra_in1, op=ALU.mult),
              reads=[th_key, extra_key], writes=[out_key])


def phase_front(nc, S, sb, L):
    w_in, w_mix, pscale, w_po, w_ao = L["w_in"], L["w_mix"], L["pscale"], L["w_po"], L["w_ao"]
    xown, xs, xhalo, corr, spool = L["xown"], L["xs"], L["xhalo"], L["corr"], L["spool"]
    poolp, pool_s, mg_d = L["poolp"], L["pool_s"], L["mg_d"]
    PX, PB, ident, identb, att = L["PX"], L["PB"], L["ident"], L["identb"], L["att"]
    es_ = contextlib.ExitStack()
    L["esF"] = es_

    def wload(name, src_ap, shape):
        t = sb(es_, shape, BF16, name)
        S.add("pool", lambda e: e.dma_start(out=t[:], in_=src_ap), writes=[name], dma="w_" + name)
        return t
    wzb = wload("wzb", w_in[:, 1536:2048].rearrange("(c p) n -> p c n", p=128), [128, 8, 512])
    wu = wload("wu", w_in[:, 2048:2560].rearrange("(c p) n -> p c n", p=128), [128, 8, 512])
    wza = wload("wza", w_in[:, 2560:3072].rearrange("(c p) n -> p c n", p=128), [128, 8, 512])
    wga = wload("wga", w_in[:, 3072:4096].rearrange("(c p) n -> p c n", p=128), [128, 8, 1024])
    wgb = wload("wgb", w_in[:, 4096:5120].rearrange("(c p) n -> p c n", p=128), [128, 8, 1024])
    wmix = wload("wmix", w_mix.rearrange("g c e -> c g e"), [128, 4, 128])
    wpo = wload("wpo", w_po.rearrange("(c p) n -> p c n", p=128), [128, 4, 1024])
    wao = wload("wao", w_ao.rearrange("(c p) n -> p c n", p=128), [128, 4, 1024])
    psc = sb(es_, [128, 4], F32, "psc")
    S.add("sp", lambda e: e.dma_start(out=psc[:], in_=pscale.rearrange("(g c) -> c g", g=4), allow_slow_non_contiguous=True),
          writes=["psc"], dma="psc")
    corb = sb(es_, [128, 16 * 4 * 16], F32, "corb")
    S.add("sp", lambda e: e.dma_start(out=corb[:], in_=corr.rearrange("a g k -> (a g k)")[None, :].to_broadcast([128, 1024])),
          writes=["corb"], dma="corb")
    cor4 = corb[:].rearrange("p (a g k) -> p a g k", a=16, g=4)

    xt = sb(es_, [128, D], F32, "fxt")
    xh = sb(es_, [16, D], F32, "fxh")
    xTu = sb(es_, [128, 8, SBK], BF16, "xTu")
    xTh = sb(es_, [128, 8, 16], BF16, "xTh")
    uext = sb(es_, [128, 4, 16 + SBK], F32, "uext")
    tA = sb(es_, [128, 16 + SBK], F32, "tA")
    tB = sb(es_, [128, 16 + SBK], F32, "tB")
    dT = sb(es_, [128, 4, SBK], BF16, "dT")
    th = sb(es_, [128, SBK], F32, "th")
    szT = sb(es_, [128, 4, SBK], BF16, "szT")
    pzT = sb(es_, [128, 4, SBK], BF16, "pzT")
    azb = sb(es_, [128, AW], BF16, "azb")
    azT = sb(es_, [128, 4, SBK], BF16, "azT")
    tg1 = sb(es_, [128, SBK], F32, "tg1")
    tg2 = sb(es_, [128, SBK], F32, "tg2")
    t1 = sb(es_, [128, SBK], F32, "t1")
    t2 = sb(es_, [128, SBK], F32, "t2")
    mgT = sb(es_, [128, 8, SBK], BF16, "mgT")
    hsb = sb(es_, [16, 4, 512], F32, "hsb")
    hT = sb(es_, [128, 4, 4, 16], F32, "hT")
    ssum = sb(es_, [128, 4], F32, "ssum")
    pout = sb(es_, [16, 512], F32, "pout")

    for u in range(5):
        T = SBK if u < 4 else 128
        nt = T // 128
        for t in range(nt):
            src = xown[u * SBK + t * 128:u * SBK + (t + 1) * 128, :] if u < 4 else xs[:, :]
            S.add("sp", lambda e, src=src: e.dma_start(out=xt[:], in_=src), writes=["fxt"], dma="fxt")

            def tr(e):
                for c in range(8):
                    i = e.transpose(PX[:, c * 128:(c + 1) * 128], xt[:, c * 128:(c + 1) * 128], ident[:])
                return i
            S.add("pe", tr, reads=["fxt", "ident"], writes=["PX"])
            S.add("act", lambda e, t=t: e.activation(out=xTu[:, :, t * 128:(t + 1) * 128],
                                                     in_=PX[:].rearrange("p (c t) -> p c t", c=8), func=AF.Copy),
                  reads=["PX"], writes=["xTu"])
        if u < 4:
            S.add("sp", lambda e, u=u: e.dma_start(out=xh[:], in_=xhalo[u * 4, :, :]), writes=["fxh"], dma="fxh")

            def trh(e):
                for c in range(8):
                    i = e.transpose(PX[:, c * 16:(c + 1) * 16], xh[:, c * 128:(c + 1) * 128], ident[0:16, 0:16])
                return i
            S.add("pe", trh, reads=["fxh", "ident"], writes=["PX"])
            S.add("act", lambda e: e.activation(out=xTh[:].rearrange("p c k -> p (c k)"), in_=PX[:, 0:128], func=AF.Copy),
                  reads=["PX"], writes=["xTh"])
        for g in range(4):
            pb, pbk = PB[g % 2], "PB%d" % (g % 2)

            def mu(e, g=g, pb=pb, T=T, u=u):
                for c in range(8):
                    i = e.matmul(pb[:, 0:T], wu[:, c, g * 128:(g + 1) * 128], xTu[:, c, 0:T], start=(c == 0), stop=(c == 7))
                return i
            S.add("pe", mu, reads=["wu", "xTu"], writes=[pbk])
            S.add("act", lambda e, g=g, pb=pb, T=T: e.activation(out=uext[:, g, 16:16 + T], in_=pb[:, 0:T], func=AF.Copy),
                  reads=[pbk], writes=["uext%d" % g])
            if u < 4:
                def muh(e, g=g, pb=pb):
                    for c in range(8):
                        i = e.matmul(pb[:, 0:16], wu[:, c, g * 128:(g + 1) * 128], xTh[:, c, :], start=(c == 0), stop=(c == 7))
                    return i
                S.add("pe", muh, reads=["wu", "xTh"], writes=[pbk])
                S.add("act", lambda e, g=g, pb=pb: e.activation(out=uext[:, g, 0:16], in_=pb[:, 0:16], func=AF.Copy),
                      reads=[pbk], writes=["uext%d" % g])
        if u < 4:
            Ln = 16 + T
            for g in range(4):
                w = 2 ** (g + 1)
                cur = uext[:, g, :]
                ck = "uext%d" % g
                S.add("dve", lambda e, cur=cur: e.tensor_tensor(out=tA[:, 1:Ln], in0=cur[:, 1:Ln], in1=cur[:, 0:Ln - 1], op=ALU.add),
                      reads=[ck], writes=["tA"])
                fin, fk = tA, "tA"
                if g >= 1:
                    S.add("dve", lambda e: e.tensor_tensor(out=tB[:, 3:Ln], in0=tA[:, 3:Ln], in1=tA[:, 1:Ln - 2], op=ALU.add),
                          reads=["tA"], writes=["tB"])
                    fin, fk = tB, "tB"
                if g >= 2:
                    S.add("dve", lambda e: e.tensor_tensor(out=tA[:, 7:Ln], in0=tB[:, 7:Ln], in1=tB[:, 3:Ln - 4], op=ALU.add),
                          reads=["tB"], writes=["tA"])
                    fin, fk = tA, "tA"
                if g >= 3:
                    S.add("dve", lambda e: e.tensor_tensor(out=tB[:, 15:Ln], in0=tA[:, 15:Ln], in1=tA[:, 7:Ln - 8], op=ALU.add),
                          reads=["tA"], writes=["tB"])
                    fin, fk = tB, "tB"
                S.add("dve", lambda e, fin=fin, g=g, u=u: e.tensor_tensor(out=fin[:, 16:32], in0=fin[:, 16:32], in1=cor4[:, u * 4, g, :], op=ALU.mult),
                      reads=[fk, "corb"], writes=[fk])
                S.add("dve", lambda e, fin=fin, g=g, w=w, cur=cur, T=T: e.scalar_tensor_tensor(
                    out=dT[:, g, 0:T], in0=fin[:, 16:16 + T], scalar=1.0 / w, in1=cur[:, 16:16 + T], op0=ALU.mult, op1=ALU.subtract),
                    reads=[fk, ck], writes=["dT"])
            if u == 3:
                def trp(e):
                    for g in range(4):
                        i = e.transpose(PB[2][0:15, g * 128:(g + 1) * 128], uext[:, g, 16 + 497:16 + 512], ident[:])
                    return i
                S.add("pe", trp, reads=["uext0", "uext1", "uext2", "uext3", "ident"], writes=["PB2"])
                S.add("dve", lambda e: e.tensor_copy(out=pout[0:15, :], in_=PB[2][0:15, :]), reads=["PB2"], writes=["pout"])
                S.add("sp", lambda e: e.dma_start(out=poolp[:, :], in_=pout[0:15, :]), reads=["pout"], writes=["OUTpoolp"], dma="opoolp")
        else:
            for s in range(NSEQ):
                S.add("sp", lambda e, s=s: e.dma_start(out=hsb[0:15, s, :], in_=spool[s, :, :]), writes=["hsb"], dma="hsb")
                S.add("sp", lambda e, s=s: e.dma_start(out=pool_s[s, 0:14, :], in_=spool[s, 1:15, :]), writes=["OUTps%d" % s], dma="ops%d" % s)
            for s in range(NSEQ):
                def trs(e, s=s):
                    for g in range(4):
                        i = e.transpose(PB[2][:, (s * 4 + g) * 16:(s * 4 + g) * 16 + 15], hsb[0:15, s, g * 128:(g + 1) * 128], ident[0:15, 0:15])
                    return i
                S.add("pe", trs, reads=["hsb", "ident"], writes=["PB2"])
            S.add("dve", lambda e: e.memset(hT[:], 0.0), writes=["hT"])
            S.add("dve", lambda e: e.tensor_copy(out=hT[:, :, :, 0:15], in_=PB[2][:, 0:256].rearrange("p (s g r) -> p s g r", s=4, g=4)[:, :, :, 0:15]),
                  reads=["PB2", "hT"], writes=["hT"])
            S.add("pool", lambda e: e.memset(dT[:], 0.0), writes=["dT"])
            for g in range(4):
                w = 2 ** (g + 1)
                S.add("dve", lambda e, g=g, w=w: e.tensor_reduce(out=ssum[:], in_=hT[:, :, g, 16 - w:15], axis=AX.X, op=ALU.add),
                      reads=["hT"], writes=["ssum"])
                S.add("dve", lambda e, g=g: e.tensor_tensor(out=ssum[:], in0=ssum[:], in1=uext[:, g, 16:16 + NSEQ], op=ALU.add),
                      reads=["ssum", "uext%d" % g], writes=["ssum"])
                S.add("dve", lambda e, g=g, w=w: e.scalar_tensor_tensor(
                    out=dT[:, g, 0:NSEQ], in0=ssum[:], scalar=1.0 / w, in1=uext[:, g, 16:16 + NSEQ], op0=ALU.mult, op1=ALU.subtract),
                    reads=["ssum", "uext%d" % g, "dT"], writes=["dT"])

            def tru(e):
                for g in range(4):
                    i = e.transpose(PB[3][0:NSEQ, g * 128:(g + 1) * 128], uext[:, g, 16:16 + NSEQ], ident[:])
                return i
            S.add("pe", tru, reads=["uext0", "uext1", "uext2", "uext3", "ident"], writes=["PB3"])
            S.add("dve", lambda e: e.tensor_copy(out=pout[0:NSEQ, :], in_=PB[3][0:NSEQ, :]), reads=["PB3"], writes=["pout"])
            S.add("sp", lambda e: e.dma_start(out=pool_s[:, 14, :], in_=pout[0:NSEQ, :]), reads=["pout"], writes=["OUTpsu"], dma="opsu")
        for g in range(4):
            pb, pbk = PB[g % 2], "PB%d" % (g % 2)

            def mz(e, g=g, pb=pb, T=T):
                for c in range(8):
                    i = e.matmul(pb[:, 0:T], wza[:, c, g * 128:(g + 1) * 128], xTu[:, c, 0:T], start=(c == 0), stop=(c == 7))
                return i
            S.add("pe", mz, reads=["wza", "xTu"], writes=[pbk])
            _silu_from_psum(S, pb[:, 0:T], pbk, th[:, 0:T], "th", szT[:, g, 0:T], "szT")
            S.add("pe", lambda e, g=g, pb=pb, T=T: e.matmul(pb[:, 0:T], wmix[:, g, :], dT[:, g, 0:T], start=True, stop=True),
                  reads=["wmix", "dT"], writes=[pbk])
            S.add("dve", lambda e, g=g, pb=pb, T=T: e.scalar_tensor_tensor(
                out=pzT[:, g, 0:T], in0=pb[:, 0:T], scalar=psc[:, g:g + 1], in1=szT[:, g, 0:T], op0=ALU.mult, op1=ALU.mult),
                reads=[pbk, "psc", "szT"], writes=["pzT"])
        for t in range(nt):
            def mzb(e, t=t):
                for c in range(8):
                    i = e.matmul(PB[0][:], xTu[:, c, t * 128:(t + 1) * 128], wzb[:, c, :], start=(c == 0), stop=(c == 7))
                return i
            S.add("pe", mzb, reads=["xTu", "wzb"], writes=["PB0"])
            atile = att[:, u * 4 + t, :]
            _silu_from_psum(S, PB[0][:], "PB0", th[:], "th", azb[:], "azb", extra_in1=atile, extra_key="att")
            ptb = PB[1][:].bitcast(BF16)

            def tra(e):
                for cc in range(4):
                    i = e.transpose(ptb[:, cc * 128:(cc + 1) * 128], azb[:, cc * 128:(cc + 1) * 128], identb[:])
                return i
            S.add("pe", tra, reads=["azb", "identb"], writes=["PB1"])
            S.add("act", lambda e, t=t, ptb=ptb: e.activation(out=azT[:, :, t * 128:(t + 1) * 128],
                                                              in_=ptb[:, 0:512].rearrange("p (c t) -> p c t", c=4), func=AF.Copy),
                  reads=["PB1"], writes=["azT"])
        for m in range(8):
            def myp(e, m=m, T=T):
                for cc in range(4):
                    i = e.matmul(PB[2][:, 0:T], wpo[:, cc, m * 128:(m + 1) * 128], pzT[:, cc, 0:T], start=(cc == 0), stop=(cc == 3))
                return i

            def mya(e, m=m, T=T):
                for cc in range(4):
                    i = e.matmul(PB[3][:, 0:T], wao[:, cc, m * 128:(m + 1) * 128], azT[:, cc, 0:T], start=(cc == 0), stop=(cc == 3))
                return i

            def mga(e, m=m, T=T):
                for c in range(8):
                    i = e.matmul(PB[4][:, 0:T], wga[:, c, m * 128:(m + 1) * 128], xTu[:, c, 0:T], start=(c == 0), stop=(c == 7))
                return i

            def mgb(e, m=m, T=T):
                for c in range(8):
                    i = e.matmul(PB[5][:, 0:T], wgb[:, c, m * 128:(m + 1) * 128], xTu[:, c, 0:T], start=(c == 0), stop=(c == 7))
                return i
            S.add("pe", myp, reads=["wpo", "pzT"], writes=["PB2"])
            S.add("pe", mya, reads=["wao", "azT"], writes=["PB3"])
            S.add("pe", mga, reads=["wga", "xTu"], writes=["PB4"])
            S.add("pe", mgb, reads=["wgb", "xTu"], writes=["PB5"])
            S.add("act", lambda e, T=T: e.activation(out=tg1[:, 0:T], in_=PB[4][:, 0:T], func=AF.Tanh, scale=0.5), reads=["PB4"], writes=["tg1"])
            S.add("act", lambda e, T=T: e.activation(out=tg2[:, 0:T], in_=PB[5][:, 0:T], func=AF.Tanh, scale=0.5), reads=["PB5"], writes=["tg2"])
            S.add("pool", lambda e, T=T: e.tensor_scalar(out=tg1[:, 0:T], in0=tg1[:, 0:T], scalar1=0.5, scalar2=0.5, op0=ALU.mult, op1=ALU.add),
                  reads=["tg1"], writes=["tg1"])
            S.add("pool", lambda e, T=T: e.tensor_scalar(out=tg2[:, 0:T], in0=tg2[:, 0:T], scalar1=0.5, scalar2=0.5, op0=ALU.mult, op1=ALU.add),
                  reads=["tg2"], writes=["tg2"])
            S.add("dve", lambda e, T=T: e.tensor_tensor(out=t1[:, 0:T], in0=PB[2][:, 0:T], in1=tg1[:, 0:T], op=ALU.mult),
                  reads=["PB2", "tg1"], writes=["t1"])
            S.add("dve", lambda e, T=T: e.tensor_tensor(out=t2[:, 0:T], in0=PB[3][:, 0:T], in1=tg2[:, 0:T], op=ALU.mult),
                  reads=["PB3", "tg2"], writes=["t2"])
            S.add("pool", lambda e, m=m, T=T: e.tensor_tensor(out=mgT[:, m, 0:T], in0=t1[:, 0:T], in1=t2[:, 0:T], op=ALU.add),
                  reads=["t1", "t2"], writes=["mgT"])
        S.add("sp", lambda e, u=u, T=T: e.dma_start(out=mg_d[u, :, :, 0:T], in_=mgT[:, :, 0:T]), reads=["mgT"], writes=["mgd%d" % u], dma="mgst")
    S.barrier()
    es_.close()


def phase_tail(nc, S, sb, L):
    w_o, w_pg, w_ple, ln_g, ln_b = L["w_o"], L["w_pg"], L["w_ple"], L["ln_g"], L["ln_b"]
    xown, xs, pown, pss, y_own, y_s, mg_d = L["xown"], L["xs"], L["pown"], L["pss"], L["y_own"], L["y_s"], L["mg_d"]
    PX, PB, ident, identb, neghalf = L["PX"], L["PB"], L["ident"], L["identb"], L["neghalf"]
    es_ = contextlib.ExitStack()

    def wload(name, src_ap, shape):
        t = sb(es_, shape, BF16, name)
        S.add("pool", lambda e: e.dma_start(out=t[:], in_=src_ap), writes=[name], dma="w_" + name)
        return t
    wo = wload("wo", w_o.rearrange("(c p) n -> p c n", p=128), [128, 8, D])
    wpg = wload("wpg", w_pg.rearrange("(c p) n -> p c n", p=128), [128, 8, D])
    wple = wload("wple", w_ple.rearrange("(c p) n -> p c n", p=128), [128, 2, D])
    lg = sb(es_, [128, D], F32, "lg")
    lb = sb(es_, [128, D], F32, "lb")
    S.add("sp", lambda e: e.dma_start(out=lg[:], in_=ln_g[None, :].to_broadcast([128, D])), writes=["lg"], dma="lg")
    S.add("sp", lambda e: e.dma_start(out=lb[:], in_=ln_b[None, :].to_broadcast([128, D])), writes=["lb"], dma="lb")
    mgT = [sb(es_, [128, 8, SBK], BF16, "mgTt") for _ in range(2)]
    xt = [sb(es_, [128, D], F32, "txt") for _ in range(2)]
    pt = [sb(es_, [128, PLE], F32, "tpt") for _ in range(2)]
    hp_ = [sb(es_, [128, D], F32, "hp") for _ in range(2)]
    hh_ = [sb(es_, [128, D], F32, "hh") for _ in range(2)]
    hb_ = [sb(es_, [128, D], BF16, "hb") for _ in range(2)]
    pendB = []
    hT = sb(es_, [128, 8, 128], BF16, "hTt")
    pT = sb(es_, [128, 2, 128], BF16, "pTt")
    sg = sb(es_, [128, D], F32, "sg")
    yt = [sb(es_, [128, D], F32, "yt") for _ in range(2)]
    st = sb(es_, [128, 2, 6], F32, "st")
    mv = sb(es_, [128, 2], F32, "mv")
    ve = sb(es_, [128, 1], F32, "ve")
    rstd = sb(es_, [128, 1], F32, "rstd")
    nmr = sb(es_, [128, 1], F32, "nmr")
    gi = 0
    for u in range(5):
        T = SBK if u < 4 else 128
        mj = u % 2
        S.add("sp", lambda e, u=u, mj=mj, T=T: e.dma_start(out=mgT[mj][:, :, 0:T], in_=mg_d[u, :, :, 0:T]),
              reads=["mgd%d" % u], writes=["mgTt%d" % mj], dma="mgld%d" % mj)
        for t in range(T // 128):
            j = gi % 2
            gi += 1
            hp, hh, hb = hp_[j], hh_[j], hb_[j]
            HP, HH, HB = "hp%d" % j, "hh%d" % j, "hb%d" % j
            xsrc = xown[u * SBK + t * 128:u * SBK + (t + 1) * 128, :] if u < 4 else xs[:, :]
            psrc = pown[u * SBK + t * 128:u * SBK + (t + 1) * 128, :] if u < 4 else pss[:, :]
            ydst = y_own[u * SBK + t * 128:u * SBK + (t + 1) * 128, :] if u < 4 else y_s[:, :]
            S.add("sp", lambda e, j=j, xsrc=xsrc: e.dma_start(out=xt[j][:], in_=xsrc), writes=["txt%d" % j], dma="txt%d" % j)
            S.add("sp", lambda e, j=j, psrc=psrc: e.dma_start(out=pt[j][:], in_=psrc), writes=["tpt%d" % j], dma="tpt%d" % j)

            def mho(e, t=t, mj=mj):
                for half in range(2):
                    for c in range(8):
                        i = e.matmul(PX[:, half * 512:(half + 1) * 512], mgT[mj][:, c, t * 128:(t + 1) * 128],
                                     wo[:, c, half * 512:(half + 1) * 512], start=(c == 0), stop=(c == 7))
                return i
            S.add("pe", mho, reads=["mgTt%d" % mj, "wo"], writes=["PX"])
            S.add("dve", lambda e, j=j, hp=hp: e.scalar_tensor_tensor(out=hp[:], in0=xt[j][:], scalar=ALPHA, in1=PX[:], op0=ALU.mult, op1=ALU.add),
                  reads=["txt%d" % j, "PX"], writes=[HP])

            def bst(e, hp=hp):
                for half in range(2):
                    i = e.bn_stats(out=st[:, half, :], in_=hp[:, half * 512:(half + 1) * 512])
                return i
            S.add("dve", bst, reads=[HP], writes=["st"])
            S.add("dve", lambda e: e.bn_aggr(out=mv[:], in_=st[:].rearrange("p a b -> p (a b)")), reads=["st"], writes=["mv"])
            S.add("dve", lambda e: e.tensor_scalar(out=ve[:], in0=mv[:, 1:2], scalar1=EPS, scalar2=None, op0=ALU.add), reads=["mv"], writes=["ve"])
            S.add("pool", lambda e: e.tensor_tensor(out=rstd[:], in0=ve[:], in1=neghalf[:], op=ALU.pow), reads=["ve", "neghalf"], writes=["rstd"])
            S.add("dve", lambda e: e.tensor_scalar(out=nmr[:], in0=mv[:, 0:1], scalar1=rstd[:, 0:1], scalar2=-1.0, op0=ALU.mult, op1=ALU.mult),
                  reads=["mv", "rstd"], writes=["nmr"])
            S.add("act", lambda e, hh=hh, hp=hp: e.activation(out=hh[:], in_=hp[:], func=AF.Identity, scale=rstd[:, 0:1], bias=nmr[:, 0:1]),
                  reads=[HP, "rstd", "nmr"], writes=[HH])
            S.add("dve", lambda e, hh=hh: e.tensor_tensor(out=hh[:], in0=hh[:], in1=lg[:], op=ALU.mult), reads=[HH, "lg"], writes=[HH])
            S.add("pool", lambda e, hh=hh: e.tensor_tensor(out=hh[:], in0=hh[:], in1=lb[:], op=ALU.add), reads=[HH, "lb"], writes=[HH])
            S.add("pool", lambda e, hh=hh, hb=hb: e.tensor_copy(out=hb[:], in_=hh[:]), reads=[HH], writes=[HB])
            def stageB(j=j, hh=hh, hb=hb, HH=HH, HB=HB, ydst=ydst):
                ptb = PB[0][:].bitcast(BF16)

                def trh(e):
                    for c in range(8):
                        i = e.transpose(ptb[:, c * 128:(c + 1) * 128], hb[:, c * 128:(c + 1) * 128], identb[:])
                    return i
                S.add("pe", trh, reads=[HB, "identb"], writes=["PB0"])
                S.add("act", lambda e, ptb=ptb: e.activation(out=hT[:].rearrange("p c t -> p (c t)"), in_=ptb[:, 0:1024], func=AF.Copy),
                      reads=["PB0"], writes=["hTt"])

                def trp(e, j=j):
                    for c in range(2):
                        i = e.transpose(PB[3][:, c * 128:(c + 1) * 128], pt[j][:, c * 128:(c + 1) * 128], ident[:])
                    return i
                S.add("pe", trp, reads=["tpt%d" % j, "ident"], writes=["PB3"])
                S.add("act", lambda e: e.activation(out=pT[:].rearrange("p c t -> p (c t)"), in_=PB[3][:, 0:256], func=AF.Copy),
                      reads=["PB3"], writes=["pTt"])
                for half in range(2):
                    def mg_(e, half=half):
                        for c in range(8):
                            i = e.matmul(PB[1 + half][:], hT[:, c, :], wpg[:, c, half * 512:(half + 1) * 512], start=(c == 0), stop=(c == 7))
                        return i
                    S.add("pe", mg_, reads=["hTt", "wpg"], writes=["PB%d" % (1 + half)])

                    def mp_(e, half=half):
                        for c in range(2):
                            i = e.matmul(PB[4 + half][:], pT[:, c, :], wple[:, c, half * 512:(half + 1) * 512], start=(c == 0), stop=(c == 1))
                        return i
                    S.add("pe", mp_, reads=["pTt", "wple"], writes=["PB%d" % (4 + half)])
                    hs = slice(half * 512, (half + 1) * 512)
                    S.add("act", lambda e, half=half, hs=hs: e.activation(out=sg[:, hs], in_=PB[1 + half][:], func=AF.Tanh, scale=0.5),
                          reads=["PB%d" % (1 + half)], writes=["sg%d" % half])
                    S.add("pool", lambda e, hs=hs: e.tensor_scalar(out=sg[:, hs], in0=sg[:, hs], scalar1=0.5, scalar2=0.5, op0=ALU.mult, op1=ALU.add),
                          reads=["sg%d" % half], writes=["sg%d" % half])
                    S.add("dve", lambda e, half=half, hs=hs: e.tensor_tensor(out=sg[:, hs], in0=PB[4 + half][:], in1=sg[:, hs], op=ALU.mult),
                          reads=["PB%d" % (4 + half), "sg%d" % half], writes=["sg%d" % half])
                S.add("pool", lambda e, j=j: e.tensor_tensor(out=yt[j][:], in0=sg[:], in1=hh[:], op=ALU.add),
                      reads=["sg0", "sg1", HH], writes=["yt%d" % j])
                S.add("sp", lambda e, j=j, ydst=ydst: e.dma_start(out=ydst, in_=yt[j][:]), reads=["yt%d" % j], writes=["OUTy%d" % j], dma="oy%d" % j)
            if pendB:
                pendB.pop()()
            pendB.append(stageB)
    pendB.pop()()
    S.barrier()
    es_.close()
```
